# Optimizing a Trainium2 kernel written in Bass

```python
import math
import jax, jax.numpy as jnp
from jax import lax
import numpy as np

D_MODEL = 4096
BATCH = 4
SEQ = 4096
DEPTH = 2

N_MIXERS = 2
N_S5_LAYERS = (DEPTH + 1) // 2
N_ATTN_LAYERS = DEPTH // 2

S5_GROUP = 16
S5_GROUPS = D_MODEL // S5_GROUP
S5_STATE = 64
DT_MIN = 1e-3
DT_MAX = 1e-1

HEAD_DIM = 64
N_Q_HEADS = D_MODEL // HEAD_DIM
N_KV_HEADS = 8
Q_PER_KV = N_Q_HEADS // N_KV_HEADS
WINDOW = 128
BLOCK = 128
ROPE_THETA = 10000.0
Q_WIDTH = N_Q_HEADS * HEAD_DIM
KV_WIDTH = N_KV_HEADS * HEAD_DIM

D_FF = -(-8 * D_MODEL // (3 * 256)) * 256

EPS = 1e-6

kernel_name = "s5_swa_sink_hybrid_sandwich"


def rms_norm(x, g):
    xf = x.astype(jnp.float32)
    y = xf * lax.rsqrt(jnp.mean(xf * xf, axis=-1, keepdims=True) + EPS)
    return (y * g.astype(jnp.float32)).astype(x.dtype)


def s5_mixer(h, lam_re, lam_im, log_step, b_re, b_im, c_re, c_im, d_skip,
             w_out1, b_out1, w_out2, b_out2):
    f32 = jnp.float32
    bsz, seq, _ = h.shape
    hf = h.astype(f32)
    u = hf.reshape(bsz, seq, S5_GROUPS, S5_GROUP).astype(jnp.complex64)
    lam = lax.complex(lam_re.astype(f32), lam_im.astype(f32))
    step = jnp.exp(log_step.astype(f32))[:, None]
    lam_bar = jnp.exp(lam * step)
    b = lax.complex(b_re.astype(f32), b_im.astype(f32))
    b_bar = ((lam_bar - 1.0) / lam)[..., None] * b
    bu = jnp.einsum('gpc,blgc->blgp', b_bar, u)
    a = jnp.broadcast_to(lam_bar, bu.shape)

    def combine(left, right):
        a_l, b_l = left
        a_r, b_r = right
        return a_r * a_l, a_r * b_l + b_r

    _, states = lax.associative_scan(combine, (a, bu), axis=1)
    c = lax.complex(c_re.astype(f32), c_im.astype(f32))
    y = jnp.einsum('gcp,blgp->blgc', c, states).real.reshape(bsz, seq, D_MODEL)
    y = y + d_skip.astype(f32) * hf
    g = jax.nn.gelu(y).astype(h.dtype)
    return (g @ w_out1 + b_out1) * jax.nn.sigmoid(g @ w_out2 + b_out2)


def rope(x, positions):
    half = HEAD_DIM // 2
    inv_freq = jnp.power(ROPE_THETA, -jnp.arange(half, dtype=jnp.float32) / half)
    ang = positions.astype(jnp.float32)[..., None] * inv_freq
    cos = jnp.cos(ang)[:, :, None, :]
    sin = jnp.sin(ang)[:, :, None, :]
    xf = x.astype(jnp.float32)
    x1, x2 = xf[..., :half], xf[..., half:]
    return jnp.concatenate([x1 * cos - x2 * sin, x2 * cos + x1 * sin], axis=-1).astype(x.dtype)


def swa_mixer(h, positions, w_qkv, b_qkv, w_o, b_o, sinks):
    bsz, seq, _ = h.shape
    nblk = seq // BLOCK
    qkv = h @ w_qkv + b_qkv
    q, k, v = jnp.split(qkv, [Q_WIDTH, Q_WIDTH + KV_WIDTH], axis=-1)
    q = rope(q.reshape(bsz, seq, N_Q_HEADS, HEAD_DIM), positions)
    k = rope(k.reshape(bsz, seq, N_KV_HEADS, HEAD_DIM), positions)
    v = v.reshape(bsz, seq, N_KV_HEADS, HEAD_DIM)
    qb = q.reshape(bsz, nblk, BLOCK, N_KV_HEADS, Q_PER_KV, HEAD_DIM)

    def band(t):
        tb = t.reshape(bsz, nblk, BLOCK, N_KV_HEADS, HEAD_DIM)
        prev = jnp.pad(tb[:, :-1], ((0, 0), (1, 0), (0, 0), (0, 0), (0, 0)))
        return jnp.concatenate([prev, tb], axis=2)

    kb, vb = band(k), band(v)
    scores = jnp.einsum('bnqkgd,bnskd->bnkgqs', qb, kb).astype(jnp.float32) * (HEAD_DIM ** -0.5)
    qi = jnp.arange(BLOCK)[:, None]
    si = jnp.arange(2 * BLOCK)[None, :]
    diff = BLOCK + qi - si
    band_ok = (diff >= 0) & (diff < WINDOW)
    blk = jnp.arange(nblk)[:, None, None]
    in_seq = (blk * BLOCK - BLOCK + si[None]) >= 0
    mask = band_ok[None] & in_seq
    scores = jnp.where(mask[None, :, None, None], scores, -jnp.inf)
    sink = jnp.broadcast_to(
        sinks.astype(jnp.float32).reshape(1, 1, N_KV_HEADS, Q_PER_KV, 1, 1),
        scores.shape[:-1] + (1,))
    probs = jax.nn.softmax(jnp.concatenate([scores, sink], axis=-1), axis=-1)[..., :-1]
    out = jnp.einsum('bnkgqs,bnskd->bnqkgd', probs.astype(v.dtype), vb)
    out = out.reshape(bsz, seq, Q_WIDTH)
    return out @ w_o + b_o


def swiglu(h, w_gate, w_up, w_down):
    return (jax.nn.silu(h @ w_gate) * (h @ w_up)) @ w_down


def setup_inputs(seed: int = 0) -> dict:
    key = jax.random.key(seed)
    ks = jax.random.split(key, 32)
    f32 = jnp.float32

    def nrm(k, shape, scale):
        return jax.random.normal(k, shape, f32) * scale

    x = jax.random.normal(ks[0], (BATCH, SEQ, D_MODEL), f32)
    offsets = jax.random.randint(ks[1], (BATCH, 1), 0, 1024, dtype=jnp.int32)
    positions = (offsets + jnp.arange(SEQ, dtype=jnp.int32)[None, :]).astype(jnp.int32)

    norm_pre_mix = 1.0 + nrm(ks[2], (DEPTH, D_MODEL), 0.02)
    norm_post_mix = 1.0 + nrm(ks[3], (DEPTH, D_MODEL), 0.02)
    norm_pre_ffn = 1.0 + nrm(ks[4], (DEPTH, D_MODEL), 0.02)
    norm_post_ffn = 1.0 + nrm(ks[5], (DEPTH, D_MODEL), 0.02)

    ns = N_S5_LAYERS
    n_idx = jnp.arange(S5_STATE, dtype=f32)
    s5_lam_re = -0.5 + nrm(ks[6], (ns, S5_GROUPS, S5_STATE), 0.01)
    s5_lam_im = math.pi * n_idx + nrm(ks[7], (ns, S5_GROUPS, S5_STATE), 0.01)
    s5_log_step = jax.random.uniform(ks[8], (ns, S5_GROUPS), f32,
                                     math.log(DT_MIN), math.log(DT_MAX))
    s5_b_re = nrm(ks[9], (ns, S5_GROUPS, S5_STATE, S5_GROUP), (2 * S5_GROUP) ** -0.5)
    s5_b_im = nrm(ks[10], (ns, S5_GROUPS, S5_STATE, S5_GROUP), (2 * S5_GROUP) ** -0.5)
    s5_c_re = nrm(ks[11], (ns, S5_GROUPS, S5_GROUP, S5_STATE), (2 * S5_STATE) ** -0.5)
    s5_c_im = nrm(ks[12], (ns, S5_GROUPS, S5_GROUP, S5_STATE), (2 * S5_STATE) ** -0.5)
    s5_d = nrm(ks[13], (ns, D_MODEL), 1.0)
    s5_w_out1 = nrm(ks[14], (ns, D_MODEL, D_MODEL), D_MODEL ** -0.5)
    s5_b_out1 = nrm(ks[15], (ns, D_MODEL), 0.01)
    s5_w_out2 = nrm(ks[16], (ns, D_MODEL, D_MODEL), D_MODEL ** -0.5)
    s5_b_out2 = nrm(ks[17], (ns, D_MODEL), 0.01)

    na = N_ATTN_LAYERS
    attn_w_qkv = nrm(ks[18], (na, D_MODEL, Q_WIDTH + 2 * KV_WIDTH), D_MODEL ** -0.5)
    attn_b_qkv = nrm(ks[19], (na, Q_WIDTH + 2 * KV_WIDTH), 0.01)
    attn_w_o = nrm(ks[20], (na, Q_WIDTH, D_MODEL), Q_WIDTH ** -0.5)
    attn_b_o = nrm(ks[21], (na, D_MODEL), 0.01)
    attn_sinks = nrm(ks[22], (na, N_Q_HEADS), 0.5)

    ffn_w_gate = nrm(ks[23], (DEPTH, D_MODEL, D_FF), D_MODEL ** -0.5)
    ffn_w_up = nrm(ks[24], (DEPTH, D_MODEL, D_FF), D_MODEL ** -0.5)
    ffn_w_down = nrm(ks[25], (DEPTH, D_FF, D_MODEL), D_FF ** -0.5)

    return {
        "x": x, "positions": positions,
        "norm_pre_mix": norm_pre_mix, "norm_post_mix": norm_post_mix,
        "norm_pre_ffn": norm_pre_ffn, "norm_post_ffn": norm_post_ffn,
        "s5_lam_re": s5_lam_re, "s5_lam_im": s5_lam_im, "s5_log_step": s5_log_step,
        "s5_b_re": s5_b_re, "s5_b_im": s5_b_im, "s5_c_re": s5_c_re, "s5_c_im": s5_c_im,
        "s5_d": s5_d, "s5_w_out1": s5_w_out1, "s5_b_out1": s5_b_out1,
        "s5_w_out2": s5_w_out2, "s5_b_out2": s5_b_out2,
        "attn_w_qkv": attn_w_qkv, "attn_b_qkv": attn_b_qkv,
        "attn_w_o": attn_w_o, "attn_b_o": attn_b_o, "attn_sinks": attn_sinks,
        "ffn_w_gate": ffn_w_gate, "ffn_w_up": ffn_w_up, "ffn_w_down": ffn_w_down,
    }


def reference(x, positions, norm_pre_mix, norm_post_mix, norm_pre_ffn, norm_post_ffn,
              s5_lam_re, s5_lam_im, s5_log_step, s5_b_re, s5_b_im, s5_c_re, s5_c_im,
              s5_d, s5_w_out1, s5_b_out1, s5_w_out2, s5_b_out2,
              attn_w_qkv, attn_b_qkv, attn_w_o, attn_b_o, attn_sinks,
              ffn_w_gate, ffn_w_up, ffn_w_down):
    h = x
    for i in range(DEPTH):
        j = i // N_MIXERS
        hn = rms_norm(h, norm_pre_mix[i])
        if i % N_MIXERS == 0:
            m = s5_mixer(hn, s5_lam_re[j], s5_lam_im[j], s5_log_step[j],
                         s5_b_re[j], s5_b_im[j], s5_c_re[j], s5_c_im[j], s5_d[j],
                         s5_w_out1[j], s5_b_out1[j], s5_w_out2[j], s5_b_out2[j])
        else:
            m = swa_mixer(hn, positions, attn_w_qkv[j], attn_b_qkv[j],
                          attn_w_o[j], attn_b_o[j], attn_sinks[j])
        h = h + rms_norm(m, norm_post_mix[i])
        f = swiglu(rms_norm(h, norm_pre_ffn[i]), ffn_w_gate[i], ffn_w_up[i], ffn_w_down[i])
        h = h + rms_norm(f, norm_post_ffn[i])
    return h
```

```python
import contextlib
import numpy as np
import concourse.bass as bass
import concourse.mybir as mybir
from concourse.bass_utils import run_bass_kernel_spmd

F32 = mybir.dt.float32
BF16 = mybir.dt.bfloat16
I32 = mybir.dt.int32
ALU = mybir.AluOpType
AF = mybir.ActivationFunctionType
AX = mybir.AxisListType

D = 4096
KC = D // 128
DFF = 11008
HC = DFF // 128
NG = 256
NQH, NKV, HD = 64, 8, 64
EPS = 1e-6
MAGIC = 12582912.0
TWO_PI = float(2 * np.pi)
PI = float(np.pi)
NEG = -30000.0


class Buf:
    __slots__ = ("name", "lw", "rd", "guard", "multi")

    def __init__(self, name, multi=False):
        self.name = name
        self.lw = []
        self.rd = []
        self.guard = []
        self.multi = multi


class Sched:
    ENG = ("pe", "act", "dve", "pool", "sp")

    def __init__(self, nc, ndma=(("sp", 8), ("act", 4), ("pool", 2))):
        self.nc = nc
        self.prog = {e: [] for e in self.ENG}
        self.cnt = {e: 0 for e in self.ENG}
        self.waited = {e: {} for e in self.ENG}
        self.ndma = dict(ndma)
        self.dma_i = {q: 0 for q in self.ndma}
        self.n_inst = 0

    def _wait(self, eng, k, v):
        if self.waited[eng].get(k, 0) < v:
            self.waited[eng][k] = v
            self.prog[eng].append(("wait", k, v))

    def _deps(self, eng, reads, writes, pe_acc=False):
        need = {}

        def add(tok):
            if need.get(tok[0], 0) < tok[1]:
                need[tok[0]] = tok[1]
        for b in reads:
            for t in b.lw:
                add(t)
        for b in writes:
            if not b.multi:
                for t in b.lw:
                    if not (pe_acc and t[0] == "c_pe"):
                        add(t)
            for t in b.rd:
                add(t)
            for t in b.guard:
                add(t)
        for k, v in need.items():
            self._wait(eng, k, v)

    def _mark(self, tok, reads, writes):
        for b in reads:
            b.rd.append(tok)
        for b in writes:
            if b.multi:
                if b.rd:
                    b.guard = b.rd
                    b.rd = []
                    b.lw = []
                b.lw.append(tok)
            else:
                b.lw = [tok]
                b.rd = []

    def op(self, eng, fn, reads=(), writes=(), pe_acc=False):
        self._deps(eng, reads, writes, pe_acc)
        self.cnt[eng] += 1
        tok = ("c_" + eng, self.cnt[eng])
        self.prog[eng].append(("op", fn, tok[0], 1))
        self._mark(tok, reads, writes)
        self.n_inst += 1

    def dma(self, q, fn, reads=(), writes=()):
        self._deps(q, reads, writes)
        K = self.ndma[q]
        i = self.dma_i[q]
        self.dma_i[q] += 1
        slot, n = i % K, i // K + 1
        key = "d_%s_%d" % (q, slot)
        if n > 1:
            self._wait(q, key, 16 * (n - 1))
        tok = (key, 16 * n)
        self.prog[q].append(("op", fn, key, 16))
        self._mark(tok, reads, writes)
        self.n_inst += 1

    def barrier(self, engs=None):
        cur = {}
        for e in self.ENG:
            if self.cnt[e]:
                cur["c_" + e] = self.cnt[e]
        for q, K in self.ndma.items():
            for s in range(K):
                n = (self.dma_i[q] - s + K - 1) // K
                if n > 0:
                    cur["d_%s_%d" % (q, s)] = 16 * n
        for e in (engs or self.ENG):
            for k, v in cur.items():
                self._wait(e, k, v)

    def emit(self):
        nc = self.nc
        keys = ["c_" + e for e in self.ENG]
        for q, K in self.ndma.items():
            keys += ["d_%s_%d" % (q, s) for s in range(K)]
        sems = {}
        with contextlib.ExitStack() as st:
            for k in keys:
                sems[k] = st.enter_context(nc.semaphore(k))
            block = st.enter_context(nc.Block())

            def run(name):
                def body(eng):
                    for it in self.prog[name]:
                        if it[0] == "wait":
                            eng.wait_ge(sems[it[1]], it[2])
                        else:
                            it[1](eng).then_inc(sems[it[2]], it[3])
                return body
            block.tensor(run("pe"))
            block.scalar(run("act"))
            block.vector(run("dve"))
            block.gpsimd(run("pool"))
            block.sync(run("sp"))


class Arena:
    def __init__(self, ap_bf16, nbytes):
        self.ap = ap_bf16
        self.nbytes = nbytes
        self.off = 0

    def reset(self):
        self.off = 0

    def alloc(self, shape, dt):
        esz = 4 if dt in (F32, I32) else 2
        n = int(np.prod(shape[1:]))
        nb = n * esz
        self.off = (self.off + 63) // 64 * 64
        assert self.off + nb <= self.nbytes, ("arena overflow", self.off, nb, shape)
        a = self.ap[:, self.off // 2:(self.off + nb) // 2]
        self.off += nb
        if esz == 4:
            a = a.bitcast(dt)
        if len(shape) == 3:
            a = a.rearrange("p (a b) -> p a b", a=shape[1])
        elif len(shape) == 4:
            a = a.rearrange("p (a b c) -> p a b c", a=shape[1], b=shape[2])
        if shape[0] < 128:
            a = a[0:shape[0]]
        return a


def build_program(P, M, stop_after=None, dump_all=False):
    T = P + M
    A = 128 + M
    NB = T // 512
    nc = bass.Bass("TRN2", target_bir_lowering=False)

    def din(name, shape, dt=F32):
        return nc.dram_tensor(name, list(shape), dt, kind="ExternalInput").ap()

    def dscr(name, shape, dt):
        return nc.dram_tensor(name, list(shape), dt, kind="Internal").ap()

    xcat = din("xcat", [T, D])
    posa = din("posa", [1, A], I32)
    halo_mask = din("halo_mask", [128, 128])
    g_pre_mix = din("norm_pre_mix", [2, D])
    g_post_mix = din("norm_post_mix", [2, D])
    g_pre_ffn = din("norm_pre_ffn", [2, D])
    g_post_ffn = din("norm_post_ffn", [2, D])
    lam_re = din("s5_lam_re", [1, NG, 64])
    lam_im = din("s5_lam_im", [1, NG, 64])
    log_step = din("s5_log_step", [1, NG])
    b_re = din("s5_b_re", [1, NG, 64, 16])
    b_im = din("s5_b_im", [1, NG, 64, 16])
    c_re = din("s5_c_re", [1, NG, 16, 64])
    c_im = din("s5_c_im", [1, NG, 16, 64])
    s5_d = din("s5_d", [1, D])
    w_out1 = din("s5_w_out1", [1, D, D])
    b_out1 = din("s5_b_out1", [1, D])
    w_out2 = din("s5_w_out2", [1, D, D])
    b_out2 = din("s5_b_out2", [1, D])
    w_qkv = din("attn_w_qkv", [1, D, 5120])
    b_qkv = din("attn_b_qkv", [1, 5120])
    w_o = din("attn_w_o", [1, D, D])
    b_o = din("attn_b_o", [1, D])
    sinks = din("attn_sinks", [1, NQH])
    w_gate = din("ffn_w_gate", [2, D, DFF])
    w_up = din("ffn_w_up", [2, D, DFF])
    w_down = din("ffn_w_down", [2, DFF, D])
    out = nc.dram_tensor("out", [M, D], F32, kind="ExternalOutput").ap()

    hnT_d = dscr("hnT_d", [KC, 128, T], BF16)
    actA_d = dscr("actA_d", [KC, 128, A], BF16)
    actB_d = dscr("actB_d", [KC, 128, A], BF16)
    mT_d = dscr("mT_d", [KC, 128, A], F32)
    h_d = dscr("h_d", [A, D], F32)
    qT_d = dscr("qT_d", [KC, 128, A], BF16)
    kT_d = dscr("kT_d", [NKV, 128, A], BF16)
    v_d = dscr("v_d", [A, 512], BF16)

    S = Sched(nc)
    dbg_out = {}

    with contextlib.ExitStack() as st:
        ARENA_BYTES = 198 * 1024
        arena_t = st.enter_context(nc.sbuf_tensor("arena", [128, ARENA_BYTES // 2], BF16))
        consts_t = st.enter_context(nc.sbuf_tensor("consts", [128, 768], F32))
        AR = Arena(arena_t[:], ARENA_BYTES)
        ident_f = consts_t[:, 0:128]
        ident_b = consts_t[:, 128:192].bitcast(BF16)
        halfpi = consts_t[:, 192:193]
        small = consts_t[:, 200:768]
        banks = [st.enter_context(nc.psum_tensor("bank%d" % i, [128, 512], F32)) for i in range(8)]
        BK = [Buf("bank%d" % i) for i in range(8)]
        B_const = Buf("const")

        S.op("pool", lambda e: e.memset(consts_t[:, 0:192], 0.0), writes=[B_const])
        S.op("pool", lambda e: e.memset(halfpi, PI / 2), writes=[B_const])
        ones_tmp = small[:, 0:128]
        S.op("pool", lambda e: e.memset(ones_tmp, 1.0), writes=[B_const])
        S.op("pool", lambda e: e.affine_select(out=ident_f, in_=ones_tmp, pattern=[[1, 128]], compare_op=ALU.is_equal,
                                               fill=0.0, base=0, channel_multiplier=-1), reads=[B_const], writes=[B_const])
        S.op("dve", lambda e: e.tensor_copy(out=ident_b, in_=ident_f), reads=[B_const], writes=[B_const])
        S.barrier()

        def rstd_from_ss(ss, rstd, rd, wr):
            S.op("dve", lambda e: e.tensor_scalar(out=rstd, in0=ss, scalar1=1.0 / D, scalar2=EPS, op0=ALU.mult, op1=ALU.add),
                 reads=rd, writes=wr)
            S.op("act", lambda e: e.activation(out=rstd, in_=rstd, func=AF.Sqrt), reads=wr, writes=wr)
            S.op("dve", lambda e: e.reciprocal(out=rstd, in_=rstd), reads=wr, writes=wr)

        def load_cols(dst, vec, n, Bdst, tmp, Btmp, bank, Bbank):
            S.dma("sp", lambda e: e.dma_start(out=tmp[0:n, :], in_=vec.rearrange("(c p) -> c p", p=128)), writes=[Btmp])
            S.op("pe", lambda e: e.transpose(out=bank[:, 0:n], in_=tmp[0:n, :], identity=ident_f[0:n, 0:n]),
                 reads=[Btmp, B_const], writes=[Bbank])
            S.op("dve", lambda e: e.tensor_copy(out=dst, in_=bank[:, 0:n]), reads=[Bbank], writes=[Bdst])

        def norm_transpose(src, Bsrc, gain_bc, Bgain, dstT_d, Bdst, col0, bufs, idx):
            junk, Bjunk, hn, Bhn, hnT, BhnT, ssb, Bss = bufs
            ss = ssb[:, 0:1]
            rstd = ssb[:, 1:2]
            S.op("act", lambda e: e.activation(out=junk, in_=src, func=AF.Square), reads=[Bsrc], writes=[Bjunk])
            S.op("dve", lambda e: e.reduce_sum(out=ss, in_=junk, axis=AX.X), reads=[Bjunk], writes=[Bss])
            rstd_from_ss(ss, rstd, [Bss], [Bss])
            S.op("dve", lambda e: e.scalar_tensor_tensor(out=hn, in0=src, scalar=rstd, in1=gain_bc, op0=ALU.mult, op1=ALU.mult),
                 reads=[Bsrc, Bss, Bgain], writes=[Bhn])
            for q in range(4):
                bk = 4 + (q % 2) + 2 * (idx % 2)
                pb = banks[bk][:].bitcast(BF16)
                for i in range(8):
                    c = 8 * q + i
                    S.op("pe", lambda e, c=c, i=i, pb=pb: e.transpose(out=pb[:, 128 * i:128 * i + 128], in_=hn[:, 128 * c:128 * c + 128], identity=ident_b),
                         reads=[Bhn, B_const], writes=[BK[bk]], pe_acc=(i > 0))
                dsl = hnT[:, 8 * q:8 * q + 8, :]
                src_ps = pb.rearrange("p (a b) -> p a b", a=8)
                if q % 2 == 0:
                    S.op("act", lambda e, dsl=dsl, src_ps=src_ps: e.copy(out=dsl, in_=src_ps), reads=[BK[bk]], writes=[BhnT])
                else:
                    S.op("dve", lambda e, dsl=dsl, src_ps=src_ps: e.tensor_copy(out=dsl, in_=src_ps), reads=[BK[bk]], writes=[BhnT])
            S.dma("act", lambda e: e.dma_start(out=dstT_d.rearrange("c f t -> f c t")[:, :, col0:col0 + 128], in_=hnT),
                  reads=[BhnT], writes=[Bdst])

        def nt_bufs(tag):
            return (AR.alloc([128, D], BF16), Buf("junk" + tag), AR.alloc([128, D], BF16), Buf("hn" + tag),
                    AR.alloc([128, KC, 128], BF16), Buf("hnT" + tag), AR.alloc([128, 4], F32), Buf("ss" + tag))

        def load_gain(row):
            g = AR.alloc([128, D], F32)
            Bg = Buf("gain")
            S.dma("sp", lambda e: e.dma_start(out=g, in_=row.partition_broadcast(128)), writes=[Bg])
            return g, Bg

        B_hnT_d = Buf("hnT_d", multi=True)
        AR.reset()
        g0, Bg0 = load_gain(g_pre_mix[0])
        xs = [AR.alloc([128, D], F32) for _ in range(2)]
        Bxs = [Buf("xs%d" % i) for i in range(2)]
        ntb = [nt_bufs("a"), nt_bufs("b")]
        for j in range(T // 128):
            x_t, Bx = xs[j % 2], Bxs[j % 2]
            S.dma("sp", lambda e, x_t=x_t, j=j: e.dma_start(out=x_t, in_=xcat[128 * j:128 * j + 128, :]), writes=[Bx])
            norm_transpose(x_t, Bx, g0, Bg0, hnT_d, B_hnT_d, 128 * j, ntb[j % 2], j)
        S.barrier()
        if stop_after == 1:
            dbg_out["hnT_d"] = hnT_d

        B_actA = Buf("actA_d", multi=True)
        if stop_after is None or stop_after >= 2:
            AR.reset()
            lamre = AR.alloc([128, NG], F32)
            lamim = AR.alloc([128, NG], F32)
            stepb = AR.alloc([128, NG], F32)
            Rr = AR.alloc([128, NG], F32)
            THW = AR.alloc([128, NG], F32)
            THN = AR.alloc([128, NG], F32)
            BRE = AR.alloc([128, NG], F32)
            BIM = AR.alloc([128, NG], F32)
            w1 = AR.alloc([128, NG], F32)
            w2 = AR.alloc([128, NG], F32)
            w3 = AR.alloc([128, NG], F32)
            w4 = AR.alloc([128, NG], F32)
            dcol = AR.alloc([128, KC], F32)
            tmp32 = AR.alloc([128, 128], F32)
            Bp = Buf("s5prep")
            for h in range(2):
                S.dma("sp", lambda e, h=h: e.dma_start(out=lamre[64 * h:64 * h + 64, :], in_=lam_re[0].rearrange("g p -> p g"),
                                                       allow_slow_non_contiguous=True), writes=[Bp])
                S.dma("sp", lambda e, h=h: e.dma_start(out=lamim[64 * h:64 * h + 64, :], in_=lam_im[0].rearrange("g p -> p g"),
                                                       allow_slow_non_contiguous=True), writes=[Bp])
            S.dma("sp", lambda e: e.dma_start(out=stepb, in_=log_step[0].partition_broadcast(128)), writes=[Bp])
            Btmp32 = Buf("tmp32")
            load_cols(dcol, s5_d[0], KC, Bp, tmp32, Btmp32, banks[0], BK[0])
            S.barrier()
            V = lambda fn: S.op("dve", fn, reads=[Bp], writes=[Bp])
            Aop = lambda fn: S.op("act", fn, reads=[Bp], writes=[Bp])
            Aop(lambda e: e.activation(out=stepb, in_=stepb, func=AF.Exp))
            V(lambda e: e.tensor_tensor(out=w1, in0=lamre, in1=stepb, op=ALU.mult))
            V(lambda e: e.tensor_tensor(out=w2, in0=lamim, in1=stepb, op=ALU.mult))
            Aop(lambda e: e.activation(out=Rr, in_=w1, func=AF.Exp))
            V(lambda e: e.tensor_scalar(out=w3, in0=w2, scalar1=1.0 / TWO_PI, scalar2=MAGIC, op0=ALU.mult, op1=ALU.add))
            V(lambda e: e.tensor_scalar(out=w3, in0=w3, scalar1=-MAGIC, scalar2=-TWO_PI, op0=ALU.add, op1=ALU.mult))
            V(lambda e: e.tensor_tensor(out=THW, in0=w3, in1=w2, op=ALU.add))
            V(lambda e: e.tensor_scalar(out=THW, in0=THW, scalar1=-PI, scalar2=PI, op0=ALU.max, op1=ALU.min))
            V(lambda e: e.tensor_scalar(out=THN, in0=THW, scalar1=1.0 / TWO_PI, scalar2=None, op0=ALU.mult))
            Aop(lambda e: e.activation(out=w3, in_=THW, func=AF.Sin))
            Aop(lambda e: e.activation(out=w4, in_=THW, func=AF.Abs))
            Aop(lambda e: e.activation(out=w4, in_=w4, func=AF.Sin, scale=-1.0, bias=halfpi))
            V(lambda e: e.tensor_tensor(out=w3, in0=w3, in1=Rr, op=ALU.mult))
            V(lambda e: e.tensor_tensor(out=w4, in0=w4, in1=Rr, op=ALU.mult))
            V(lambda e: e.tensor_scalar(out=w4, in0=w4, scalar1=-1.0, scalar2=None, op0=ALU.add))
            V(lambda e: e.tensor_tensor(out=w1, in0=lamre, in1=lamre, op=ALU.mult))
            V(lambda e: e.tensor_tensor(out=w2, in0=lamim, in1=lamim, op=ALU.mult))
            V(lambda e: e.tensor_tensor(out=w1, in0=w1, in1=w2, op=ALU.add))
            V(lambda e: e.reciprocal(out=w1, in_=w1))
            V(lambda e: e.tensor_tensor(out=BRE, in0=w4, in1=lamre, op=ALU.mult))
            V(lambda e: e.tensor_tensor(out=w2, in0=w3, in1=lamim, op=ALU.mult))
            V(lambda e: e.tensor_tensor(out=BRE, in0=BRE, in1=w2, op=ALU.add))
            V(lambda e: e.tensor_tensor(out=BRE, in0=BRE, in1=w1, op=ALU.mult))
            V(lambda e: e.tensor_tensor(out=BIM, in0=w3, in1=lamre, op=ALU.mult))
            V(lambda e: e.tensor_tensor(out=w2, in0=w4, in1=lamim, op=ALU.mult))
            V(lambda e: e.tensor_tensor(out=BIM, in0=BIM, in1=w2, op=ALU.subtract))
            V(lambda e: e.tensor_tensor(out=BIM, in0=BIM, in1=w1, op=ALU.mult))
            rmask = AR.alloc([128, 8], F32)
            cmask = AR.alloc([128, 8, 128], F32)
            onesb = AR.alloc([128, 8, 128], F32)
            S.op("pool", lambda e: e.memset(onesb, 1.0), reads=[Bp], writes=[Bp])
            S.op("pool", lambda e: e.affine_select(out=rmask, in_=onesb[:, 0, 0:8], pattern=[[-16, 8]], compare_op=ALU.is_ge, fill=0.0,
                                                   base=0, channel_multiplier=1), reads=[Bp], writes=[Bp])
            S.op("pool", lambda e: e.affine_select(out=rmask, in_=rmask, pattern=[[16, 8]], compare_op=ALU.is_ge, fill=0.0,
                                                   base=15, channel_multiplier=-1), reads=[Bp], writes=[Bp])
            S.op("pool", lambda e: e.affine_select(out=cmask, in_=onesb, pattern=[[-16, 8], [1, 128]], compare_op=ALU.is_ge, fill=0.0,
                                                   base=0, channel_multiplier=0), reads=[Bp], writes=[Bp])
            S.op("pool", lambda e: e.affine_select(out=cmask, in_=cmask, pattern=[[16, 8], [-1, 128]], compare_op=ALU.is_ge, fill=0.0,
                                                   base=15, channel_multiplier=0), reads=[Bp], writes=[Bp])
            tposi = AR.alloc([128, T], I32)
            tposf = AR.alloc([128, T], F32)
            S.op("pool", lambda e: e.iota(tposi, pattern=[[1, T]], base=0, channel_multiplier=0), writes=[Bp])
            S.op("dve", lambda e: e.tensor_copy(out=tposf, in_=tposi), reads=[Bp], writes=[Bp])
            S.barrier()

            hk = [AR.alloc([128, T], BF16) for _ in range(2)]
            Bhk = [Buf("hk%d" % i) for i in range(2)]
            Bp_f = AR.alloc([64, 2, 8, 16], F32)
            BBp = Buf("Bp_f")
            Cn = AR.alloc([128, 2, 128], F32)
            BCn = Buf("Cn")
            BTcat = AR.alloc([128, 256], F32)
            BBT = Buf("BTcat")
            CT = AR.alloc([128, 2, 128], F32)
            E1 = AR.alloc([128, 128], F32)
            E2 = AR.alloc([128, 128], F32)
            cw1 = AR.alloc([128, 128], F32)
            cw2 = AR.alloc([128, 128], F32)
            BE = Buf("E")
            WB = AR.alloc([128, 8, 256], BF16)
            WC = AR.alloc([128, 8, 256], BF16)
            BW = Buf("W")
            carry = AR.alloc([128, 8], F32)
            Bcarry = Buf("carry")
            NR = 2
            tA = [AR.alloc([128, 512], F32) for _ in range(NR)]
            tB = [AR.alloc([128, 512], F32) for _ in range(NR)]
            CSt = [AR.alloc([128, 512], F32) for _ in range(NR)]
            SNt = [AR.alloc([128, 512], F32) for _ in range(NR)]
            X2s = [AR.alloc([128, 512], F32) for _ in range(NR)]
            M1 = [AR.alloc([128, 512], F32) for _ in range(NR)]
            sT = [AR.alloc([128, 512], F32) for _ in range(NR)]
            Z1 = [AR.alloc([128, 512], BF16) for _ in range(NR)]
            Z2 = [AR.alloc([128, 512], BF16) for _ in range(NR)]
            BtA = [Buf("tA%d" % i) for i in range(NR)]
            BtB = [Buf("tB%d" % i) for i in range(NR)]
            BCS = [Buf("CS%d" % i) for i in range(NR)]
            BSN = [Buf("SN%d" % i) for i in range(NR)]
            BX2 = [Buf("X2s%d" % i) for i in range(NR)]
            BM1 = [Buf("M1%d" % i) for i in range(NR)]
            BsT = [Buf("sT%d" % i) for i in range(NR)]
            BZ1 = [Buf("Z1%d" % i) for i in range(NR)]
            BZ2 = [Buf("Z2%d" % i) for i in range(NR)]
            yb = [AR.alloc([128, 512], F32) for _ in range(2)]
            Byb = [Buf("yb%d" % i) for i in range(2)]
            gb = [AR.alloc([128, 512], BF16) for _ in range(2)]
            Bgb = [Buf("gb%d" % i) for i in range(2)]
            it = 0
            for k in range(KC):
                hkt, Bh = hk[k % 2], Bhk[k % 2]
                S.dma("sp", lambda e, hkt=hkt, k=k: e.dma_start(out=hkt, in_=hnT_d[k]), reads=[B_hnT_d], writes=[Bh])
                S.dma("sp", lambda e, k=k: e.dma_start(out=Bp_f[:, 0], in_=b_re[0, 8 * k:8 * k + 8].rearrange("g p c -> p g c")), writes=[BBp])
                S.dma("sp", lambda e, k=k: e.dma_start(out=Bp_f[:, 1], in_=b_im[0, 8 * k:8 * k + 8].rearrange("g p c -> p g c")), writes=[BBp])
                for h in range(2):
                    S.dma("sp", lambda e, k=k, h=h: e.dma_start(out=Cn[:, 0, 64 * h:64 * h + 64], in_=c_re[0, 8 * k:8 * k + 8].rearrange("g c p -> (g c) p")), writes=[BCn])
                    S.dma("sp", lambda e, k=k, h=h: e.dma_start(out=Cn[:, 1, 64 * h:64 * h + 64], in_=c_im[0, 8 * k:8 * k + 8].rearrange("g c p -> (g c) p")), writes=[BCn])
                S.op("pe", lambda e: e.transpose(out=banks[0][:, 0:64], in_=Bp_f[:, 0].rearrange("p g c -> p (g c)"), identity=ident_f[0:64, 0:64]),
                     reads=[BBp, B_const], writes=[BK[0]])
                S.op("pe", lambda e: e.transpose(out=banks[0][:, 64:128], in_=Bp_f[:, 1].rearrange("p g c -> p (g c)"), identity=ident_f[0:64, 0:64]),
                     reads=[BBp, B_const], writes=[BK[0]], pe_acc=True)
                S.op("dve", lambda e: e.tensor_copy(out=BTcat[:, 0:128], in_=banks[0][:, 0:128]), reads=[BK[0]], writes=[BBT])
                S.op("dve", lambda e: e.tensor_copy(out=BTcat[:, 128:192], in_=banks[0][:, 64:128]), reads=[BK[0]], writes=[BBT])
                S.op("dve", lambda e: e.tensor_scalar(out=BTcat[:, 192:256], in0=banks[0][:, 0:64], scalar1=-1.0, scalar2=None, op0=ALU.mult),
                     reads=[BK[0]], writes=[BBT])
                S.op("pe", lambda e: e.transpose(out=banks[1][:, 0:128], in_=Cn[:, 0], identity=ident_f), reads=[BCn, B_const], writes=[BK[1]])
                S.op("pe", lambda e: e.transpose(out=banks[1][:, 128:256], in_=Cn[:, 1], identity=ident_f), reads=[BCn, B_const], writes=[BK[1]], pe_acc=True)
                S.op("dve", lambda e: e.tensor_copy(out=CT.rearrange("p a b -> p (a b)"), in_=banks[1][:, 0:256]), reads=[BK[1]], writes=[BE])
                bre_bc = BRE[:, 8 * k:8 * k + 8].unsqueeze(2).to_broadcast([128, 8, 16])
                bim_bc = BIM[:, 8 * k:8 * k + 8].unsqueeze(2).to_broadcast([128, 8, 16])
                r3 = lambda a: a.rearrange("p (g c) -> p g c", g=8)
                VE = lambda fn: S.op("dve", fn, reads=[BE, Bp], writes=[BE])
                VE(lambda e, bre_bc=bre_bc: e.tensor_tensor(out=r3(cw1), in0=r3(CT[:, 0]), in1=bre_bc, op=ALU.mult))
                VE(lambda e, bim_bc=bim_bc: e.tensor_tensor(out=r3(cw2), in0=r3(CT[:, 1]), in1=bim_bc, op=ALU.mult))
                VE(lambda e: e.tensor_tensor(out=cw1, in0=cw1, in1=cw2, op=ALU.subtract))
                VE(lambda e, bim_bc=bim_bc: e.tensor_tensor(out=r3(cw2), in0=r3(CT[:, 0]), in1=bim_bc, op=ALU.mult))
                VE(lambda e, bre_bc=bre_bc: e.tensor_tensor(out=r3(E2), in0=r3(CT[:, 1]), in1=bre_bc, op=ALU.mult))
                VE(lambda e: e.tensor_tensor(out=cw2, in0=cw2, in1=E2, op=ALU.add))
                VE(lambda e: e.tensor_copy(out=E1[0:64], in_=cw1[0:64]))
                VE(lambda e: e.tensor_scalar(out=E1[64:128], in0=cw2[64:128], scalar1=-1.0, scalar2=None, op0=ALU.mult))
                VE(lambda e: e.tensor_scalar(out=E2[0:64], in0=cw2[0:64], scalar1=-1.0, scalar2=None, op0=ALU.mult))
                VE(lambda e: e.tensor_scalar(out=E2[64:128], in0=cw1[64:128], scalar1=-1.0, scalar2=None, op0=ALU.mult))
                for gl in range(8):
                    S.op("dve", lambda e, gl=gl: e.tensor_scalar(out=WB[:, gl, :], in0=BTcat, scalar1=rmask[:, gl:gl + 1], scalar2=None, op0=ALU.mult),
                         reads=[BBT, Bp], writes=[BW])
                    S.op("dve", lambda e, gl=gl: e.tensor_tensor(out=WC[:, gl, 0:128], in0=E1, in1=cmask[:, gl, :], op=ALU.mult), reads=[BE, Bp], writes=[BW])
                    S.op("dve", lambda e, gl=gl: e.tensor_tensor(out=WC[:, gl, 128:256], in0=E2, in1=cmask[:, gl, :], op=ALU.mult), reads=[BE, Bp], writes=[BW])
                S.op("pool", lambda e: e.memset(carry, 0.0), writes=[Bcarry])
                for b in range(NB):
                    c0 = 512 * b
                    need_lo = max(c0, P - 128)
                    need = need_lo < c0 + 512
                    ybk = 2 + (b % 2)
                    for gl in range(8):
                        g = 8 * k + gl
                        r = it % NR
                        it += 1
                        xb = 4 + 2 * (it % 2)
                        psX1, psX2 = banks[xb], banks[xb + 1]
                        S.op("pe", lambda e, psX1=psX1, gl=gl, hkt=hkt, c0=c0: e.matmul(psX1[:, :], lhsT=WB[:, gl, 0:128], rhs=hkt[:, c0:c0 + 512], start=True, stop=True),
                             reads=[BW, Bh], writes=[BK[xb]])
                        S.op("pe", lambda e, psX2=psX2, gl=gl, hkt=hkt, c0=c0: e.matmul(psX2[:, :], lhsT=WB[:, gl, 128:256], rhs=hkt[:, c0:c0 + 512], start=True, stop=True),
                             reads=[BW, Bh], writes=[BK[xb + 1]])
                        tp = tposf[:, c0:c0 + 512]
                        S.op("pool", lambda e, r=r, g=g, tp=tp: e.tensor_scalar(out=tA[r], in0=tp, scalar1=THN[:, g:g + 1], scalar2=MAGIC, op0=ALU.mult, op1=ALU.add),
                             reads=[Bp], writes=[BtA[r]])
                        S.op("pool", lambda e, r=r: e.tensor_scalar(out=tA[r], in0=tA[r], scalar1=-MAGIC, scalar2=-TWO_PI, op0=ALU.add, op1=ALU.mult),
                             reads=[BtA[r]], writes=[BtA[r]])
                        S.op("dve", lambda e, r=r, g=g, tp=tp: e.scalar_tensor_tensor(out=tA[r], in0=tp, scalar=THW[:, g:g + 1], in1=tA[r], op0=ALU.mult, op1=ALU.add),
                             reads=[BtA[r], Bp], writes=[BtA[r]])
                        S.op("pool", lambda e, r=r: e.tensor_scalar(out=tA[r], in0=tA[r], scalar1=-PI, scalar2=PI, op0=ALU.max, op1=ALU.min),
                             reads=[BtA[r]], writes=[BtA[r]])
                        S.op("act", lambda e, r=r: e.activation(out=tB[r], in_=tA[r], func=AF.Abs), reads=[BtA[r]], writes=[BtB[r]])
                        S.op("act", lambda e, r=r: e.activation(out=SNt[r], in_=tA[r], func=AF.Sin), reads=[BtA[r]], writes=[BSN[r]])
                        S.op("act", lambda e, r=r: e.activation(out=CSt[r], in_=tB[r], func=AF.Sin, scale=-1.0, bias=halfpi), reads=[BtB[r], B_const], writes=[BCS[r]])
                        S.op("act", lambda e, r=r, psX2=psX2: e.copy(out=X2s[r], in_=psX2[:, :]), reads=[BK[xb + 1]], writes=[BX2[r]])
                        S.op("dve", lambda e, r=r, psX1=psX1: e.tensor_tensor(out=M1[r], in0=psX1[:, :], in1=CSt[r], op=ALU.mult), reads=[BK[xb], BCS[r]], writes=[BM1[r]])
                        S.op("pool", lambda e, r=r: e.tensor_tensor(out=X2s[r], in0=X2s[r], in1=SNt[r], op=ALU.mult), reads=[BX2[r], BSN[r]], writes=[BX2[r]])
                        S.op("pool", lambda e, r=r: e.tensor_tensor(out=M1[r], in0=M1[r], in1=X2s[r], op=ALU.add), reads=[BM1[r], BX2[r]], writes=[BM1[r]])
                        S.op("dve", lambda e, r=r, g=g, gl=gl: e.tensor_tensor_scan(out=sT[r], data0=Rr[:, g:g + 1].to_broadcast([128, 512]), data1=M1[r],
                                                                                   initial=carry[:, gl:gl + 1], op0=ALU.mult, op1=ALU.add),
                             reads=[BM1[r], Bp, Bcarry], writes=[BsT[r]])
                        S.op("act", lambda e, r=r, gl=gl: e.copy(out=carry[:, gl:gl + 1], in_=sT[r][:, 511:512]), reads=[BsT[r]], writes=[Bcarry])
                        if need:
                            S.op("pool", lambda e, r=r: e.tensor_tensor(out=Z1[r], in0=sT[r], in1=CSt[r], op=ALU.mult), reads=[BsT[r], BCS[r]], writes=[BZ1[r]])
                            S.op("dve", lambda e, r=r: e.tensor_tensor(out=Z2[r], in0=sT[r], in1=SNt[r], op=ALU.mult), reads=[BsT[r], BSN[r]], writes=[BZ2[r]])
                            S.op("pe", lambda e, r=r, gl=gl, ybk=ybk: e.matmul(banks[ybk][:, :], lhsT=WC[:, gl, 0:128], rhs=Z1[r], start=(gl == 0), stop=False),
                                 reads=[BW, BZ1[r]], writes=[BK[ybk]], pe_acc=(gl > 0))
                            S.op("pe", lambda e, r=r, gl=gl, ybk=ybk: e.matmul(banks[ybk][:, :], lhsT=WC[:, gl, 128:256], rhs=Z2[r], start=False, stop=(gl == 7)),
                                 reads=[BW, BZ2[r]], writes=[BK[ybk]], pe_acc=True)
                    if need:
                        lo = need_lo - c0
                        n = 512 - lo
                        yt, By = yb[b % 2], Byb[b % 2]
                        gt, Bg = gb[b % 2], Bgb[b % 2]
                        S.op("dve", lambda e, yt=yt, k=k, hkt=hkt, c0=c0, lo=lo, n=n, ybk=ybk: e.scalar_tensor_tensor(
                            out=yt[:, 0:n], in0=hkt[:, c0 + lo:c0 + 512], scalar=dcol[:, k:k + 1], in1=banks[ybk][:, lo:512], op0=ALU.mult, op1=ALU.add),
                            reads=[Bh, BK[ybk], Bp], writes=[By])
                        t2 = tA[0]
                        S.op("dve", lambda e, yt=yt, n=n, t2=t2: e.tensor_tensor(out=t2[:, 0:n], in0=yt[:, 0:n], in1=yt[:, 0:n], op=ALU.mult), reads=[By], writes=[BtA[0]])
                        S.op("dve", lambda e, n=n, t2=t2: e.tensor_scalar(out=t2[:, 0:n], in0=t2[:, 0:n], scalar1=0.044715, scalar2=1.0, op0=ALU.mult, op1=ALU.add),
                             reads=[BtA[0]], writes=[BtA[0]])
                        S.op("dve", lambda e, yt=yt, n=n, t2=t2: e.tensor_tensor(out=t2[:, 0:n], in0=t2[:, 0:n], in1=yt[:, 0:n], op=ALU.mult), reads=[BtA[0], By], writes=[BtA[0]])
                        S.op("act", lambda e, n=n, t2=t2: e.activation(out=t2[:, 0:n], in_=t2[:, 0:n], func=AF.Sigmoid, scale=1.5957691216057308), reads=[BtA[0]], writes=[BtA[0]])
                        S.op("dve", lambda e, yt=yt, gt=gt, n=n, t2=t2: e.tensor_tensor(out=gt[:, 0:n], in0=t2[:, 0:n], in1=yt[:, 0:n], op=ALU.mult), reads=[BtA[0], By], writes=[Bg])
                        a0 = need_lo - (P - 128)
                        S.dma("act", lambda e, gt=gt, k=k, a0=a0, n=n: e.dma_start(out=actA_d[k, :, a0:a0 + n], in_=gt[:, 0:n]), reads=[Bg], writes=[B_actA])
            S.barrier()
        if stop_after == 2:
            dbg_out["actA_d"] = actA_d

        def gemm_b(act_d, Bact, KCn, tok_blocks, steps, nw, epilogue, slab_bufs=3):
            AR_mark = AR.off
            maxnt = max(nt for _, nt in tok_blocks)
            act = AR.alloc([128, KCn, maxnt], BF16)
            Bact_sb = Buf("act_sb")
            slabs = [[AR.alloc([128, KCn, 128], BF16) for _ in range(nw)] for _ in range(slab_bufs)]
            Bsl = [[Buf("slab%d_%d" % (i, w)) for w in range(nw)] for i in range(slab_bufs)]
            ctr = 0
            for (t0, nt) in tok_blocks:
                step_kc = 8
                for kc0 in range(0, KCn, step_kc):
                    kc1 = min(KCn, kc0 + step_kc)
                    S.dma("sp", lambda e, kc0=kc0, kc1=kc1, t0=t0, nt=nt: e.dma_start(
                        out=act[:, kc0:kc1, 0:nt], in_=act_d.rearrange("c f t -> f c t")[:, kc0:kc1, t0:t0 + nt]),
                        reads=[Bact], writes=[Bact_sb])
                for si, step in enumerate(steps):
                    sb_i = ctr % slab_bufs
                    pbase = (ctr % 2) * nw
                    ctr += 1
                    for w in range(nw):
                        for (W2d, c0, ncols, doff) in step[w]:
                            S.dma("pool", lambda e, w=w, sb_i=sb_i, W2d=W2d, c0=c0, ncols=ncols, doff=doff: e.dma_start(
                                out=slabs[sb_i][w][:, :, doff:doff + ncols], in_=W2d[:, c0:c0 + ncols].rearrange("(kc p) n -> p kc n", p=128)),
                                writes=[Bsl[sb_i][w]])
                    for w in range(nw):
                        for kc in range(KCn):
                            S.op("pe", lambda e, w=w, kc=kc, sb_i=sb_i, pbase=pbase, nt=nt: e.matmul(
                                banks[pbase + w][:, 0:nt], lhsT=slabs[sb_i][w][:, kc, :], rhs=act[:, kc, 0:nt], start=(kc == 0), stop=(kc == KCn - 1)),
                                reads=[Bsl[sb_i][w], Bact_sb], writes=[BK[pbase + w]], pe_acc=(kc > 0))
                    epilogue(si, [banks[pbase + w][:, 0:nt] for w in range(nw)], [BK[pbase + w] for w in range(nw)], t0, nt)
            return AR_mark

        def tokblocks(lo, hi):
            out_ = []
            t = lo
            while t < hi:
                n = min(512, hi - t)
                out_.append((t, n))
                t += n
            return out_

        B_mT = Buf("mT_d", multi=True)
        if stop_after is None or stop_after >= 3:
            AR.reset()
            b1c = AR.alloc([128, KC], F32)
            b2c = AR.alloc([128, KC], F32)
            tmp32 = AR.alloc([128, 128], F32)
            Bb = Buf("bias")
            Bt32 = Buf("t32")
            load_cols(b1c, b_out1[0], KC, Bb, tmp32, Bt32, banks[7], BK[7])
            load_cols(b2c, b_out2[0], KC, Bb, tmp32, Bt32, banks[7], BK[7])
            S.barrier()
            sg = [AR.alloc([128, 512], F32) for _ in range(2)]
            Bsg = [Buf("sg%d" % i) for i in range(2)]
            mo = [AR.alloc([128, 512], F32) for _ in range(2)]
            Bmo = [Buf("mo%d" % i) for i in range(2)]
            cnt3 = [0]

            def epi3(si, ps, Bps, t0, nt):
                i = cnt3[0] % 2
                cnt3[0] += 1
                S.op("act", lambda e: e.activation(out=sg[i][:, 0:nt], in_=ps[1], func=AF.Sigmoid, bias=b2c[:, si:si + 1], scale=1.0),
                     reads=[Bps[1], Bb], writes=[Bsg[i]])
                S.op("dve", lambda e: e.scalar_tensor_tensor(out=mo[i][:, 0:nt], in0=ps[0], scalar=b1c[:, si:si + 1], in1=sg[i][:, 0:nt], op0=ALU.add, op1=ALU.mult),
                     reads=[Bps[0], Bsg[i], Bb], writes=[Bmo[i]])
                S.dma("act", lambda e: e.dma_start(out=mT_d[si, :, t0:t0 + nt], in_=mo[i][:, 0:nt]), reads=[Bmo[i]], writes=[B_mT])
            steps = [[[(w_out1[0], 128 * n, 128, 0)], [(w_out2[0], 128 * n, 128, 0)]] for n in range(KC)]
            gemm_b(actA_d, B_actA, KC, tokblocks(0, A), steps, 2, epi3)
            S.barrier()
        if stop_after == 3:
            dbg_out["mT_d"] = mT_d

        def epilogue_phase(h_src, h_dst, Bh_dst, g_post_row, g_next_row, dstT_d, Bdst, tok_lo, tok_hi, h_src_is_x=False, final_out=None):
            AR.reset()
            gp, Bgp = load_gain(g_post_row)
            if g_next_row is not None:
                gn, Bgn = load_gain(g_next_row)
                ntb_ = [nt_bufs("e0"), nt_bufs("e1")]
            mTt = [AR.alloc([128, KC, 128], F32) for _ in range(2)]
            BmTt = [Buf("mTt%d" % i) for i in range(2)]
            mtok = AR.alloc([128, D], F32)
            Bmtok = Buf("mtok")
            hres = [AR.alloc([128, D], F32) for _ in range(2)]
            Bhres = [Buf("hres%d" % i) for i in range(2)]
            ssb = AR.alloc([128, 4], F32)
            Bssb = Buf("ssb")
            junk = AR.alloc([128, D], BF16)
            Bjunk = Buf("junk")
            for j, a0 in enumerate(range(tok_lo, tok_hi, 128)):
                i = j % 2
                S.dma("sp", lambda e, i=i, a0=a0: e.dma_start(out=mTt[i], in_=mT_d.rearrange("c f t -> f c t")[:, :, a0:a0 + 128]), reads=[B_mT], writes=[BmTt[i]])
                S.dma("sp", lambda e, i=i, a0=a0: e.dma_start(out=hres[i], in_=h_src[a0:a0 + 128, :]), reads=([] if h_src_is_x else [Bh_dst]), writes=[Bhres[i]])
                for q in range(8):
                    bk = q % 4
                    for c4 in range(4):
                        c = 4 * q + c4
                        S.op("pe", lambda e, i=i, c=c, c4=c4, bk=bk: e.transpose(out=banks[bk][:, 128 * c4:128 * c4 + 128], in_=mTt[i][:, c, :], identity=ident_f),
                             reads=[BmTt[i], B_const], writes=[BK[bk]], pe_acc=(c4 > 0))
                    if q % 2 == 0:
                        S.op("act", lambda e, q=q, bk=bk: e.copy(out=mtok[:, 512 * q:512 * q + 512], in_=banks[bk][:, :]), reads=[BK[bk]], writes=[Bmtok])
                    else:
                        S.op("dve", lambda e, q=q, bk=bk: e.tensor_copy(out=mtok[:, 512 * q:512 * q + 512], in_=banks[bk][:, :]), reads=[BK[bk]], writes=[Bmtok])
                ss, rstd = ssb[:, 0:1], ssb[:, 1:2]
                S.op("act", lambda e: e.activation(out=junk, in_=mtok, func=AF.Square), reads=[Bmtok], writes=[Bjunk])
                S.op("dve", lambda e: e.reduce_sum(out=ss, in_=junk, axis=AX.X), reads=[Bjunk], writes=[Bssb])
                rstd_from_ss(ss, rstd, [Bssb], [Bssb])
                S.op("dve", lambda e: e.scalar_tensor_tensor(out=mtok, in0=mtok, scalar=rstd, in1=gp, op0=ALU.mult, op1=ALU.mult),
                     reads=[Bmtok, Bssb, Bgp], writes=[Bmtok])
                S.op("dve", lambda e, i=i: e.tensor_tensor(out=hres[i], in0=hres[i], in1=mtok, op=ALU.add), reads=[Bmtok, Bhres[i]], writes=[Bhres[i]])
                if final_out is not None:
                    S.dma("act", lambda e, i=i, a0=a0: e.dma_start(out=final_out[a0 - tok_lo:a0 - tok_lo + 128, :], in_=hres[i]), reads=[Bhres[i]], writes=[Bh_dst])
                else:
                    S.dma("act", lambda e, i=i, a0=a0: e.dma_start(out=h_dst[a0:a0 + 128, :], in_=hres[i]), reads=[Bhres[i]], writes=[Bh_dst])
                if g_next_row is not None:
                    norm_transpose(hres[i], Bhres[i], gn, Bgn, dstT_d, Bdst, a0, ntb_[i], j)
            S.barrier()

        B_h = Buf("h_d", multi=True)
        B_actB = Buf("actB_d", multi=True)
        if stop_after is None or stop_after >= 4:
            epilogue_phase(xcat[P - 128:T, :], h_d, B_h, g_post_mix[0], g_pre_ffn[0], actB_d, B_actB, 0, A, h_src_is_x=True)
        if stop_after == 4:
            dbg_out["h_d"] = h_d
            dbg_out["actB_d"] = actB_d

        def ffn_phase(layer, actin_d, Bactin, tok_lo, tok_hi):
            wg, wu, wd = w_gate[layer], w_up[layer], w_down[layer]
            for (t0, nt) in tokblocks(tok_lo, tok_hi):
                AR.reset()
                aT = AR.alloc([128, HC, 512], BF16)
                BaT = Buf("aT")
                sg = [AR.alloc([128, 512], F32) for _ in range(2)]
                Bsg = [Buf("fsg%d" % i) for i in range(2)]
                cnt = [0]

                def epi_gu(si, ps, Bps, t0_, nt_):
                    i = cnt[0] % 2
                    cnt[0] += 1
                    S.op("act", lambda e: e.activation(out=sg[i][:, 0:nt_], in_=ps[0], func=AF.Silu), reads=[Bps[0]], writes=[Bsg[i]])
                    S.op("dve", lambda e: e.tensor_tensor(out=aT[:, si, 0:nt_], in0=ps[1], in1=sg[i][:, 0:nt_], op=ALU.mult), reads=[Bps[1], Bsg[i]], writes=[BaT])
                steps = [[[(wg, 128 * n, 128, 0)], [(wu, 128 * n, 128, 0)]] for n in range(HC)]
                mark = gemm_b(actin_d, Bactin, KC, [(t0, nt)], steps, 2, epi_gu)
                S.barrier()
                AR.off = mark
                fo = [AR.alloc([128, 512], F32) for _ in range(2)]
                Bfo = [Buf("fo%d" % i) for i in range(2)]
                slabs = [AR.alloc([128, HC, 128], BF16) for _ in range(2)]
                Bsl = [Buf("dsl%d" % i) for i in range(2)]
                for n in range(KC):
                    i = n % 2
                    for h0 in range(0, HC, 43):
                        S.dma("pool", lambda e, i=i, n=n, h0=h0: e.dma_start(
                            out=slabs[i][:, h0:h0 + 43, :], in_=wd[128 * h0:128 * (h0 + 43), 128 * n:128 * n + 128].rearrange("(kc p) n -> p kc n", p=128)),
                            writes=[Bsl[i]])
                    for kc in range(HC):
                        S.op("pe", lambda e, i=i, kc=kc, nt=nt: e.matmul(banks[i][:, 0:nt], lhsT=slabs[i][:, kc, :], rhs=aT[:, kc, 0:nt], start=(kc == 0), stop=(kc == HC - 1)),
                             reads=[Bsl[i], BaT], writes=[BK[i]], pe_acc=(kc > 0))
                    if i == 0:
                        S.op("act", lambda e, i=i, nt=nt: e.copy(out=fo[i][:, 0:nt], in_=banks[i][:, 0:nt]), reads=[BK[i]], writes=[Bfo[i]])
                    else:
                        S.op("dve", lambda e, i=i, nt=nt: e.tensor_copy(out=fo[i][:, 0:nt], in_=banks[i][:, 0:nt]), reads=[BK[i]], writes=[Bfo[i]])
                    S.dma("act", lambda e, i=i, n=n, t0=t0, nt=nt: e.dma_start(out=mT_d[n, :, t0:t0 + nt], in_=fo[i][:, 0:nt]), reads=[Bfo[i]], writes=[B_mT])
                S.barrier()

        if stop_after is None or stop_after >= 5:
            ffn_phase(0, actB_d, B_actB, 0, A)
        if stop_after == 5:
            dbg_out["mT_d"] = mT_d
        if stop_after is None or stop_after >= 6:
            epilogue_phase(h_d, h_d, B_h, g_post_ffn[0], g_pre_mix[1], actA_d, B_actA, 0, A)
        if stop_after == 6:
            dbg_out["h_d"] = h_d
            dbg_out["actA_d"] = actA_d

        B_qT = Buf("qT_d", multi=True)
        B_kT = Buf("kT_d", multi=True)
        B_v = Buf("v_d", multi=True)
        if stop_after is None or stop_after >= 7:
            AR.reset()
            wq = w_qkv[0]
            COS = AR.alloc([128, A], F32)
            SINS = AR.alloc([128, A], F32)
            bq = AR.alloc([128, 40], F32)
            bqs = AR.alloc([128, 40], F32)
            bkd = AR.alloc([128, 8], F32)
            bkds = AR.alloc([128, 8], F32)
            invf = AR.alloc([128, 2], F32)
            sgn = AR.alloc([128, 2], F32)
            pi_i = AR.alloc([128, 2], I32)
            posi = AR.alloc([128, A], I32)
            wk_ = AR.alloc([128, A], F32)
            wk2 = AR.alloc([128, A], F32)
            tmp32 = AR.alloc([128, 128], F32)
            Bt32 = Buf("t32")
            Bat = Buf("attnprep")
            load_cols(bq, b_qkv[0], 40, Bat, tmp32, Bt32, banks[7], BK[7])
            S.barrier()
            bsw = b_qkv[0, 0:5120].rearrange("(c h t i) -> c h t i", h=2, t=2, i=32)
            for hh in range(2):
                for tt in range(2):
                    S.dma("sp", lambda e, hh=hh, tt=tt: e.dma_start(out=tmp32[0:40, 64 * hh + 32 * tt:64 * hh + 32 * tt + 32], in_=bsw[:, hh, 1 - tt, :]), writes=[Bt32])
            S.op("pe", lambda e: e.transpose(out=banks[7][:, 0:40], in_=tmp32[0:40, :], identity=ident_f[0:40, 0:40]), reads=[Bt32, B_const], writes=[BK[7]])
            S.op("dve", lambda e: e.tensor_copy(out=bqs, in_=banks[7][:, 0:40]), reads=[BK[7]], writes=[Bat])
            S.barrier()
            bkv = b_qkv[0, 4096:4608].rearrange("(j d) -> j d", d=64)
            bkvs = b_qkv[0, 4096:4608].rearrange("(j t i) -> j t i", t=2, i=32)
            for hh in range(2):
                S.dma("sp", lambda e, hh=hh: e.dma_start(out=tmp32[0:8, 64 * hh:64 * hh + 64], in_=bkv), writes=[Bt32])
            S.op("pe", lambda e: e.transpose(out=banks[7][:, 0:8], in_=tmp32[0:8, :], identity=ident_f[0:8, 0:8]), reads=[Bt32, B_const], writes=[BK[7]])
            S.op("dve", lambda e: e.tensor_copy(out=bkd, in_=banks[7][:, 0:8]), reads=[BK[7]], writes=[Bat])
            S.barrier()
            for hh in range(2):
                for tt in range(2):
                    S.dma("sp", lambda e, hh=hh, tt=tt: e.dma_start(out=tmp32[0:8, 64 * hh + 32 * tt:64 * hh + 32 * tt + 32], in_=bkvs[:, 1 - tt, :]), writes=[Bt32])
            S.op("pe", lambda e: e.transpose(out=banks[7][:, 0:8], in_=tmp32[0:8, :], identity=ident_f[0:8, 0:8]), reads=[Bt32, B_const], writes=[BK[7]])
            S.op("dve", lambda e: e.tensor_copy(out=bkds, in_=banks[7][:, 0:8]), reads=[BK[7]], writes=[Bat])
            S.op("pool", lambda e: e.iota(pi_i, pattern=[[0, 2]], base=0, channel_multiplier=1), writes=[Bat])
            S.op("dve", lambda e: e.tensor_single_scalar(out=pi_i[:, 1:2], in_=pi_i[:, 0:1], scalar=31, op=ALU.bitwise_and), reads=[Bat], writes=[Bat])
            S.op("dve", lambda e: e.tensor_copy(out=invf[:, 0:1], in_=pi_i[:, 1:2]), reads=[Bat], writes=[Bat])
            S.op("act", lambda e: e.activation(out=invf[:, 0:1], in_=invf[:, 0:1], func=AF.Exp, scale=float(-np.log(10000.0) / 32.0)), reads=[Bat], writes=[Bat])
            S.op("dve", lambda e: e.tensor_scalar(out=invf[:, 1:2], in0=invf[:, 0:1], scalar1=1.0 / TWO_PI, scalar2=None, op0=ALU.mult), reads=[Bat], writes=[Bat])
            S.op("dve", lambda e: e.tensor_single_scalar(out=pi_i[:, 1:2], in_=pi_i[:, 0:1], scalar=32, op=ALU.bitwise_and), reads=[Bat], writes=[Bat])
            S.op("dve", lambda e: e.tensor_copy(out=sgn[:, 0:1], in_=pi_i[:, 1:2]), reads=[Bat], writes=[Bat])
            S.op("dve", lambda e: e.tensor_scalar(out=sgn[:, 0:1], in0=sgn[:, 0:1], scalar1=1.0 / 16.0, scalar2=-1.0, op0=ALU.mult, op1=ALU.add), reads=[Bat], writes=[Bat])
            S.dma("sp", lambda e: e.dma_start(out=posi, in_=posa[0].partition_broadcast(128)), writes=[Bat])
            S.barrier()
            Vq = lambda fn: S.op("dve", fn, reads=[Bat], writes=[Bat])
            Vq(lambda e: e.tensor_copy(out=wk_, in_=posi))
            Vq(lambda e: e.tensor_scalar(out=wk2, in0=wk_, scalar1=invf[:, 1:2], scalar2=MAGIC, op0=ALU.mult, op1=ALU.add))
            Vq(lambda e: e.tensor_scalar(out=wk2, in0=wk2, scalar1=-MAGIC, scalar2=-TWO_PI, op0=ALU.add, op1=ALU.mult))
            Vq(lambda e: e.scalar_tensor_tensor(out=wk2, in0=wk_, scalar=invf[:, 0:1], in1=wk2, op0=ALU.mult, op1=ALU.add))
            Vq(lambda e: e.tensor_scalar(out=wk2, in0=wk2, scalar1=-PI, scalar2=PI, op0=ALU.max, op1=ALU.min))
            S.op("act", lambda e: e.activation(out=SINS, in_=wk2, func=AF.Sin), reads=[Bat], writes=[Bat])
            Vq(lambda e: e.tensor_scalar(out=SINS, in0=SINS, scalar1=sgn[:, 0:1], scalar2=None, op0=ALU.mult))
            S.op("act", lambda e: e.activation(out=wk2, in_=wk2, func=AF.Abs), reads=[Bat], writes=[Bat])
            S.op("act", lambda e: e.activation(out=COS, in_=wk2, func=AF.Sin, scale=-1.0, bias=halfpi), reads=[Bat, B_const], writes=[Bat])
            S.barrier()
            keep = AR.off
            t1 = [AR.alloc([128, 512], F32) for _ in range(2)]
            t2 = [AR.alloc([128, 512], F32) for _ in range(2)]
            qo = [AR.alloc([128, 512], BF16) for _ in range(2)]
            Bt1 = [Buf("rt1%d" % i) for i in range(2)]
            Bt2 = [Buf("rt2%d" % i) for i in range(2)]
            Bqo = [Buf("qo%d" % i) for i in range(2)]
            cntq = [0]

            def epi_rope(si, ps, Bps, t0, nt):
                i = cntq[0] % 2
                cntq[0] += 1
                if si < 32:
                    bc, bsc, dst, Bd, ci = bq[:, si:si + 1], bqs[:, si:si + 1], qT_d, B_qT, si
                else:
                    j = si - 32
                    bc, bsc, dst, Bd, ci = bkd[:, j:j + 1], bkds[:, j:j + 1], kT_d, B_kT, j
                S.op("dve", lambda e: e.scalar_tensor_tensor(out=t1[i][:, 0:nt], in0=ps[0], scalar=bc, in1=COS[:, t0:t0 + nt], op0=ALU.add, op1=ALU.mult),
                     reads=[Bps[0], Bat], writes=[Bt1[i]])
                S.op("dve", lambda e: e.scalar_tensor_tensor(out=t2[i][:, 0:nt], in0=ps[1], scalar=bsc, in1=SINS[:, t0:t0 + nt], op0=ALU.add, op1=ALU.mult),
                     reads=[Bps[1], Bat], writes=[Bt2[i]])
                S.op("dve", lambda e: e.tensor_tensor(out=qo[i][:, 0:nt], in0=t1[i][:, 0:nt], in1=t2[i][:, 0:nt], op=ALU.add), reads=[Bt1[i], Bt2[i]], writes=[Bqo[i]])
                S.dma("act", lambda e: e.dma_start(out=dst[ci, :, t0:t0 + nt], in_=qo[i][:, 0:nt]), reads=[Bqo[i]], writes=[Bd])

            def swap_pieces(cb):
                return [(wq, cb + 32, 32, 0), (wq, cb, 32, 32), (wq, cb + 96, 32, 64), (wq, cb + 64, 32, 96)]
            steps = []
            for c in range(32):
                steps.append([[(wq, 128 * c, 128, 0)], swap_pieces(128 * c)])
            for j in range(8):
                cb = 4096 + 64 * j
                steps.append([[(wq, cb, 64, 0), (wq, cb, 64, 64)],
                              [(wq, cb + 32, 32, 0), (wq, cb, 32, 32), (wq, cb + 32, 32, 64), (wq, cb, 32, 96)]])
            gemm_b(actA_d, B_actA, KC, tokblocks(0, A), steps, 2, epi_rope)
            S.barrier()
            AR.off = keep
            wv = AR.alloc([128, KC, 512], BF16)
            Bwv = Buf("wv")
            bvb = AR.alloc([128, 512], F32)
            actv = [AR.alloc([128, KC, 128], BF16) for _ in range(2)]
            Bactv = [Buf("actv%d" % i) for i in range(2)]
            vo = [AR.alloc([128, 512], BF16) for _ in range(2)]
            Bvo = [Buf("vo%d" % i) for i in range(2)]
            for q in range(4):
                S.dma("pool", lambda e, q=q: e.dma_start(out=wv[:, 8 * q:8 * q + 8, :], in_=wq[1024 * q:1024 * q + 1024, 4608:5120].rearrange("(kc p) n -> p kc n", p=128)), writes=[Bwv])
            S.dma("sp", lambda e: e.dma_start(out=bvb, in_=b_qkv[0, 4608:5120].partition_broadcast(128)), writes=[Bwv])
            for j in range(A // 128):
                i = j % 2
                S.dma("sp", lambda e, i=i, j=j: e.dma_start(out=actv[i], in_=actA_d.rearrange("c f t -> f c t")[:, :, 128 * j:128 * j + 128]), reads=[B_actA], writes=[Bactv[i]])
                for kc in range(KC):
                    S.op("pe", lambda e, i=i, kc=kc: e.matmul(banks[i][:, :], lhsT=actv[i][:, kc, :], rhs=wv[:, kc, :], start=(kc == 0), stop=(kc == KC - 1)),
                         reads=[Bactv[i], Bwv], writes=[BK[i]], pe_acc=(kc > 0))
                S.op("dve", lambda e, i=i: e.tensor_tensor(out=vo[i], in0=banks[i][:, :], in1=bvb, op=ALU.add), reads=[BK[i], Bwv], writes=[Bvo[i]])
                S.dma("act", lambda e, i=i, j=j: e.dma_start(out=v_d[128 * j:128 * j + 128, :], in_=vo[i]), reads=[Bvo[i]], writes=[B_v])
            S.barrier()
            if stop_after == 7:
                dbg_out["qT_d"] = qT_d
                dbg_out["kT_d"] = kT_d
                dbg_out["v_d"] = v_d

        if stop_after is None or stop_after >= 8:
            AR.reset()
            NT = A // 128
            mask = AR.alloc([128, 256], F32)
            mask0 = AR.alloc([128, 256], F32)
            sinkb = AR.alloc([128, NQH], F32)
            Bm = Buf("mask")
            S.op("pool", lambda e: e.memset(mask, 0.0), writes=[Bm])
            S.op("pool", lambda e: e.affine_select(out=mask, in_=mask, pattern=[[1, 256]], compare_op=ALU.is_ge, fill=NEG, base=-1, channel_multiplier=-1), reads=[Bm], writes=[Bm])
            S.op("pool", lambda e: e.affine_select(out=mask, in_=mask, pattern=[[-1, 256]], compare_op=ALU.is_ge, fill=NEG, base=128, channel_multiplier=1), reads=[Bm], writes=[Bm])
            S.dma("sp", lambda e: e.dma_start(out=mask0[:, 0:128], in_=halo_mask), writes=[Bm])
            S.dma("sp", lambda e: e.dma_start(out=sinkb, in_=sinks[0].partition_broadcast(128)), writes=[Bm])
            S.barrier()
            S.op("dve", lambda e: e.tensor_tensor(out=mask0[:, 0:128], in0=mask0[:, 0:128], in1=mask[:, 0:128], op=ALU.add), reads=[Bm], writes=[Bm])
            S.op("dve", lambda e: e.tensor_copy(out=mask0[:, 128:256], in_=mask[:, 128:256]), reads=[Bm], writes=[Bm])
            S.barrier()
            kTj = AR.alloc([128, A], BF16)
            BkTj = Buf("kTj")
            vj = AR.alloc([128, NT, 64], BF16)
            Bvj = Buf("vj")
            vpad = [AR.alloc([128, NT, 128], BF16) for _ in range(2)]
            Bvpad = Buf("vpad")
            qTc = [AR.alloc([128, A], BF16) for _ in range(2)]
            BqTc = [Buf("qTc%d" % i) for i in range(2)]
            aTc = [AR.alloc([128, M], BF16) for _ in range(2)]
            BaTc = [Buf("aTc%d" % i) for i in range(2)]
            NRr = 2
            s1 = [AR.alloc([128, 256], F32) for _ in range(NRr)]
            ex = [AR.alloc([128, 256], F32) for _ in range(NRr)]
            pb_ = [AR.alloc([128, 256], BF16) for _ in range(NRr)]
            pT = [AR.alloc([128, 256], BF16) for _ in range(NRr)]
            st_ = [AR.alloc([128, 8], F32) for _ in range(NRr)]
            Bs1 = [Buf("s1%d" % i) for i in range(NRr)]
            Bex = [Buf("ex%d" % i) for i in range(NRr)]
            Bpb = [Buf("pb%d" % i) for i in range(NRr)]
            BpT = [Buf("pT%d" % i) for i in range(NRr)]
            Bst = [Buf("st%d" % i) for i in range(NRr)]
            SC = 1.0 / 8.0
            B_attnT = B_actB
            it = 0
            for j in range(NKV):
                S.dma("sp", lambda e, j=j: e.dma_start(out=kTj, in_=kT_d[j]), reads=[B_kT], writes=[BkTj])
                S.dma("sp", lambda e, j=j: e.dma_start(out=vj, in_=v_d[:, 64 * j:64 * j + 64].rearrange("(t p) d -> p t d", p=128)), reads=[B_v], writes=[Bvj])
                S.op("pool", lambda e: e.memset(vpad[0], 0.0), writes=[Bvpad])
                S.op("pool", lambda e: e.memset(vpad[1], 0.0), writes=[Bvpad])
                S.op("dve", lambda e: e.tensor_copy(out=vpad[0][:, :, 0:64], in_=vj), reads=[Bvj, Bvpad], writes=[Bvpad])
                S.op("dve", lambda e: e.tensor_copy(out=vpad[1][:, :, 64:128], in_=vj), reads=[Bvj, Bvpad], writes=[Bvpad])
                for ci in range(4):
                    c = 4 * j + ci
                    qt, Bq = qTc[c % 2], BqTc[c % 2]
                    at, Ba = aTc[c % 2], BaTc[c % 2]
                    S.dma("sp", lambda e, qt=qt, c=c: e.dma_start(out=qt, in_=qT_d[c]), reads=[B_qT], writes=[Bq])
                    for n in range(M // 128):
                        a0 = 128 + 128 * n
                        ob = 6 + (n % 2)
                        for hh in range(2):
                            r = it % NRr
                            it += 1
                            sb_ = 2 * (it % 2)
                            hsl = slice(64 * hh, 64 * hh + 64)
                            head = 2 * c + hh
                            S.op("pe", lambda e, qt=qt, hsl=hsl, a0=a0, sb_=sb_: e.matmul(banks[sb_][:, 0:256], lhsT=qt[hsl, a0:a0 + 128], rhs=kTj[hsl, a0 - 128:a0 + 128], start=True, stop=True),
                                 reads=[Bq, BkTj], writes=[BK[sb_]])
                            mk = mask0 if n == 0 else mask
                            S.op("dve", lambda e, r=r, sb_=sb_, mk=mk: e.tensor_tensor(out=s1[r], in0=banks[sb_][:, 0:256], in1=mk, op=ALU.add), reads=[BK[sb_], Bm], writes=[Bs1[r]])
                            S.op("dve", lambda e, r=r: e.reduce_max(out=st_[r][:, 0:1], in_=s1[r], axis=AX.X), reads=[Bs1[r]], writes=[Bst[r]])
                            S.op("dve", lambda e, r=r: e.tensor_scalar(out=st_[r][:, 1:2], in0=st_[r][:, 0:1], scalar1=-SC, scalar2=None, op0=ALU.mult), reads=[Bst[r]], writes=[Bst[r]])
                            S.op("act", lambda e, r=r: e.activation(out=ex[r], in_=s1[r], func=AF.Exp, bias=st_[r][:, 1:2], scale=SC),
                                 reads=[Bs1[r], Bst[r]], writes=[Bex[r]])
                            S.op("dve", lambda e, r=r: e.reduce_sum(out=st_[r][:, 2:3], in_=ex[r], axis=AX.X), reads=[Bex[r]], writes=[Bst[r]])
                            S.op("act", lambda e, r=r, head=head: e.activation(out=st_[r][:, 3:4], in_=st_[r][:, 1:2], func=AF.Exp, bias=sinkb[:, head:head + 1], scale=1.0),
                                 reads=[Bst[r], Bm], writes=[Bst[r]])
                            S.op("dve", lambda e, r=r: e.tensor_tensor(out=st_[r][:, 4:5], in0=st_[r][:, 2:3], in1=st_[r][:, 3:4], op=ALU.add), reads=[Bst[r]], writes=[Bst[r]])
                            S.op("dve", lambda e, r=r: e.reciprocal(out=st_[r][:, 5:6], in_=st_[r][:, 4:5]), reads=[Bst[r]], writes=[Bst[r]])
                            S.op("dve", lambda e, r=r: e.tensor_scalar(out=pb_[r], in0=ex[r], scalar1=st_[r][:, 5:6], scalar2=None, op0=ALU.mult), reads=[Bex[r], Bst[r]], writes=[Bpb[r]])
                            ptb = banks[sb_ + 1][:].bitcast(BF16)
                            S.op("pe", lambda e, r=r, ptb=ptb: e.transpose(out=ptb[:, 0:128], in_=pb_[r][:, 0:128], identity=ident_b), reads=[Bpb[r], B_const], writes=[BK[sb_ + 1]])
                            S.op("pe", lambda e, r=r, ptb=ptb: e.transpose(out=ptb[:, 128:256], in_=pb_[r][:, 128:256], identity=ident_b), reads=[Bpb[r], B_const], writes=[BK[sb_ + 1]], pe_acc=True)
                            S.op("act", lambda e, r=r, ptb=ptb: e.copy(out=pT[r], in_=ptb[:, 0:256]), reads=[BK[sb_ + 1]], writes=[BpT[r]])
                            tl = a0 // 128
                            S.op("pe", lambda e, r=r, hh=hh, tl=tl, ob=ob: e.matmul(banks[ob][:, 0:128], lhsT=vpad[hh][:, tl - 1, :], rhs=pT[r][:, 0:128], start=(hh == 0), stop=False),
                                 reads=[Bvpad, BpT[r]], writes=[BK[ob]], pe_acc=(hh > 0))
                            S.op("pe", lambda e, r=r, hh=hh, tl=tl, ob=ob: e.matmul(banks[ob][:, 0:128], lhsT=vpad[hh][:, tl, :], rhs=pT[r][:, 128:256], start=False, stop=(hh == 1)),
                                 reads=[Bvpad, BpT[r]], writes=[BK[ob]], pe_acc=True)
                        S.op("act", lambda e, at=at, n=n, ob=ob: e.copy(out=at[:, 128 * n:128 * n + 128], in_=banks[ob][:, 0:128]), reads=[BK[ob]], writes=[Ba])
                    S.dma("act", lambda e, at=at, c=c: e.dma_start(out=actB_d[c, :, 128:A], in_=at), reads=[Ba], writes=[B_attnT])
            S.barrier()
            if stop_after == 8:
                dbg_out["actB_d"] = actB_d

        if stop_after is None or stop_after >= 9:
            AR.reset()
            boc = AR.alloc([128, KC], F32)
            tmp32 = AR.alloc([128, 128], F32)
            Bb = Buf("bo")
            Bt32 = Buf("t32")
            load_cols(boc, b_o[0], KC, Bb, tmp32, Bt32, banks[7], BK[7])
            S.barrier()
            mo = [AR.alloc([128, 512], F32) for _ in range(2)]
            Bmo = [Buf("omo%d" % i) for i in range(2)]
            cnt9 = [0]

            def epi9(si, ps, Bps, t0, nt):
                i = cnt9[0] % 2
                cnt9[0] += 1
                S.op("act", lambda e: e.activation(out=mo[i][:, 0:nt], in_=ps[0], func=AF.Identity, bias=boc[:, si:si + 1], scale=1.0), reads=[Bps[0], Bb], writes=[Bmo[i]])
                S.dma("act", lambda e: e.dma_start(out=mT_d[si, :, t0:t0 + nt], in_=mo[i][:, 0:nt]), reads=[Bmo[i]], writes=[B_mT])
            steps = [[[(w_o[0], 128 * n, 128, 0)]] for n in range(KC)]
            gemm_b(actB_d, B_actB, KC, tokblocks(128, A), steps, 1, epi9)
            S.barrier()
            epilogue_phase(h_d, h_d, B_h, g_post_mix[1], g_pre_ffn[1], actA_d, B_actA, 128, A)
        if stop_after == 9:
            dbg_out["h_d"] = h_d
        if stop_after is None or stop_after >= 10:
            ffn_phase(1, actA_d, B_actA, 128, A)
            B_out = Buf("out", multi=True)
            epilogue_phase(h_d, None, B_out, g_post_ffn[1], None, None, None, 128, A, final_out=out)

        if dump_all:
            dbg_out.update(dict(h_d=h_d, actA_d=actA_d, actB_d=actB_d, mT_d=mT_d, qT_d=qT_d, kT_d=kT_d, v_d=v_d))
        dbg_specs = {}
        if dbg_out:
            AR.reset()
            for name, src in dbg_out.items():
                shp = list(src.shape)
                dt = src.dtype
                o = nc.dram_tensor("dbg_" + name, shp, dt, kind="ExternalOutput").ap()
                dbg_specs[name] = (shp, dt)
                flat_s = src.rearrange("a b c -> (a b) c") if len(shp) == 3 else src
                flat_o = o.rearrange("a b c -> (a b) c") if len(shp) == 3 else o
                rows = flat_s.shape[0]
                cols = flat_s.shape[1]
                tb = AR.alloc([128, cols], dt)
                Btb = Buf("dbgt")
                for r0 in range(0, rows, 128):
                    S.dma("sp", lambda e, r0=r0, tb=tb, flat_s=flat_s: e.dma_start(out=tb, in_=flat_s[r0:r0 + 128, :]), writes=[Btb])
                    S.dma("sp", lambda e, r0=r0, tb=tb, flat_o=flat_o: e.dma_start(out=flat_o[r0:r0 + 128, :], in_=tb), reads=[Btb], writes=[Buf("dbgo")])
        S.barrier()
        S.emit()
    return nc, S


WEIGHT_KEYS = ["norm_pre_mix", "norm_post_mix", "norm_pre_ffn", "norm_post_ffn", "s5_lam_re", "s5_lam_im", "s5_log_step",
               "s5_b_re", "s5_b_im", "s5_c_re", "s5_c_im", "s5_d", "s5_w_out1", "s5_b_out1", "s5_w_out2", "s5_b_out2",
               "attn_w_qkv", "attn_b_qkv", "attn_w_o", "attn_b_o", "attn_sinks", "ffn_w_gate", "ffn_w_up", "ffn_w_down"]

_CACHE = {}


def make_in_maps(inputs, P, M, n_cores, seq):
    x = np.asarray(inputs["x"])
    pos = np.asarray(inputs["positions"])
    w = {k: np.ascontiguousarray(np.asarray(inputs[k], dtype=np.float32)) for k in WEIGHT_KEYS}
    in_maps = []
    for core in range(n_cores):
        b, s = core // 2, core % 2
        xc = np.zeros((P + M, D), np.float32)
        pa = np.zeros((1, 128 + M), np.int32)
        if s == 0:
            xc[P:] = x[b, 0:M]
            pa[0, 128:] = pos[b, 0:M]
            hm = np.full((128, 128), NEG, np.float32)
        else:
            xc[:] = x[b, 0:P + M]
            pa[0, :] = pos[b, P - 128:P + M]
            hm = np.zeros((128, 128), np.float32)
        m = dict(w)
        m["xcat"] = xc
        m["posa"] = pa
        m["halo_mask"] = hm
        in_maps.append(m)
    return in_maps


def kernel(**inputs):
    P = M = 2048
    n_cores = 8
    if "prog" not in _CACHE:
        _CACHE["prog"] = build_program(P, M)[0]
    nc = _CACHE["prog"]
    in_maps = make_in_maps(inputs, P, M, n_cores, 4096)
    res = run_bass_kernel_spmd(nc, in_maps, core_ids=list(range(n_cores)))
    out = np.zeros((4, 4096, D), np.float32)
    for core in range(n_cores):
        b, s = core // 2, core % 2
        out[b, s * M:(s + 1) * M] = res.results[core]["out"]
    return out
```

```python
import contextlib
import numpy as np
import concourse.bass as bass
import concourse.mybir as mybir
from concourse.bass_utils import run_bass_kernel_spmd

F32 = mybir.dt.float32
BF16 = mybir.dt.bfloat16
I32 = mybir.dt.int32
ALU = mybir.AluOpType
AF = mybir.ActivationFunctionType
AX = mybir.AxisListType

D = 4096
KC = D // 128
DFF = 11008
HC = DFF // 128
NG = 256
NQH, NKV, HD = 64, 8, 64
EPS = 1e-6
MAGIC = 12582912.0
TWO_PI = float(2 * np.pi)
PI = float(np.pi)
NEG = -30000.0


class Buf:
    __slots__ = ("name", "lw", "rd", "guard", "multi")

    def __init__(self, name, multi=False):
        self.name = name
        self.lw = []
        self.rd = []
        self.guard = []
        self.multi = multi


class Sched:
    ENG = ("pe", "act", "dve", "pool", "sp")

    def __init__(self, nc, ndma=(("sp", 8), ("act", 4), ("pool", 2))):
        self.nc = nc
        self.prog = {e: [] for e in self.ENG}
        self.cnt = {e: 0 for e in self.ENG}
        self.waited = {e: {} for e in self.ENG}
        self.ndma = dict(ndma)
        self.dma_i = {q: 0 for q in self.ndma}
        self.n_inst = 0
        self.delay_fn = None
        self.delay_buf = Buf("delay")

    def _wait(self, eng, k, v):
        if self.waited[eng].get(k, 0) < v:
            self.waited[eng][k] = v
            self.prog[eng].append(("wait", k, v))

    def _deps(self, eng, reads, writes, pe_acc=False):
        need = {}

        def add(tok):
            if need.get(tok[0], 0) < tok[1]:
                need[tok[0]] = tok[1]
        for b in reads:
            for t in b.lw:
                add(t)
        for b in writes:
            if not b.multi:
                for t in b.lw:
                    if not (pe_acc and t[0] == "c_pe"):
                        add(t)
            for t in b.rd:
                add(t)
            for t in b.guard:
                add(t)
        for k, v in need.items():
            self._wait(eng, k, v)

    def _mark(self, tok, reads, writes):
        for b in reads:
            b.rd.append(tok)
        for b in writes:
            if b.multi:
                if b.rd:
                    b.guard = b.rd
                    b.rd = []
                    b.lw = []
                b.lw.append(tok)
            else:
                b.lw = [tok]
                b.rd = []

    def op(self, eng, fn, reads=(), writes=(), pe_acc=False):
        self._deps(eng, reads, writes, pe_acc)
        self.cnt[eng] += 1
        tok = ("c_" + eng, self.cnt[eng])
        self.prog[eng].append(("op", fn, tok[0], 1))
        self._mark(tok, reads, writes)
        self.n_inst += 1

    def dma(self, q, fn, reads=(), writes=()):
        self._deps(q, reads, writes)
        K = self.ndma[q]
        i = self.dma_i[q]
        self.dma_i[q] += 1
        slot, n = i % K, i // K + 1
        key = "d_%s_%d" % (q, slot)
        if n > 1:
            self._wait(q, key, 16 * (n - 1))
        tok = (key, 16 * n)
        self.prog[q].append(("op", fn, key, 16))
        self._mark(tok, reads, writes)
        self.n_inst += 1

    def barrier(self, engs=None):
        self._barrier_round(engs)
        if self.delay_fn is not None:
            for _ in range(10):
                self.op("dve", self.delay_fn, reads=[self.delay_buf], writes=[self.delay_buf])
            self._barrier_round(engs)

    def _barrier_round(self, engs=None):
        cur = {}
        for e in self.ENG:
            if self.cnt[e]:
                cur["c_" + e] = self.cnt[e]
        for q, K in self.ndma.items():
            for s in range(K):
                n = (self.dma_i[q] - s + K - 1) // K
                if n > 0:
                    cur["d_%s_%d" % (q, s)] = 16 * n
        for e in (engs or self.ENG):
            for k, v in cur.items():
                self._wait(e, k, v)

    def emit(self):
        nc = self.nc
        keys = ["c_" + e for e in self.ENG]
        for q, K in self.ndma.items():
            keys += ["d_%s_%d" % (q, s) for s in range(K)]
        sems = {}
        with contextlib.ExitStack() as st:
            for k in keys:
                sems[k] = st.enter_context(nc.semaphore(k))
            block = st.enter_context(nc.Block())

            def run(name):
                def body(eng):
                    for it in self.prog[name]:
                        if it[0] == "wait":
                            eng.wait_ge(sems[it[1]], it[2])
                        else:
                            it[1](eng).then_inc(sems[it[2]], it[3])
                return body
            block.tensor(run("pe"))
            block.scalar(run("act"))
            block.vector(run("dve"))
            block.gpsimd(run("pool"))
            block.sync(run("sp"))


class Arena:
    def __init__(self, ap_bf16, nbytes):
        self.ap = ap_bf16
        self.nbytes = nbytes
        self.off = 0

    def reset(self):
        self.off = 0

    def alloc(self, shape, dt):
        esz = 4 if dt in (F32, I32) else 2
        n = int(np.prod(shape[1:]))
        nb = n * esz
        self.off = (self.off + 63) // 64 * 64
        assert self.off + nb <= self.nbytes, ("arena overflow", self.off, nb, shape)
        a = self.ap[:, self.off // 2:(self.off + nb) // 2]
        self.off += nb
        if esz == 4:
            a = a.bitcast(dt)
        if len(shape) == 3:
            a = a.rearrange("p (a b) -> p a b", a=shape[1])
        elif len(shape) == 4:
            a = a.rearrange("p (a b c) -> p a b c", a=shape[1], b=shape[2])
        if shape[0] < 128:
            a = a[0:shape[0]]
        return a


def build_program(P, M, stop_after=None, dump_all=False):
    T = P + M
    A = 128 + M
    NB = T // 512
    nc = bass.Bass("TRN2", target_bir_lowering=False)

    def din(name, shape, dt=F32):
        return nc.dram_tensor(name, list(shape), dt, kind="ExternalInput").ap()

    def dscr(name, shape, dt):
        return nc.dram_tensor(name, list(shape), dt, kind="Internal").ap()

    xcat = din("xcat", [T, D])
    posa = din("posa", [1, A], I32)
    halo_mask = din("halo_mask", [128, 128])
    g_pre_mix = din("norm_pre_mix", [2, D])
    g_post_mix = din("norm_post_mix", [2, D])
    g_pre_ffn = din("norm_pre_ffn", [2, D])
    g_post_ffn = din("norm_post_ffn", [2, D])
    lam_re = din("s5_lam_re", [1, NG, 64])
    lam_im = din("s5_lam_im", [1, NG, 64])
    log_step = din("s5_log_step", [1, NG])
    b_re = din("s5_b_re", [1, NG, 64, 16])
    b_im = din("s5_b_im", [1, NG, 64, 16])
    c_re = din("s5_c_re", [1, NG, 16, 64])
    c_im = din("s5_c_im", [1, NG, 16, 64])
    s5_d = din("s5_d", [1, D])
    w_out1 = din("s5_w_out1", [1, D, D])
    b_out1 = din("s5_b_out1", [1, D])
    w_out2 = din("s5_w_out2", [1, D, D])
    b_out2 = din("s5_b_out2", [1, D])
    w_qkv = din("attn_w_qkv", [1, D, 5120])
    b_qkv = din("attn_b_qkv", [1, 5120])
    w_o = din("attn_w_o", [1, D, D])
    b_o = din("attn_b_o", [1, D])
    sinks = din("attn_sinks", [1, NQH])
    if stop_after is None or stop_after >= 5:
        w_gate = din("ffn_w_gate", [2, D, DFF])
        w_up = din("ffn_w_up", [2, D, DFF])
        w_down = din("ffn_w_down", [2, DFF, D])
    out = nc.dram_tensor("out", [M, D], F32, kind="ExternalOutput").ap()

    hnT_d = dscr("hnT_d", [KC, 128, T], BF16)
    actA_d = dscr("actA_d", [KC, 128, A], BF16)
    actB_d = dscr("actB_d", [KC, 128, A], BF16)
    mT_d = dscr("mT_d", [KC, 128, A], F32)
    h_d = dscr("h_d", [A, D], F32)
    qT_d = dscr("qT_d", [KC, 128, A], BF16)
    kT_d = dscr("kT_d", [NKV, 128, A], BF16)
    v_d = dscr("v_d", [A, 512], BF16)

    S = Sched(nc)
    dbg_out = {}

    with contextlib.ExitStack() as st:
        ARENA_BYTES = 198 * 1024
        arena_t = st.enter_context(nc.sbuf_tensor("arena", [128, ARENA_BYTES // 2], BF16))
        consts_t = st.enter_context(nc.sbuf_tensor("consts", [128, 768], F32))
        AR = Arena(arena_t[:], ARENA_BYTES)
        ident_f = consts_t[:, 0:128]
        ident_b = consts_t[:, 128:192].bitcast(BF16)
        halfpi = consts_t[:, 192:193]
        small = consts_t[:, 200:768]
        dly_t = st.enter_context(nc.sbuf_tensor("dly", [128, 1024], F32))
        S.delay_fn = lambda e: e.memset(dly_t[:], 0.0)
        banks = [st.enter_context(nc.psum_tensor("bank%d" % i, [128, 512], F32)) for i in range(8)]
        BK = [Buf("bank%d" % i) for i in range(8)]
        B_const = Buf("const")

        S.op("pool", lambda e: e.memset(consts_t[:, 0:192], 0.0), writes=[B_const])
        S.op("pool", lambda e: e.memset(halfpi, PI / 2), writes=[B_const])
        ones_tmp = small[:, 0:128]
        S.op("pool", lambda e: e.memset(ones_tmp, 1.0), writes=[B_const])
        S.op("pool", lambda e: e.affine_select(out=ident_f, in_=ones_tmp, pattern=[[1, 128]], compare_op=ALU.is_equal,
                                               fill=0.0, base=0, channel_multiplier=-1), reads=[B_const], writes=[B_const])
        S.op("dve", lambda e: e.tensor_copy(out=ident_b, in_=ident_f), reads=[B_const], writes=[B_const])
        S.barrier()

        def rstd_from_ss(ss, rstd, rd, wr):
            S.op("dve", lambda e: e.tensor_scalar(out=rstd, in0=ss, scalar1=1.0 / D, scalar2=EPS, op0=ALU.mult, op1=ALU.add),
                 reads=rd, writes=wr)
            S.op("act", lambda e: e.activation(out=rstd, in_=rstd, func=AF.Sqrt), reads=wr, writes=wr)
            S.op("dve", lambda e: e.reciprocal(out=rstd, in_=rstd), reads=wr, writes=wr)

        def load_cols(dst, vec, n, Bdst, tmp, Btmp, bank, Bbank):
            S.dma("sp", lambda e: e.dma_start(out=tmp[0:n, :], in_=vec.rearrange("(c p) -> c p", p=128)), writes=[Btmp])
            S.op("pe", lambda e: e.transpose(out=bank[:, 0:n], in_=tmp[0:n, :], identity=ident_f[0:n, 0:n]),
                 reads=[Btmp, B_const], writes=[Bbank])
            S.op("dve", lambda e: e.tensor_copy(out=dst, in_=bank[:, 0:n]), reads=[Bbank], writes=[Bdst])

        def norm_transpose(src, Bsrc, gain_bc, Bgain, dstT_d, Bdst, col0, bufs, idx):
            junk, Bjunk, hn, Bhn, hnT, BhnT, ssb, Bss = bufs
            ss = ssb[:, 0:1]
            rstd = ssb[:, 1:2]
            S.op("act", lambda e: e.activation(out=junk, in_=src, func=AF.Square), reads=[Bsrc], writes=[Bjunk])
            S.op("dve", lambda e: e.reduce_sum(out=ss, in_=junk, axis=AX.X), reads=[Bjunk], writes=[Bss])
            rstd_from_ss(ss, rstd, [Bss], [Bss])
            S.op("dve", lambda e: e.scalar_tensor_tensor(out=hn, in0=src, scalar=rstd, in1=gain_bc, op0=ALU.mult, op1=ALU.mult),
                 reads=[Bsrc, Bss, Bgain], writes=[Bhn])
            for q in range(4):
                bk = 4 + (q % 2) + 2 * (idx % 2)
                pb = banks[bk][:].bitcast(BF16)
                for i in range(8):
                    c = 8 * q + i
                    S.op("pe", lambda e, c=c, i=i, pb=pb: e.transpose(out=pb[:, 128 * i:128 * i + 128], in_=hn[:, 128 * c:128 * c + 128], identity=ident_b),
                         reads=[Bhn, B_const], writes=[BK[bk]], pe_acc=(i > 0))
                dsl = hnT[:, 8 * q:8 * q + 8, :]
                src_ps = pb.rearrange("p (a b) -> p a b", a=8)
                if q % 2 == 0:
                    S.op("act", lambda e, dsl=dsl, src_ps=src_ps: e.copy(out=dsl, in_=src_ps), reads=[BK[bk]], writes=[BhnT])
                else:
                    S.op("dve", lambda e, dsl=dsl, src_ps=src_ps: e.tensor_copy(out=dsl, in_=src_ps), reads=[BK[bk]], writes=[BhnT])
            S.dma("act", lambda e: e.dma_start(out=dstT_d.rearrange("c f t -> f c t")[:, :, col0:col0 + 128], in_=hnT),
                  reads=[BhnT], writes=[Bdst])

        def nt_bufs(tag):
            return (AR.alloc([128, D], BF16), Buf("junk" + tag), AR.alloc([128, D], BF16), Buf("hn" + tag),
                    AR.alloc([128, KC, 128], BF16), Buf("hnT" + tag), AR.alloc([128, 4], F32), Buf("ss" + tag))

        def load_gain(row):
            g = AR.alloc([128, D], F32)
            Bg = Buf("gain")
            S.dma("sp", lambda e: e.dma_start(out=g, in_=row.partition_broadcast(128)), writes=[Bg])
            return g, Bg

        B_hnT_d = Buf("hnT_d", multi=True)
        AR.reset()
        g0, Bg0 = load_gain(g_pre_mix[0])
        xs = [AR.alloc([128, D], F32) for _ in range(2)]
        Bxs = [Buf("xs%d" % i) for i in range(2)]
        ntb = [nt_bufs("a"), nt_bufs("b")]
        for j in range(T // 128):
            x_t, Bx = xs[j % 2], Bxs[j % 2]
            S.dma("sp", lambda e, x_t=x_t, j=j: e.dma_start(out=x_t, in_=xcat[128 * j:128 * j + 128, :]), writes=[Bx])
            norm_transpose(x_t, Bx, g0, Bg0, hnT_d, B_hnT_d, 128 * j, ntb[j % 2], j)
        S.barrier()
        if stop_after == 1:
            dbg_out["hnT_d"] = hnT_d

        B_actA = Buf("actA_d", multi=True)
        if stop_after is None or stop_after >= 2:
            AR.reset()
            lamre = AR.alloc([128, NG], F32)
            lamim = AR.alloc([128, NG], F32)
            stepb = AR.alloc([128, NG], F32)
            Rr = AR.alloc([128, NG], F32)
            THW = AR.alloc([128, NG], F32)
            THN = AR.alloc([128, NG], F32)
            BRE = AR.alloc([128, NG], F32)
            BIM = AR.alloc([128, NG], F32)
            w1 = AR.alloc([128, NG], F32)
            w2 = AR.alloc([128, NG], F32)
            w3 = AR.alloc([128, NG], F32)
            w4 = AR.alloc([128, NG], F32)
            dcol = AR.alloc([128, KC], F32)
            tmp32 = AR.alloc([128, 128], F32)
            Bp = Buf("s5prep")
            for h in range(2):
                S.dma("sp", lambda e, h=h: e.dma_start(out=lamre[64 * h:64 * h + 64, :], in_=lam_re[0].rearrange("g p -> p g"),
                                                       allow_slow_non_contiguous=True), writes=[Bp])
                S.dma("sp", lambda e, h=h: e.dma_start(out=lamim[64 * h:64 * h + 64, :], in_=lam_im[0].rearrange("g p -> p g"),
                                                       allow_slow_non_contiguous=True), writes=[Bp])
            S.dma("sp", lambda e: e.dma_start(out=stepb, in_=log_step[0].partition_broadcast(128)), writes=[Bp])
            Btmp32 = Buf("tmp32")
            load_cols(dcol, s5_d[0], KC, Bp, tmp32, Btmp32, banks[0], BK[0])
            S.barrier()
            V = lambda fn: S.op("dve", fn, reads=[Bp], writes=[Bp])
            Aop = lambda fn: S.op("act", fn, reads=[Bp], writes=[Bp])
            Aop(lambda e: e.activation(out=stepb, in_=stepb, func=AF.Exp))
            V(lambda e: e.tensor_tensor(out=w1, in0=lamre, in1=stepb, op=ALU.mult))
            V(lambda e: e.tensor_tensor(out=w2, in0=lamim, in1=stepb, op=ALU.mult))
            Aop(lambda e: e.activation(out=Rr, in_=w1, func=AF.Exp))
            V(lambda e: e.tensor_scalar(out=w3, in0=w2, scalar1=1.0 / TWO_PI, scalar2=MAGIC, op0=ALU.mult, op1=ALU.add))
            V(lambda e: e.tensor_scalar(out=w3, in0=w3, scalar1=-MAGIC, scalar2=-TWO_PI, op0=ALU.add, op1=ALU.mult))
            V(lambda e: e.tensor_tensor(out=THW, in0=w3, in1=w2, op=ALU.add))
            V(lambda e: e.tensor_scalar(out=THW, in0=THW, scalar1=-PI, scalar2=PI, op0=ALU.max, op1=ALU.min))
            V(lambda e: e.tensor_scalar(out=THN, in0=THW, scalar1=1.0 / TWO_PI, scalar2=None, op0=ALU.mult))
            Aop(lambda e: e.activation(out=w3, in_=THW, func=AF.Sin))
            Aop(lambda e: e.activation(out=w4, in_=THW, func=AF.Abs))
            Aop(lambda e: e.activation(out=w4, in_=w4, func=AF.Sin, scale=-1.0, bias=halfpi))
            V(lambda e: e.tensor_tensor(out=w3, in0=w3, in1=Rr, op=ALU.mult))
            V(lambda e: e.tensor_tensor(out=w4, in0=w4, in1=Rr, op=ALU.mult))
            V(lambda e: e.tensor_scalar(out=w4, in0=w4, scalar1=-1.0, scalar2=None, op0=ALU.add))
            V(lambda e: e.tensor_tensor(out=w1, in0=lamre, in1=lamre, op=ALU.mult))
            V(lambda e: e.tensor_tensor(out=w2, in0=lamim, in1=lamim, op=ALU.mult))
            V(lambda e: e.tensor_tensor(out=w1, in0=w1, in1=w2, op=ALU.add))
            V(lambda e: e.reciprocal(out=w1, in_=w1))
            V(lambda e: e.tensor_tensor(out=BRE, in0=w4, in1=lamre, op=ALU.mult))
            V(lambda e: e.tensor_tensor(out=w2, in0=w3, in1=lamim, op=ALU.mult))
            V(lambda e: e.tensor_tensor(out=BRE, in0=BRE, in1=w2, op=ALU.add))
            V(lambda e: e.tensor_tensor(out=BRE, in0=BRE, in1=w1, op=ALU.mult))
            V(lambda e: e.tensor_tensor(out=BIM, in0=w3, in1=lamre, op=ALU.mult))
            V(lambda e: e.tensor_tensor(out=w2, in0=w4, in1=lamim, op=ALU.mult))
            V(lambda e: e.tensor_tensor(out=BIM, in0=BIM, in1=w2, op=ALU.subtract))
            V(lambda e: e.tensor_tensor(out=BIM, in0=BIM, in1=w1, op=ALU.mult))
            rmask = AR.alloc([128, 8], F32)
            cmask = AR.alloc([128, 8, 128], F32)
            onesb = AR.alloc([128, 8, 128], F32)
            S.op("pool", lambda e: e.memset(onesb, 1.0), reads=[Bp], writes=[Bp])
            S.op("pool", lambda e: e.affine_select(out=rmask, in_=onesb[:, 0, 0:8], pattern=[[-16, 8]], compare_op=ALU.is_ge, fill=0.0,
                                                   base=0, channel_multiplier=1), reads=[Bp], writes=[Bp])
            S.op("pool", lambda e: e.affine_select(out=rmask, in_=rmask, pattern=[[16, 8]], compare_op=ALU.is_ge, fill=0.0,
                                                   base=15, channel_multiplier=-1), reads=[Bp], writes=[Bp])
            S.op("pool", lambda e: e.affine_select(out=cmask, in_=onesb, pattern=[[-16, 8], [1, 128]], compare_op=ALU.is_ge, fill=0.0,
                                                   base=0, channel_multiplier=0), reads=[Bp], writes=[Bp])
            S.op("pool", lambda e: e.affine_select(out=cmask, in_=cmask, pattern=[[16, 8], [-1, 128]], compare_op=ALU.is_ge, fill=0.0,
                                                   base=15, channel_multiplier=0), reads=[Bp], writes=[Bp])
            tposi = AR.alloc([128, T], I32)
            tposf = AR.alloc([128, T], F32)
            S.op("pool", lambda e: e.iota(tposi, pattern=[[1, T]], base=0, channel_multiplier=0), writes=[Bp])
            S.op("dve", lambda e: e.tensor_copy(out=tposf, in_=tposi), reads=[Bp], writes=[Bp])
            S.barrier()

            hk = [AR.alloc([128, T], BF16) for _ in range(2)]
            Bhk = [Buf("hk%d" % i) for i in range(2)]
            Bp_f = AR.alloc([64, 2, 8, 16], F32)
            BBp = Buf("Bp_f")
            Cn = AR.alloc([128, 2, 128], F32)
            BCn = Buf("Cn")
            BTcat = AR.alloc([128, 256], F32)
            BBT = Buf("BTcat")
            CT = AR.alloc([128, 2, 128], F32)
            E1 = AR.alloc([128, 128], F32)
            E2 = AR.alloc([128, 128], F32)
            cw1 = AR.alloc([128, 128], F32)
            cw2 = AR.alloc([128, 128], F32)
            BE = Buf("E")
            WB = AR.alloc([128, 8, 256], BF16)
            WC = AR.alloc([128, 8, 256], BF16)
            BW = Buf("W")
            carry = AR.alloc([128, 8], F32)
            Bcarry = Buf("carry")
            NR = 2
            tA = [AR.alloc([128, 512], F32) for _ in range(NR)]
            tB = [AR.alloc([128, 512], F32) for _ in range(NR)]
            CSt = [AR.alloc([128, 512], F32) for _ in range(NR)]
            SNt = [AR.alloc([128, 512], F32) for _ in range(NR)]
            X2s = [AR.alloc([128, 512], F32) for _ in range(NR)]
            M1 = [AR.alloc([128, 512], F32) for _ in range(NR)]
            sT = [AR.alloc([128, 512], F32) for _ in range(NR)]
            Z1 = [AR.alloc([128, 512], BF16) for _ in range(NR)]
            Z2 = [AR.alloc([128, 512], BF16) for _ in range(NR)]
            BtA = [Buf("tA%d" % i) for i in range(NR)]
            BtB = [Buf("tB%d" % i) for i in range(NR)]
            BCS = [Buf("CS%d" % i) for i in range(NR)]
            BSN = [Buf("SN%d" % i) for i in range(NR)]
            BX2 = [Buf("X2s%d" % i) for i in range(NR)]
            BM1 = [Buf("M1%d" % i) for i in range(NR)]
            BsT = [Buf("sT%d" % i) for i in range(NR)]
            BZ1 = [Buf("Z1%d" % i) for i in range(NR)]
            BZ2 = [Buf("Z2%d" % i) for i in range(NR)]
            yb = [AR.alloc([128, 512], F32) for _ in range(2)]
            Byb = [Buf("yb%d" % i) for i in range(2)]
            gb = [AR.alloc([128, 512], BF16) for _ in range(2)]
            Bgb = [Buf("gb%d" % i) for i in range(2)]
            it = 0
            for k in range(KC):
                hkt, Bh = hk[k % 2], Bhk[k % 2]
                S.dma("sp", lambda e, hkt=hkt, k=k: e.dma_start(out=hkt, in_=hnT_d[k]), reads=[B_hnT_d], writes=[Bh])
                S.dma("sp", lambda e, k=k: e.dma_start(out=Bp_f[:, 0], in_=b_re[0, 8 * k:8 * k + 8].rearrange("g p c -> p g c")), writes=[BBp])
                S.dma("sp", lambda e, k=k: e.dma_start(out=Bp_f[:, 1], in_=b_im[0, 8 * k:8 * k + 8].rearrange("g p c -> p g c")), writes=[BBp])
                for h in range(2):
                    S.dma("sp", lambda e, k=k, h=h: e.dma_start(out=Cn[:, 0, 64 * h:64 * h + 64], in_=c_re[0, 8 * k:8 * k + 8].rearrange("g c p -> (g c) p")), writes=[BCn])
                    S.dma("sp", lambda e, k=k, h=h: e.dma_start(out=Cn[:, 1, 64 * h:64 * h + 64], in_=c_im[0, 8 * k:8 * k + 8].rearrange("g c p -> (g c) p")), writes=[BCn])
                S.op("pe", lambda e: e.transpose(out=banks[0][:, 0:64], in_=Bp_f[:, 0].rearrange("p g c -> p (g c)"), identity=ident_f[0:64, 0:64]),
                     reads=[BBp, B_const], writes=[BK[0]])
                S.op("pe", lambda e: e.transpose(out=banks[0][:, 64:128], in_=Bp_f[:, 1].rearrange("p g c -> p (g c)"), identity=ident_f[0:64, 0:64]),
                     reads=[BBp, B_const], writes=[BK[0]], pe_acc=True)
                S.op("dve", lambda e: e.tensor_copy(out=BTcat[:, 0:128], in_=banks[0][:, 0:128]), reads=[BK[0]], writes=[BBT])
                S.op("dve", lambda e: e.tensor_copy(out=BTcat[:, 128:192], in_=banks[0][:, 64:128]), reads=[BK[0]], writes=[BBT])
                S.op("dve", lambda e: e.tensor_scalar(out=BTcat[:, 192:256], in0=banks[0][:, 0:64], scalar1=-1.0, scalar2=None, op0=ALU.mult),
                     reads=[BK[0]], writes=[BBT])
                S.op("pe", lambda e: e.transpose(out=banks[1][:, 0:128], in_=Cn[:, 0], identity=ident_f), reads=[BCn, B_const], writes=[BK[1]])
                S.op("pe", lambda e: e.transpose(out=banks[1][:, 128:256], in_=Cn[:, 1], identity=ident_f), reads=[BCn, B_const], writes=[BK[1]], pe_acc=True)
                S.op("dve", lambda e: e.tensor_copy(out=CT.rearrange("p a b -> p (a b)"), in_=banks[1][:, 0:256]), reads=[BK[1]], writes=[BE])
                bre_bc = BRE[:, 8 * k:8 * k + 8].unsqueeze(2).to_broadcast([128, 8, 16])
                bim_bc = BIM[:, 8 * k:8 * k + 8].unsqueeze(2).to_broadcast([128, 8, 16])
                r3 = lambda a: a.rearrange("p (g c) -> p g c", g=8)
                VE = lambda fn: S.op("dve", fn, reads=[BE, Bp], writes=[BE])
                VE(lambda e, bre_bc=bre_bc: e.tensor_tensor(out=r3(cw1), in0=r3(CT[:, 0]), in1=bre_bc, op=ALU.mult))
                VE(lambda e, bim_bc=bim_bc: e.tensor_tensor(out=r3(cw2), in0=r3(CT[:, 1]), in1=bim_bc, op=ALU.mult))
                VE(lambda e: e.tensor_tensor(out=cw1, in0=cw1, in1=cw2, op=ALU.subtract))
                VE(lambda e, bim_bc=bim_bc: e.tensor_tensor(out=r3(cw2), in0=r3(CT[:, 0]), in1=bim_bc, op=ALU.mult))
                VE(lambda e, bre_bc=bre_bc: e.tensor_tensor(out=r3(E2), in0=r3(CT[:, 1]), in1=bre_bc, op=ALU.mult))
                VE(lambda e: e.tensor_tensor(out=cw2, in0=cw2, in1=E2, op=ALU.add))
                VE(lambda e: e.tensor_copy(out=E1[0:64], in_=cw1[0:64]))
                VE(lambda e: e.tensor_scalar(out=E1[64:128], in0=cw2[64:128], scalar1=-1.0, scalar2=None, op0=ALU.mult))
                VE(lambda e: e.tensor_scalar(out=E2[0:64], in0=cw2[0:64], scalar1=-1.0, scalar2=None, op0=ALU.mult))
                VE(lambda e: e.tensor_scalar(out=E2[64:128], in0=cw1[64:128], scalar1=-1.0, scalar2=None, op0=ALU.mult))
                for gl in range(8):
                    S.op("dve", lambda e, gl=gl: e.tensor_scalar(out=WB[:, gl, :], in0=BTcat, scalar1=rmask[:, gl:gl + 1], scalar2=None, op0=ALU.mult),
                         reads=[BBT, Bp], writes=[BW])
                    S.op("dve", lambda e, gl=gl: e.tensor_tensor(out=WC[:, gl, 0:128], in0=E1, in1=cmask[:, gl, :], op=ALU.mult), reads=[BE, Bp], writes=[BW])
                    S.op("dve", lambda e, gl=gl: e.tensor_tensor(out=WC[:, gl, 128:256], in0=E2, in1=cmask[:, gl, :], op=ALU.mult), reads=[BE, Bp], writes=[BW])
                S.op("pool", lambda e: e.memset(carry, 0.0), writes=[Bcarry])
                for b in range(NB):
                    c0 = 512 * b
                    need_lo = max(c0, P - 128)
                    need = need_lo < c0 + 512
                    ybk = 2 + (b % 2)
                    for gl in range(8):
                        g = 8 * k + gl
                        r = it % NR
                        it += 1
                        xb = 4 + 2 * (it % 2)
                        psX1, psX2 = banks[xb], banks[xb + 1]
                        S.op("pe", lambda e, psX1=psX1, gl=gl, hkt=hkt, c0=c0: e.matmul(psX1[:, :], lhsT=WB[:, gl, 0:128], rhs=hkt[:, c0:c0 + 512], start=True, stop=True),
                             reads=[BW, Bh], writes=[BK[xb]])
                        S.op("pe", lambda e, psX2=psX2, gl=gl, hkt=hkt, c0=c0: e.matmul(psX2[:, :], lhsT=WB[:, gl, 128:256], rhs=hkt[:, c0:c0 + 512], start=True, stop=True),
                             reads=[BW, Bh], writes=[BK[xb + 1]])
                        tp = tposf[:, c0:c0 + 512]
                        S.op("pool", lambda e, r=r, g=g, tp=tp: e.tensor_scalar(out=tA[r], in0=tp, scalar1=THN[:, g:g + 1], scalar2=MAGIC, op0=ALU.mult, op1=ALU.add),
                             reads=[Bp], writes=[BtA[r]])
                        S.op("pool", lambda e, r=r: e.tensor_scalar(out=tA[r], in0=tA[r], scalar1=-MAGIC, scalar2=-TWO_PI, op0=ALU.add, op1=ALU.mult),
                             reads=[BtA[r]], writes=[BtA[r]])
                        S.op("dve", lambda e, r=r, g=g, tp=tp: e.scalar_tensor_tensor(out=tA[r], in0=tp, scalar=THW[:, g:g + 1], in1=tA[r], op0=ALU.mult, op1=ALU.add),
                             reads=[BtA[r], Bp], writes=[BtA[r]])
                        S.op("pool", lambda e, r=r: e.tensor_scalar(out=tA[r], in0=tA[r], scalar1=-PI, scalar2=PI, op0=ALU.max, op1=ALU.min),
                             reads=[BtA[r]], writes=[BtA[r]])
                        S.op("act", lambda e, r=r: e.activation(out=tB[r], in_=tA[r], func=AF.Abs), reads=[BtA[r]], writes=[BtB[r]])
                        S.op("act", lambda e, r=r: e.activation(out=SNt[r], in_=tA[r], func=AF.Sin), reads=[BtA[r]], writes=[BSN[r]])
                        S.op("act", lambda e, r=r: e.activation(out=CSt[r], in_=tB[r], func=AF.Sin, scale=-1.0, bias=halfpi), reads=[BtB[r], B_const], writes=[BCS[r]])
                        S.op("act", lambda e, r=r, psX2=psX2: e.copy(out=X2s[r], in_=psX2[:, :]), reads=[BK[xb + 1]], writes=[BX2[r]])
                        S.op("dve", lambda e, r=r, psX1=psX1: e.tensor_tensor(out=M1[r], in0=psX1[:, :], in1=CSt[r], op=ALU.mult), reads=[BK[xb], BCS[r]], writes=[BM1[r]])
                        S.op("pool", lambda e, r=r: e.tensor_tensor(out=X2s[r], in0=X2s[r], in1=SNt[r], op=ALU.mult), reads=[BX2[r], BSN[r]], writes=[BX2[r]])
                        S.op("pool", lambda e, r=r: e.tensor_tensor(out=M1[r], in0=M1[r], in1=X2s[r], op=ALU.add), reads=[BM1[r], BX2[r]], writes=[BM1[r]])
                        S.op("dve", lambda e, r=r, g=g, gl=gl: e.tensor_tensor_scan(out=sT[r], data0=Rr[:, g:g + 1].to_broadcast([128, 512]), data1=M1[r],
                                                                                   initial=carry[:, gl:gl + 1], op0=ALU.mult, op1=ALU.add),
                             reads=[BM1[r], Bp, Bcarry], writes=[BsT[r]])
                        S.op("act", lambda e, r=r, gl=gl: e.copy(out=carry[:, gl:gl + 1], in_=sT[r][:, 511:512]), reads=[BsT[r]], writes=[Bcarry])
                        if need:
                            S.op("pool", lambda e, r=r: e.tensor_tensor(out=Z1[r], in0=sT[r], in1=CSt[r], op=ALU.mult), reads=[BsT[r], BCS[r]], writes=[BZ1[r]])
                            S.op("dve", lambda e, r=r: e.tensor_tensor(out=Z2[r], in0=sT[r], in1=SNt[r], op=ALU.mult), reads=[BsT[r], BSN[r]], writes=[BZ2[r]])
                            S.op("pe", lambda e, r=r, gl=gl, ybk=ybk: e.matmul(banks[ybk][:, :], lhsT=WC[:, gl, 0:128], rhs=Z1[r], start=(gl == 0), stop=False),
                                 reads=[BW, BZ1[r]], writes=[BK[ybk]], pe_acc=(gl > 0))
                            S.op("pe", lambda e, r=r, gl=gl, ybk=ybk: e.matmul(banks[ybk][:, :], lhsT=WC[:, gl, 128:256], rhs=Z2[r], start=False, stop=(gl == 7)),
                                 reads=[BW, BZ2[r]], writes=[BK[ybk]], pe_acc=True)
                    if need:
                        lo = need_lo - c0
                        n = 512 - lo
                        yt, By = yb[b % 2], Byb[b % 2]
                        gt, Bg = gb[b % 2], Bgb[b % 2]
                        S.op("dve", lambda e, yt=yt, k=k, hkt=hkt, c0=c0, lo=lo, n=n, ybk=ybk: e.scalar_tensor_tensor(
                            out=yt[:, 0:n], in0=hkt[:, c0 + lo:c0 + 512], scalar=dcol[:, k:k + 1], in1=banks[ybk][:, lo:512], op0=ALU.mult, op1=ALU.add),
                            reads=[Bh, BK[ybk], Bp], writes=[By])
                        t2 = tA[0]
                        S.op("dve", lambda e, yt=yt, n=n, t2=t2: e.tensor_tensor(out=t2[:, 0:n], in0=yt[:, 0:n], in1=yt[:, 0:n], op=ALU.mult), reads=[By], writes=[BtA[0]])
                        S.op("dve", lambda e, n=n, t2=t2: e.tensor_scalar(out=t2[:, 0:n], in0=t2[:, 0:n], scalar1=0.044715, scalar2=1.0, op0=ALU.mult, op1=ALU.add),
                             reads=[BtA[0]], writes=[BtA[0]])
                        S.op("dve", lambda e, yt=yt, n=n, t2=t2: e.tensor_tensor(out=t2[:, 0:n], in0=t2[:, 0:n], in1=yt[:, 0:n], op=ALU.mult), reads=[BtA[0], By], writes=[BtA[0]])
                        S.op("act", lambda e, n=n, t2=t2: e.activation(out=t2[:, 0:n], in_=t2[:, 0:n], func=AF.Sigmoid, scale=1.5957691216057308), reads=[BtA[0]], writes=[BtA[0]])
                        S.op("dve", lambda e, yt=yt, gt=gt, n=n, t2=t2: e.tensor_tensor(out=gt[:, 0:n], in0=t2[:, 0:n], in1=yt[:, 0:n], op=ALU.mult), reads=[BtA[0], By], writes=[Bg])
                        a0 = need_lo - (P - 128)
                        S.dma("act", lambda e, gt=gt, k=k, a0=a0, n=n: e.dma_start(out=actA_d[k, :, a0:a0 + n], in_=gt[:, 0:n]), reads=[Bg], writes=[B_actA])
            S.barrier()
        if stop_after == 2:
            dbg_out["actA_d"] = actA_d

        def gemm_b(act_d, Bact, KCn, tok_blocks, steps, nw, epilogue, slab_bufs=2):
            AR_mark = AR.off
            assert len(steps) % 2 == 0
            maxnt = max(nt for _, nt in tok_blocks)
            act = AR.alloc([128, KCn, maxnt], BF16)
            Bact_sb = Buf("act_sb", multi=True)
            slabs = [[AR.alloc([128, KCn, 256], BF16) for _ in range(nw)] for _ in range(slab_bufs)]
            Bsl = [[Buf("slab%d_%d" % (i, w), multi=True) for w in range(nw)] for i in range(slab_bufs)]
            pairs = []
            for p in range(len(steps) // 2):
                per_w = []
                for w in range(nw):
                    pcs = [(W2d, c0, n, d) for (W2d, c0, n, d) in steps[2 * p][w]] + [(W2d, c0, n, d + 128) for (W2d, c0, n, d) in steps[2 * p + 1][w]]
                    merged = []
                    for pc in pcs:
                        if merged and merged[-1][0] is pc[0] and merged[-1][1] + merged[-1][2] == pc[1] and merged[-1][3] + merged[-1][2] == pc[3]:
                            m = merged[-1]
                            merged[-1] = (m[0], m[1], m[2] + pc[2], m[3])
                        else:
                            merged.append(pc)
                    per_w.append(merged)
                pairs.append(per_w)
            ctr = 0
            for (t0, nt) in tok_blocks:
                step_kc = 8
                for kc0 in range(0, KCn, step_kc):
                    kc1 = min(KCn, kc0 + step_kc)
                    S.dma("sp", lambda e, kc0=kc0, kc1=kc1, t0=t0, nt=nt: e.dma_start(
                        out=act[:, kc0:kc1, 0:nt], in_=act_d.rearrange("c f t -> f c t")[:, kc0:kc1, t0:t0 + nt]),
                        reads=[Bact], writes=[Bact_sb])
                for p, per_w in enumerate(pairs):
                    sb_i = ctr % slab_bufs
                    pbase = (ctr % 2) * 2 * nw
                    ctr += 1
                    for w in range(nw):
                        for (W2d, c0, ncols, doff) in per_w[w]:
                            S.dma("pool", lambda e, w=w, sb_i=sb_i, W2d=W2d, c0=c0, ncols=ncols, doff=doff: e.dma_start(
                                out=slabs[sb_i][w][:, :, doff:doff + ncols], in_=W2d[:, c0:c0 + ncols].rearrange("(kc p) n -> p kc n", p=128)),
                                writes=[Bsl[sb_i][w]])
                    for half in range(2):
                        for w in range(nw):
                            bk = pbase + 2 * w + half
                            for kc in range(KCn):
                                S.op("pe", lambda e, w=w, kc=kc, sb_i=sb_i, bk=bk, half=half, nt=nt: e.matmul(
                                    banks[bk][:, 0:nt], lhsT=slabs[sb_i][w][:, kc, 128 * half:128 * half + 128], rhs=act[:, kc, 0:nt], start=(kc == 0), stop=(kc == KCn - 1)),
                                    reads=[Bsl[sb_i][w], Bact_sb], writes=[BK[bk]], pe_acc=(kc > 0))
                        epilogue(2 * p + half, [banks[pbase + 2 * w + half][:, 0:nt] for w in range(nw)], [BK[pbase + 2 * w + half] for w in range(nw)], t0, nt)
            return AR_mark

        def tokblocks(lo, hi):
            out_ = []
            t = lo
            while t < hi:
                n = min(512, hi - t)
                out_.append((t, n))
                t += n
            return out_

        B_mT = Buf("mT_d", multi=True)
        if stop_after is None or stop_after >= 3:
            AR.reset()
            b1c = AR.alloc([128, KC], F32)
            b2c = AR.alloc([128, KC], F32)
            tmp32 = AR.alloc([128, 128], F32)
            Bb = Buf("bias")
            Bt32 = Buf("t32")
            load_cols(b1c, b_out1[0], KC, Bb, tmp32, Bt32, banks[7], BK[7])
            load_cols(b2c, b_out2[0], KC, Bb, tmp32, Bt32, banks[7], BK[7])
            S.barrier()
            sg = [AR.alloc([128, 512], F32) for _ in range(2)]
            Bsg = [Buf("sg%d" % i) for i in range(2)]
            mo = [AR.alloc([128, 512], F32) for _ in range(2)]
            Bmo = [Buf("mo%d" % i) for i in range(2)]
            cnt3 = [0]

            def epi3(si, ps, Bps, t0, nt):
                i = cnt3[0] % 2
                cnt3[0] += 1
                S.op("act", lambda e: e.activation(out=sg[i][:, 0:nt], in_=ps[1], func=AF.Sigmoid, bias=b2c[:, si:si + 1], scale=1.0),
                     reads=[Bps[1], Bb], writes=[Bsg[i]])
                S.op("dve", lambda e: e.scalar_tensor_tensor(out=mo[i][:, 0:nt], in0=ps[0], scalar=b1c[:, si:si + 1], in1=sg[i][:, 0:nt], op0=ALU.add, op1=ALU.mult),
                     reads=[Bps[0], Bsg[i], Bb], writes=[Bmo[i]])
                S.dma("act", lambda e: e.dma_start(out=mT_d[si, :, t0:t0 + nt], in_=mo[i][:, 0:nt]), reads=[Bmo[i]], writes=[B_mT])
            steps = [[[(w_out1[0], 128 * n, 128, 0)], [(w_out2[0], 128 * n, 128, 0)]] for n in range(KC)]
            gemm_b(actA_d, B_actA, KC, tokblocks(0, A), steps, 2, epi3)
            S.barrier()
        if stop_after == 3:
            dbg_out["mT_d"] = mT_d

        def epilogue_phase(h_src, h_dst, Bh_dst, g_post_row, g_next_row, dstT_d, Bdst, tok_lo, tok_hi, h_src_is_x=False, final_out=None):
            AR.reset()
            gp, Bgp = load_gain(g_post_row)
            if g_next_row is not None:
                gn, Bgn = load_gain(g_next_row)
                ntb_ = [nt_bufs("e0"), nt_bufs("e1")]
            mTt = [AR.alloc([128, KC, 128], F32) for _ in range(2)]
            BmTt = [Buf("mTt%d" % i) for i in range(2)]
            mtok = AR.alloc([128, D], F32)
            Bmtok = Buf("mtok")
            hres = [AR.alloc([128, D], F32) for _ in range(2)]
            Bhres = [Buf("hres%d" % i) for i in range(2)]
            ssb = AR.alloc([128, 4], F32)
            Bssb = Buf("ssb")
            junk = AR.alloc([128, D], BF16)
            Bjunk = Buf("junk")
            for j, a0 in enumerate(range(tok_lo, tok_hi, 128)):
                i = j % 2
                S.dma("sp", lambda e, i=i, a0=a0: e.dma_start(out=mTt[i], in_=mT_d.rearrange("c f t -> f c t")[:, :, a0:a0 + 128]), reads=[B_mT], writes=[BmTt[i]])
                S.dma("sp", lambda e, i=i, a0=a0: e.dma_start(out=hres[i], in_=h_src[a0:a0 + 128, :]), reads=([] if h_src_is_x else [Bh_dst]), writes=[Bhres[i]])
                for q in range(8):
                    bk = q % 4
                    for c4 in range(4):
                        c = 4 * q + c4
                        S.op("pe", lambda e, i=i, c=c, c4=c4, bk=bk: e.transpose(out=banks[bk][:, 128 * c4:128 * c4 + 128], in_=mTt[i][:, c, :], identity=ident_f),
                             reads=[BmTt[i], B_const], writes=[BK[bk]], pe_acc=(c4 > 0))
                    if q % 2 == 0:
                        S.op("act", lambda e, q=q, bk=bk: e.copy(out=mtok[:, 512 * q:512 * q + 512], in_=banks[bk][:, :]), reads=[BK[bk]], writes=[Bmtok])
                    else:
                        S.op("dve", lambda e, q=q, bk=bk: e.tensor_copy(out=mtok[:, 512 * q:512 * q + 512], in_=banks[bk][:, :]), reads=[BK[bk]], writes=[Bmtok])
                ss, rstd = ssb[:, 0:1], ssb[:, 1:2]
                S.op("act", lambda e: e.activation(out=junk, in_=mtok, func=AF.Square), reads=[Bmtok], writes=[Bjunk])
                S.op("dve", lambda e: e.reduce_sum(out=ss, in_=junk, axis=AX.X), reads=[Bjunk], writes=[Bssb])
                rstd_from_ss(ss, rstd, [Bssb], [Bssb])
                S.op("dve", lambda e: e.scalar_tensor_tensor(out=mtok, in0=mtok, scalar=rstd, in1=gp, op0=ALU.mult, op1=ALU.mult),
                     reads=[Bmtok, Bssb, Bgp], writes=[Bmtok])
                S.op("dve", lambda e, i=i: e.tensor_tensor(out=hres[i], in0=hres[i], in1=mtok, op=ALU.add), reads=[Bmtok, Bhres[i]], writes=[Bhres[i]])
                if final_out is not None:
                    S.dma("act", lambda e, i=i, a0=a0: e.dma_start(out=final_out[a0 - tok_lo:a0 - tok_lo + 128, :], in_=hres[i]), reads=[Bhres[i]], writes=[Bh_dst])
                else:
                    S.dma("act", lambda e, i=i, a0=a0: e.dma_start(out=h_dst[a0:a0 + 128, :], in_=hres[i]), reads=[Bhres[i]], writes=[Bh_dst])
                if g_next_row is not None:
                    norm_transpose(hres[i], Bhres[i], gn, Bgn, dstT_d, Bdst, a0, ntb_[i], j)
            S.barrier()

        B_h = Buf("h_d", multi=True)
        B_actB = Buf("actB_d", multi=True)
        if stop_after is None or stop_after >= 4:
            epilogue_phase(xcat[P - 128:T, :], h_d, B_h, g_post_mix[0], g_pre_ffn[0], actB_d, B_actB, 0, A, h_src_is_x=True)
        if stop_after == 4:
            dbg_out["h_d"] = h_d
            dbg_out["actB_d"] = actB_d

        def ffn_phase(layer, actin_d, Bactin, tok_lo, tok_hi):
            wg, wu, wd = w_gate[layer], w_up[layer], w_down[layer]
            for (t0, nt) in tokblocks(tok_lo, tok_hi):
                AR.reset()
                aT = AR.alloc([128, HC, 512], BF16)
                BaT = Buf("aT")
                sg = [AR.alloc([128, 512], F32) for _ in range(2)]
                Bsg = [Buf("fsg%d" % i) for i in range(2)]
                cnt = [0]

                def epi_gu(si, ps, Bps, t0_, nt_):
                    i = cnt[0] % 2
                    cnt[0] += 1
                    S.op("act", lambda e: e.activation(out=sg[i][:, 0:nt_], in_=ps[0], func=AF.Silu), reads=[Bps[0]], writes=[Bsg[i]])
                    S.op("dve", lambda e: e.tensor_tensor(out=aT[:, si, 0:nt_], in0=ps[1], in1=sg[i][:, 0:nt_], op=ALU.mult), reads=[Bps[1], Bsg[i]], writes=[BaT])
                steps = [[[(wg, 128 * n, 128, 0)], [(wu, 128 * n, 128, 0)]] for n in range(HC)]
                mark = gemm_b(actin_d, Bactin, KC, [(t0, nt)], steps, 2, epi_gu)
                S.barrier()
                AR.off = mark
                fo = [AR.alloc([128, 512], F32) for _ in range(2)]
                Bfo = [Buf("fo%d" % i) for i in range(2)]
                slabs = [AR.alloc([128, HC, 256], BF16) for _ in range(2)]
                Bsl = [Buf("dsl%d" % i, multi=True) for i in range(2)]
                for n2 in range(KC // 2):
                    i = n2 % 2
                    for h0 in range(0, HC, 43):
                        S.dma("pool", lambda e, i=i, n2=n2, h0=h0: e.dma_start(
                            out=slabs[i][:, h0:h0 + 43, :], in_=wd[128 * h0:128 * (h0 + 43), 256 * n2:256 * n2 + 256].rearrange("(kc p) n -> p kc n", p=128)),
                            writes=[Bsl[i]])
                    for half in range(2):
                        n = 2 * n2 + half
                        bk = 2 * i + half
                        fi = n % 2
                        for kc in range(HC):
                            S.op("pe", lambda e, i=i, kc=kc, nt=nt, bk=bk, half=half: e.matmul(banks[bk][:, 0:nt], lhsT=slabs[i][:, kc, 128 * half:128 * half + 128], rhs=aT[:, kc, 0:nt], start=(kc == 0), stop=(kc == HC - 1)),
                                 reads=[Bsl[i], BaT], writes=[BK[bk]], pe_acc=(kc > 0))
                        if fi == 0:
                            S.op("act", lambda e, fi=fi, nt=nt, bk=bk: e.copy(out=fo[fi][:, 0:nt], in_=banks[bk][:, 0:nt]), reads=[BK[bk]], writes=[Bfo[fi]])
                        else:
                            S.op("dve", lambda e, fi=fi, nt=nt, bk=bk: e.tensor_copy(out=fo[fi][:, 0:nt], in_=banks[bk][:, 0:nt]), reads=[BK[bk]], writes=[Bfo[fi]])
                        S.dma("act", lambda e, fi=fi, n=n, t0=t0, nt=nt: e.dma_start(out=mT_d[n, :, t0:t0 + nt], in_=fo[fi][:, 0:nt]), reads=[Bfo[fi]], writes=[B_mT])
                S.barrier()

        if stop_after is None or stop_after >= 5:
            ffn_phase(0, actB_d, B_actB, 0, A)
        if stop_after == 5:
            dbg_out["mT_d"] = mT_d
        if stop_after is None or stop_after >= 6:
            epilogue_phase(h_d, h_d, B_h, g_post_ffn[0], g_pre_mix[1], actA_d, B_actA, 0, A)
        if stop_after == 6:
            dbg_out["h_d"] = h_d
            dbg_out["actA_d"] = actA_d

        B_qT = Buf("qT_d", multi=True)
        B_kT = Buf("kT_d", multi=True)
        B_v = Buf("v_d", multi=True)
        if stop_after is None or stop_after >= 7:
            AR.reset()
            wq = w_qkv[0]
            COS = AR.alloc([128, A], F32)
            SINS = AR.alloc([128, A], F32)
            bq = AR.alloc([128, 40], F32)
            bqs = AR.alloc([128, 40], F32)
            bkd = AR.alloc([128, 8], F32)
            bkds = AR.alloc([128, 8], F32)
            invf = AR.alloc([128, 2], F32)
            sgn = AR.alloc([128, 2], F32)
            pi_i = AR.alloc([128, 2], I32)
            posi = AR.alloc([128, A], I32)
            wk_ = AR.alloc([128, A], F32)
            wk2 = AR.alloc([128, A], F32)
            tmp32 = AR.alloc([128, 128], F32)
            Bt32 = Buf("t32")
            Bat = Buf("attnprep")
            load_cols(bq, b_qkv[0], 40, Bat, tmp32, Bt32, banks[7], BK[7])
            S.barrier()
            bsw = b_qkv[0, 0:5120].rearrange("(c h t i) -> c h t i", h=2, t=2, i=32)
            for hh in range(2):
                for tt in range(2):
                    S.dma("sp", lambda e, hh=hh, tt=tt: e.dma_start(out=tmp32[0:40, 64 * hh + 32 * tt:64 * hh + 32 * tt + 32], in_=bsw[:, hh, 1 - tt, :]), writes=[Bt32])
            S.op("pe", lambda e: e.transpose(out=banks[7][:, 0:40], in_=tmp32[0:40, :], identity=ident_f[0:40, 0:40]), reads=[Bt32, B_const], writes=[BK[7]])
            S.op("dve", lambda e: e.tensor_copy(out=bqs, in_=banks[7][:, 0:40]), reads=[BK[7]], writes=[Bat])
            S.barrier()
            bkv = b_qkv[0, 4096:4608].rearrange("(j d) -> j d", d=64)
            bkvs = b_qkv[0, 4096:4608].rearrange("(j t i) -> j t i", t=2, i=32)
            for hh in range(2):
                S.dma("sp", lambda e, hh=hh: e.dma_start(out=tmp32[0:8, 64 * hh:64 * hh + 64], in_=bkv), writes=[Bt32])
            S.op("pe", lambda e: e.transpose(out=banks[7][:, 0:8], in_=tmp32[0:8, :], identity=ident_f[0:8, 0:8]), reads=[Bt32, B_const], writes=[BK[7]])
            S.op("dve", lambda e: e.tensor_copy(out=bkd, in_=banks[7][:, 0:8]), reads=[BK[7]], writes=[Bat])
            S.barrier()
            for hh in range(2):
                for tt in range(2):
                    S.dma("sp", lambda e, hh=hh, tt=tt: e.dma_start(out=tmp32[0:8, 64 * hh + 32 * tt:64 * hh + 32 * tt + 32], in_=bkvs[:, 1 - tt, :]), writes=[Bt32])
            S.op("pe", lambda e: e.transpose(out=banks[7][:, 0:8], in_=tmp32[0:8, :], identity=ident_f[0:8, 0:8]), reads=[Bt32, B_const], writes=[BK[7]])
            S.op("dve", lambda e: e.tensor_copy(out=bkds, in_=banks[7][:, 0:8]), reads=[BK[7]], writes=[Bat])
            S.op("pool", lambda e: e.iota(pi_i, pattern=[[0, 2]], base=0, channel_multiplier=1), writes=[Bat])
            S.op("dve", lambda e: e.tensor_single_scalar(out=pi_i[:, 1:2], in_=pi_i[:, 0:1], scalar=31, op=ALU.bitwise_and), reads=[Bat], writes=[Bat])
            S.op("dve", lambda e: e.tensor_copy(out=invf[:, 0:1], in_=pi_i[:, 1:2]), reads=[Bat], writes=[Bat])
            S.op("act", lambda e: e.activation(out=invf[:, 0:1], in_=invf[:, 0:1], func=AF.Exp, scale=float(-np.log(10000.0) / 32.0)), reads=[Bat], writes=[Bat])
            S.op("dve", lambda e: e.tensor_scalar(out=invf[:, 1:2], in0=invf[:, 0:1], scalar1=1.0 / TWO_PI, scalar2=None, op0=ALU.mult), reads=[Bat], writes=[Bat])
            S.op("dve", lambda e: e.tensor_single_scalar(out=pi_i[:, 1:2], in_=pi_i[:, 0:1], scalar=32, op=ALU.bitwise_and), reads=[Bat], writes=[Bat])
            S.op("dve", lambda e: e.tensor_copy(out=sgn[:, 0:1], in_=pi_i[:, 1:2]), reads=[Bat], writes=[Bat])
            S.op("dve", lambda e: e.tensor_scalar(out=sgn[:, 0:1], in0=sgn[:, 0:1], scalar1=1.0 / 16.0, scalar2=-1.0, op0=ALU.mult, op1=ALU.add), reads=[Bat], writes=[Bat])
            S.dma("sp", lambda e: e.dma_start(out=posi, in_=posa[0].partition_broadcast(128)), writes=[Bat])
            S.barrier()
            Vq = lambda fn: S.op("dve", fn, reads=[Bat], writes=[Bat])
            Vq(lambda e: e.tensor_copy(out=wk_, in_=posi))
            Vq(lambda e: e.tensor_scalar(out=wk2, in0=wk_, scalar1=invf[:, 1:2], scalar2=MAGIC, op0=ALU.mult, op1=ALU.add))
            Vq(lambda e: e.tensor_scalar(out=wk2, in0=wk2, scalar1=-MAGIC, scalar2=-TWO_PI, op0=ALU.add, op1=ALU.mult))
            Vq(lambda e: e.scalar_tensor_tensor(out=wk2, in0=wk_, scalar=invf[:, 0:1], in1=wk2, op0=ALU.mult, op1=ALU.add))
            Vq(lambda e: e.tensor_scalar(out=wk2, in0=wk2, scalar1=-PI, scalar2=PI, op0=ALU.max, op1=ALU.min))
            S.op("act", lambda e: e.activation(out=SINS, in_=wk2, func=AF.Sin), reads=[Bat], writes=[Bat])
            Vq(lambda e: e.tensor_scalar(out=SINS, in0=SINS, scalar1=sgn[:, 0:1], scalar2=None, op0=ALU.mult))
            S.op("act", lambda e: e.activation(out=wk2, in_=wk2, func=AF.Abs), reads=[Bat], writes=[Bat])
            S.op("act", lambda e: e.activation(out=COS, in_=wk2, func=AF.Sin, scale=-1.0, bias=halfpi), reads=[Bat, B_const], writes=[Bat])
            S.barrier()
            keep = AR.off
            t1 = [AR.alloc([128, 512], F32) for _ in range(2)]
            t2 = [AR.alloc([128, 512], F32) for _ in range(2)]
            qo = [AR.alloc([128, 512], BF16) for _ in range(2)]
            Bt1 = [Buf("rt1%d" % i) for i in range(2)]
            Bt2 = [Buf("rt2%d" % i) for i in range(2)]
            Bqo = [Buf("qo%d" % i) for i in range(2)]
            cntq = [0]

            def epi_rope(si, ps, Bps, t0, nt):
                i = cntq[0] % 2
                cntq[0] += 1
                if si < 32:
                    bc, bsc, dst, Bd, ci = bq[:, si:si + 1], bqs[:, si:si + 1], qT_d, B_qT, si
                else:
                    j = si - 32
                    bc, bsc, dst, Bd, ci = bkd[:, j:j + 1], bkds[:, j:j + 1], kT_d, B_kT, j
                S.op("dve", lambda e: e.scalar_tensor_tensor(out=t1[i][:, 0:nt], in0=ps[0], scalar=bc, in1=COS[:, t0:t0 + nt], op0=ALU.add, op1=ALU.mult),
                     reads=[Bps[0], Bat], writes=[Bt1[i]])
                S.op("dve", lambda e: e.scalar_tensor_tensor(out=t2[i][:, 0:nt], in0=ps[1], scalar=bsc, in1=SINS[:, t0:t0 + nt], op0=ALU.add, op1=ALU.mult),
                     reads=[Bps[1], Bat], writes=[Bt2[i]])
                S.op("dve", lambda e: e.tensor_tensor(out=qo[i][:, 0:nt], in0=t1[i][:, 0:nt], in1=t2[i][:, 0:nt], op=ALU.add), reads=[Bt1[i], Bt2[i]], writes=[Bqo[i]])
                S.dma("act", lambda e: e.dma_start(out=dst[ci, :, t0:t0 + nt], in_=qo[i][:, 0:nt]), reads=[Bqo[i]], writes=[Bd])

            def swap_pieces(cb):
                return [(wq, cb + 32, 32, 0), (wq, cb, 32, 32), (wq, cb + 96, 32, 64), (wq, cb + 64, 32, 96)]
            steps = []
            for c in range(32):
                steps.append([[(wq, 128 * c, 128, 0)], swap_pieces(128 * c)])
            for j in range(8):
                cb = 4096 + 64 * j
                steps.append([[(wq, cb, 64, 0), (wq, cb, 64, 64)],
                              [(wq, cb + 32, 32, 0), (wq, cb, 32, 32), (wq, cb + 32, 32, 64), (wq, cb, 32, 96)]])
            gemm_b(actA_d, B_actA, KC, tokblocks(0, A), steps, 2, epi_rope)
            S.barrier()
            AR.off = keep
            wv = AR.alloc([128, KC, 512], BF16)
            Bwv = Buf("wv")
            bvb = AR.alloc([128, 512], F32)
            actv = [AR.alloc([128, KC, 128], BF16) for _ in range(2)]
            Bactv = [Buf("actv%d" % i) for i in range(2)]
            vo = [AR.alloc([128, 512], BF16) for _ in range(2)]
            Bvo = [Buf("vo%d" % i) for i in range(2)]
            for q in range(4):
                S.dma("pool", lambda e, q=q: e.dma_start(out=wv[:, 8 * q:8 * q + 8, :], in_=wq[1024 * q:1024 * q + 1024, 4608:5120].rearrange("(kc p) n -> p kc n", p=128)), writes=[Bwv])
            S.dma("sp", lambda e: e.dma_start(out=bvb, in_=b_qkv[0, 4608:5120].partition_broadcast(128)), writes=[Bwv])
            for j in range(A // 128):
                i = j % 2
                S.dma("sp", lambda e, i=i, j=j: e.dma_start(out=actv[i], in_=actA_d.rearrange("c f t -> f c t")[:, :, 128 * j:128 * j + 128]), reads=[B_actA], writes=[Bactv[i]])
                for kc in range(KC):
                    S.op("pe", lambda e, i=i, kc=kc: e.matmul(banks[i][:, :], lhsT=actv[i][:, kc, :], rhs=wv[:, kc, :], start=(kc == 0), stop=(kc == KC - 1)),
                         reads=[Bactv[i], Bwv], writes=[BK[i]], pe_acc=(kc > 0))
                S.op("dve", lambda e, i=i: e.tensor_tensor(out=vo[i], in0=banks[i][:, :], in1=bvb, op=ALU.add), reads=[BK[i], Bwv], writes=[Bvo[i]])
                S.dma("act", lambda e, i=i, j=j: e.dma_start(out=v_d[128 * j:128 * j + 128, :], in_=vo[i]), reads=[Bvo[i]], writes=[B_v])
            S.barrier()
            if stop_after == 7:
                dbg_out["qT_d"] = qT_d
                dbg_out["kT_d"] = kT_d
                dbg_out["v_d"] = v_d

        if stop_after is None or stop_after >= 8:
            AR.reset()
            NT = A // 128
            mask = AR.alloc([128, 256], F32)
            mask0 = AR.alloc([128, 256], F32)
            sinkb = AR.alloc([128, NQH], F32)
            Bm = Buf("mask")
            S.op("pool", lambda e: e.memset(mask, 0.0), writes=[Bm])
            S.op("pool", lambda e: e.affine_select(out=mask, in_=mask, pattern=[[1, 256]], compare_op=ALU.is_ge, fill=NEG, base=-1, channel_multiplier=-1), reads=[Bm], writes=[Bm])
            S.op("pool", lambda e: e.affine_select(out=mask, in_=mask, pattern=[[-1, 256]], compare_op=ALU.is_ge, fill=NEG, base=128, channel_multiplier=1), reads=[Bm], writes=[Bm])
            S.dma("sp", lambda e: e.dma_start(out=mask0[:, 0:128], in_=halo_mask), writes=[Bm])
            S.dma("sp", lambda e: e.dma_start(out=sinkb, in_=sinks[0].partition_broadcast(128)), writes=[Bm])
            S.barrier()
            S.op("dve", lambda e: e.tensor_tensor(out=mask0[:, 0:128], in0=mask0[:, 0:128], in1=mask[:, 0:128], op=ALU.add), reads=[Bm], writes=[Bm])
            S.op("dve", lambda e: e.tensor_copy(out=mask0[:, 128:256], in_=mask[:, 128:256]), reads=[Bm], writes=[Bm])
            S.barrier()
            kTj = AR.alloc([128, A], BF16)
            BkTj = Buf("kTj")
            vj = AR.alloc([128, NT, 64], BF16)
            Bvj = Buf("vj")
            vpad = [AR.alloc([128, NT, 128], BF16) for _ in range(2)]
            Bvpad = Buf("vpad")
            qTc = [AR.alloc([128, A], BF16) for _ in range(2)]
            BqTc = [Buf("qTc%d" % i) for i in range(2)]
            aTc = [AR.alloc([128, M], BF16) for _ in range(2)]
            BaTc = [Buf("aTc%d" % i) for i in range(2)]
            NRr = 2
            s1 = [AR.alloc([128, 256], F32) for _ in range(NRr)]
            ex = [AR.alloc([128, 256], F32) for _ in range(NRr)]
            pb_ = [AR.alloc([128, 256], BF16) for _ in range(NRr)]
            pT = [AR.alloc([128, 256], BF16) for _ in range(NRr)]
            st_ = [AR.alloc([128, 8], F32) for _ in range(NRr)]
            Bs1 = [Buf("s1%d" % i) for i in range(NRr)]
            Bex = [Buf("ex%d" % i) for i in range(NRr)]
            Bpb = [Buf("pb%d" % i) for i in range(NRr)]
            BpT = [Buf("pT%d" % i) for i in range(NRr)]
            Bst = [Buf("st%d" % i) for i in range(NRr)]
            SC = 1.0 / 8.0
            B_attnT = B_actB
            it = 0
            for j in range(NKV):
                S.dma("sp", lambda e, j=j: e.dma_start(out=kTj, in_=kT_d[j]), reads=[B_kT], writes=[BkTj])
                S.dma("sp", lambda e, j=j: e.dma_start(out=vj, in_=v_d[:, 64 * j:64 * j + 64].rearrange("(t p) d -> p t d", p=128)), reads=[B_v], writes=[Bvj])
                S.op("pool", lambda e: e.memset(vpad[0], 0.0), writes=[Bvpad])
                S.op("pool", lambda e: e.memset(vpad[1], 0.0), writes=[Bvpad])
                S.op("dve", lambda e: e.tensor_copy(out=vpad[0][:, :, 0:64], in_=vj), reads=[Bvj, Bvpad], writes=[Bvpad])
                S.op("dve", lambda e: e.tensor_copy(out=vpad[1][:, :, 64:128], in_=vj), reads=[Bvj, Bvpad], writes=[Bvpad])
                for ci in range(4):
                    c = 4 * j + ci
                    qt, Bq = qTc[c % 2], BqTc[c % 2]
                    at, Ba = aTc[c % 2], BaTc[c % 2]
                    S.dma("sp", lambda e, qt=qt, c=c: e.dma_start(out=qt, in_=qT_d[c]), reads=[B_qT], writes=[Bq])
                    for n in range(M // 128):
                        a0 = 128 + 128 * n
                        ob = 6 + (n % 2)
                        for hh in range(2):
                            r = it % NRr
                            it += 1
                            sb_ = 2 * (it % 2)
                            hsl = slice(64 * hh, 64 * hh + 64)
                            head = 2 * c + hh
                            S.op("pe", lambda e, qt=qt, hsl=hsl, a0=a0, sb_=sb_: e.matmul(banks[sb_][:, 0:256], lhsT=qt[hsl, a0:a0 + 128], rhs=kTj[hsl, a0 - 128:a0 + 128], start=True, stop=True),
                                 reads=[Bq, BkTj], writes=[BK[sb_]])
                            mk = mask0 if n == 0 else mask
                            S.op("dve", lambda e, r=r, sb_=sb_, mk=mk: e.tensor_tensor(out=s1[r], in0=banks[sb_][:, 0:256], in1=mk, op=ALU.add), reads=[BK[sb_], Bm], writes=[Bs1[r]])
                            S.op("dve", lambda e, r=r: e.reduce_max(out=st_[r][:, 0:1], in_=s1[r], axis=AX.X), reads=[Bs1[r]], writes=[Bst[r]])
                            S.op("dve", lambda e, r=r: e.tensor_scalar(out=st_[r][:, 1:2], in0=st_[r][:, 0:1], scalar1=-SC, scalar2=None, op0=ALU.mult), reads=[Bst[r]], writes=[Bst[r]])
                            S.op("act", lambda e, r=r: e.activation(out=ex[r], in_=s1[r], func=AF.Exp, bias=st_[r][:, 1:2], scale=SC),
                                 reads=[Bs1[r], Bst[r]], writes=[Bex[r]])
                            S.op("dve", lambda e, r=r: e.reduce_sum(out=st_[r][:, 2:3], in_=ex[r], axis=AX.X), reads=[Bex[r]], writes=[Bst[r]])
                            S.op("act", lambda e, r=r, head=head: e.activation(out=st_[r][:, 3:4], in_=st_[r][:, 1:2], func=AF.Exp, bias=sinkb[:, head:head + 1], scale=1.0),
                                 reads=[Bst[r], Bm], writes=[Bst[r]])
                            S.op("dve", lambda e, r=r: e.tensor_tensor(out=st_[r][:, 4:5], in0=st_[r][:, 2:3], in1=st_[r][:, 3:4], op=ALU.add), reads=[Bst[r]], writes=[Bst[r]])
                            S.op("dve", lambda e, r=r: e.reciprocal(out=st_[r][:, 5:6], in_=st_[r][:, 4:5]), reads=[Bst[r]], writes=[Bst[r]])
                            S.op("dve", lambda e, r=r: e.tensor_scalar(out=pb_[r], in0=ex[r], scalar1=st_[r][:, 5:6], scalar2=None, op0=ALU.mult), reads=[Bex[r], Bst[r]], writes=[Bpb[r]])
                            ptb = banks[sb_ + 1][:].bitcast(BF16)
                            S.op("pe", lambda e, r=r, ptb=ptb: e.transpose(out=ptb[:, 0:128], in_=pb_[r][:, 0:128], identity=ident_b), reads=[Bpb[r], B_const], writes=[BK[sb_ + 1]])
                            S.op("pe", lambda e, r=r, ptb=ptb: e.transpose(out=ptb[:, 128:256], in_=pb_[r][:, 128:256], identity=ident_b), reads=[Bpb[r], B_const], writes=[BK[sb_ + 1]], pe_acc=True)
                            S.op("act", lambda e, r=r, ptb=ptb: e.copy(out=pT[r], in_=ptb[:, 0:256]), reads=[BK[sb_ + 1]], writes=[BpT[r]])
                            tl = a0 // 128
                            S.op("pe", lambda e, r=r, hh=hh, tl=tl, ob=ob: e.matmul(banks[ob][:, 0:128], lhsT=vpad[hh][:, tl - 1, :], rhs=pT[r][:, 0:128], start=(hh == 0), stop=False),
                                 reads=[Bvpad, BpT[r]], writes=[BK[ob]], pe_acc=(hh > 0))
                            S.op("pe", lambda e, r=r, hh=hh, tl=tl, ob=ob: e.matmul(banks[ob][:, 0:128], lhsT=vpad[hh][:, tl, :], rhs=pT[r][:, 128:256], start=False, stop=(hh == 1)),
                                 reads=[Bvpad, BpT[r]], writes=[BK[ob]], pe_acc=True)
                        S.op("act", lambda e, at=at, n=n, ob=ob: e.copy(out=at[:, 128 * n:128 * n + 128], in_=banks[ob][:, 0:128]), reads=[BK[ob]], writes=[Ba])
                    S.dma("act", lambda e, at=at, c=c: e.dma_start(out=actB_d[c, :, 128:A], in_=at), reads=[Ba], writes=[B_attnT])
            S.barrier()
            if stop_after == 8:
                dbg_out["actB_d"] = actB_d

        if stop_after is None or stop_after >= 9:
            AR.reset()
            boc = AR.alloc([128, KC], F32)
            tmp32 = AR.alloc([128, 128], F32)
            Bb = Buf("bo")
            Bt32 = Buf("t32")
            load_cols(boc, b_o[0], KC, Bb, tmp32, Bt32, banks[7], BK[7])
            S.barrier()
            mo = [AR.alloc([128, 512], F32) for _ in range(2)]
            Bmo = [Buf("omo%d" % i) for i in range(2)]
            cnt9 = [0]

            def epi9(si, ps, Bps, t0, nt):
                i = cnt9[0] % 2
                cnt9[0] += 1
                S.op("act", lambda e: e.activation(out=mo[i][:, 0:nt], in_=ps[0], func=AF.Identity, bias=boc[:, si:si + 1], scale=1.0), reads=[Bps[0], Bb], writes=[Bmo[i]])
                S.dma("act", lambda e: e.dma_start(out=mT_d[si, :, t0:t0 + nt], in_=mo[i][:, 0:nt]), reads=[Bmo[i]], writes=[B_mT])
            steps = [[[(w_o[0], 128 * n, 128, 0)]] for n in range(KC)]
            gemm_b(actB_d, B_actB, KC, tokblocks(128, A), steps, 1, epi9)
            S.barrier()
            epilogue_phase(h_d, h_d, B_h, g_post_mix[1], g_pre_ffn[1], actA_d, B_actA, 128, A)
        if stop_after == 9:
            dbg_out["h_d"] = h_d
        if stop_after is None or stop_after >= 10:
            ffn_phase(1, actA_d, B_actA, 128, A)
            B_out = Buf("out", multi=True)
            epilogue_phase(h_d, None, B_out, g_post_ffn[1], None, None, None, 128, A, final_out=out)

        if dump_all:
            dbg_out.update(dict(h_d=h_d, actA_d=actA_d, actB_d=actB_d, mT_d=mT_d, qT_d=qT_d, kT_d=kT_d, v_d=v_d))
            if stop_after is not None and stop_after <= 4:
                for kk in ("qT_d", "kT_d", "v_d"):
                    dbg_out.pop(kk)
                dbg_out["hnT_d"] = hnT_d
        dbg_specs = {}
        if dbg_out:
            AR.reset()
            for name, src in dbg_out.items():
                shp = list(src.shape)
                dt = src.dtype
                o = nc.dram_tensor("dbg_" + name, shp, dt, kind="ExternalOutput").ap()
                dbg_specs[name] = (shp, dt)
                flat_s = src.rearrange("a b c -> (a b) c") if len(shp) == 3 else src
                flat_o = o.rearrange("a b c -> (a b) c") if len(shp) == 3 else o
                rows = flat_s.shape[0]
                cols = flat_s.shape[1]
                tb = AR.alloc([128, cols], dt)
                Btb = Buf("dbgt")
                for r0 in range(0, rows, 128):
                    S.dma("sp", lambda e, r0=r0, tb=tb, flat_s=flat_s: e.dma_start(out=tb, in_=flat_s[r0:r0 + 128, :]), writes=[Btb])
                    S.dma("sp", lambda e, r0=r0, tb=tb, flat_o=flat_o: e.dma_start(out=flat_o[r0:r0 + 128, :], in_=tb), reads=[Btb], writes=[Buf("dbgo")])
        S.barrier()
        S.emit()
    return nc, S


WEIGHT_KEYS = ["norm_pre_mix", "norm_post_mix", "norm_pre_ffn", "norm_post_ffn", "s5_lam_re", "s5_lam_im", "s5_log_step",
               "s5_b_re", "s5_b_im", "s5_c_re", "s5_c_im", "s5_d", "s5_w_out1", "s5_b_out1", "s5_w_out2", "s5_b_out2",
               "attn_w_qkv", "attn_b_qkv", "attn_w_o", "attn_b_o", "attn_sinks", "ffn_w_gate", "ffn_w_up", "ffn_w_down"]

_CACHE = {}


def make_in_maps(inputs, P, M, n_cores, seq):
    x = np.asarray(inputs["x"])
    pos = np.asarray(inputs["positions"])
    w = {k: np.ascontiguousarray(np.asarray(inputs[k], dtype=np.float32)) for k in WEIGHT_KEYS}
    in_maps = []
    for core in range(n_cores):
        b, s = core // 2, core % 2
        xc = np.zeros((P + M, D), np.float32)
        pa = np.zeros((1, 128 + M), np.int32)
        if s == 0:
            xc[P:] = x[b, 0:M]
            pa[0, 128:] = pos[b, 0:M]
            hm = np.full((128, 128), NEG, np.float32)
        else:
            xc[:] = x[b, 0:P + M]
            pa[0, :] = pos[b, P - 128:P + M]
            hm = np.zeros((128, 128), np.float32)
        m = dict(w)
        m["xcat"] = xc
        m["posa"] = pa
        m["halo_mask"] = hm
        in_maps.append(m)
    return in_maps


def kernel(**inputs):
    P = M = 2048
    n_cores = 8
    if "prog" not in _CACHE:
        _CACHE["prog"] = build_program(P, M)[0]
    nc = _CACHE["prog"]
    in_maps = make_in_maps(inputs, P, M, n_cores, 4096)
    res = run_bass_kernel_spmd(nc, in_maps, core_ids=list(range(n_cores)))
    out = np.zeros((4, 4096, D), np.float32)
    for core in range(n_cores):
        b, s = core // 2, core % 2
        out[b, s * M:(s + 1) * M] = res.results[core]["out"]
    return out
```

```python
import contextlib
import numpy as np
import concourse.bass as bass
import concourse.mybir as mybir
from concourse.bass_utils import run_bass_kernel_spmd

F32 = mybir.dt.float32
BF16 = mybir.dt.bfloat16
I32 = mybir.dt.int32
ALU = mybir.AluOpType
AF = mybir.ActivationFunctionType
AX = mybir.AxisListType

D = 4096
KC = D // 128
DFF = 11008
HC = DFF // 128
NG = 256
NQH, NKV, HD = 64, 8, 64
EPS = 1e-6
MAGIC = 12582912.0
TWO_PI = float(2 * np.pi)
PI = float(np.pi)
NEG = -30000.0


class Buf:
    __slots__ = ("name", "lw", "rd", "guard", "multi")

    def __init__(self, name, multi=False):
        self.name = name
        self.lw = []
        self.rd = []
        self.guard = []
        self.multi = multi


class Sched:
    ENG = ("pe", "act", "dve", "pool", "sp")

    def __init__(self, nc, ndma=(("sp", 8), ("act", 4), ("pool", 3))):
        self.nc = nc
        self.prog = {e: [] for e in self.ENG}
        self.cnt = {e: 0 for e in self.ENG}
        self.waited = {e: {} for e in self.ENG}
        self.ndma = dict(ndma)
        self.dma_i = {q: 0 for q in self.ndma}
        self.n_inst = 0
        self.delay_fn = None
        self.delay_buf = Buf("delay")

    def _wait(self, eng, k, v):
        if self.waited[eng].get(k, 0) < v:
            self.waited[eng][k] = v
            self.prog[eng].append(("wait", k, v))

    def _deps(self, eng, reads, writes, pe_acc=False):
        need = {}

        def add(tok):
            if need.get(tok[0], 0) < tok[1]:
                need[tok[0]] = tok[1]
        for b in reads:
            for t in b.lw:
                add(t)
        for b in writes:
            if not b.multi:
                for t in b.lw:
                    if not (pe_acc and t[0] == "c_pe"):
                        add(t)
            for t in b.rd:
                add(t)
            for t in b.guard:
                add(t)
        for k, v in need.items():
            self._wait(eng, k, v)

    def _mark(self, tok, reads, writes):
        for b in reads:
            b.rd.append(tok)
        for b in writes:
            if b.multi:
                if b.rd:
                    b.guard = b.rd
                    b.rd = []
                    b.lw = []
                b.lw.append(tok)
            else:
                b.lw = [tok]
                b.rd = []

    def op(self, eng, fn, reads=(), writes=(), pe_acc=False):
        self._deps(eng, reads, writes, pe_acc)
        self.cnt[eng] += 1
        tok = ("c_" + eng, self.cnt[eng])
        self.prog[eng].append(("op", fn, tok[0], 1))
        self._mark(tok, reads, writes)
        self.n_inst += 1

    def dma(self, q, fn, reads=(), writes=()):
        self._deps(q, reads, writes)
        K = self.ndma[q]
        i = self.dma_i[q]
        self.dma_i[q] += 1
        slot, n = i % K, i // K + 1
        key = "d_%s_%d" % (q, slot)
        if n > 1:
            self._wait(q, key, 16 * (n - 1))
        tok = (key, 16 * n)
        self.prog[q].append(("op", fn, key, 16))
        self._mark(tok, reads, writes)
        self.n_inst += 1

    def barrier(self, engs=None):
        self._barrier_round(engs)
        if self.delay_fn is not None:
            for _ in range(10):
                self.op("dve", self.delay_fn, reads=[self.delay_buf], writes=[self.delay_buf])
            self._barrier_round(engs)

    def _barrier_round(self, engs=None):
        cur = {}
        for e in self.ENG:
            if self.cnt[e]:
                cur["c_" + e] = self.cnt[e]
        for q, K in self.ndma.items():
            for s in range(K):
                n = (self.dma_i[q] - s + K - 1) // K
                if n > 0:
                    cur["d_%s_%d" % (q, s)] = 16 * n
        for e in (engs or self.ENG):
            for k, v in cur.items():
                self._wait(e, k, v)

    def emit(self):
        nc = self.nc
        keys = ["c_" + e for e in self.ENG]
        for q, K in self.ndma.items():
            keys += ["d_%s_%d" % (q, s) for s in range(K)]
        sems = {}
        with contextlib.ExitStack() as st:
            for k in keys:
                sems[k] = st.enter_context(nc.semaphore(k))
            block = st.enter_context(nc.Block())

            def run(name):
                def body(eng):
                    for it in self.prog[name]:
                        if it[0] == "wait":
                            eng.wait_ge(sems[it[1]], it[2])
                        else:
                            it[1](eng).then_inc(sems[it[2]], it[3])
                return body
            block.tensor(run("pe"))
            block.scalar(run("act"))
            block.vector(run("dve"))
            block.gpsimd(run("pool"))
            block.sync(run("sp"))


class Arena:
    def __init__(self, ap_bf16, nbytes):
        self.ap = ap_bf16
        self.nbytes = nbytes
        self.off = 0

    def reset(self):
        self.off = 0

    def alloc(self, shape, dt):
        esz = 4 if dt in (F32, I32) else 2
        n = int(np.prod(shape[1:]))
        nb = n * esz
        self.off = (self.off + 63) // 64 * 64
        assert self.off + nb <= self.nbytes, ("arena overflow", self.off, nb, shape)
        a = self.ap[:, self.off // 2:(self.off + nb) // 2]
        self.off += nb
        if esz == 4:
            a = a.bitcast(dt)
        if len(shape) == 3:
            a = a.rearrange("p (a b) -> p a b", a=shape[1])
        elif len(shape) == 4:
            a = a.rearrange("p (a b c) -> p a b c", a=shape[1], b=shape[2])
        if shape[0] < 128:
            a = a[0:shape[0]]
        return a


def build_program(P, M, stop_after=None, dump_all=False):
    T = P + M
    A = 128 + M
    NB = T // 512
    nc = bass.Bass("TRN2", target_bir_lowering=False)

    def din(name, shape, dt=F32):
        return nc.dram_tensor(name, list(shape), dt, kind="ExternalInput").ap()

    def dscr(name, shape, dt):
        return nc.dram_tensor(name, list(shape), dt, kind="Internal").ap()

    xcat = din("xcat", [T, D])
    posa = din("posa", [1, A], I32)
    halo_mask = din("halo_mask", [128, 128])
    g_pre_mix = din("norm_pre_mix", [2, D])
    g_post_mix = din("norm_post_mix", [2, D])
    g_pre_ffn = din("norm_pre_ffn", [2, D])
    g_post_ffn = din("norm_post_ffn", [2, D])
    lam_re = din("s5_lam_re", [1, NG, 64])
    lam_im = din("s5_lam_im", [1, NG, 64])
    log_step = din("s5_log_step", [1, NG])
    b_re = din("s5_b_re", [1, NG, 64, 16])
    b_im = din("s5_b_im", [1, NG, 64, 16])
    c_re = din("s5_c_re", [1, NG, 16, 64])
    c_im = din("s5_c_im", [1, NG, 16, 64])
    s5_d = din("s5_d", [1, D])
    w_out1 = din("s5_w_out1", [1, D, D])
    b_out1 = din("s5_b_out1", [1, D])
    w_out2 = din("s5_w_out2", [1, D, D])
    b_out2 = din("s5_b_out2", [1, D])
    w_qkv = din("attn_w_qkv", [1, D, 5120])
    b_qkv = din("attn_b_qkv", [1, 5120])
    w_o = din("attn_w_o", [1, D, D])
    b_o = din("attn_b_o", [1, D])
    sinks = din("attn_sinks", [1, NQH])
    if stop_after is None or stop_after >= 5:
        w_gate = din("ffn_w_gate", [2, D, DFF])
        w_up = din("ffn_w_up", [2, D, DFF])
        w_down = din("ffn_w_down", [2, DFF, D])
    out = nc.dram_tensor("out", [M, D], F32, kind="ExternalOutput").ap()

    hnT_d = dscr("hnT_d", [KC, 128, T], BF16)
    actA_d = dscr("actA_d", [KC, 128, A], BF16)
    actB_d = dscr("actB_d", [KC, 128, A], BF16)
    mT_d = dscr("mT_d", [KC, 128, A], F32)
    h_d = dscr("h_d", [A, D], F32)
    qT_d = dscr("qT_d", [KC, 128, A], BF16)
    kT_d = dscr("kT_d", [NKV, 128, A], BF16)
    v_d = dscr("v_d", [A, 512], BF16)

    S = Sched(nc)
    dbg_out = {}

    with contextlib.ExitStack() as st:
        ARENA_BYTES = 198 * 1024
        arena_t = st.enter_context(nc.sbuf_tensor("arena", [128, ARENA_BYTES // 2], BF16))
        consts_t = st.enter_context(nc.sbuf_tensor("consts", [128, 768], F32))
        AR = Arena(arena_t[:], ARENA_BYTES)
        ident_f = consts_t[:, 0:128]
        ident_b = consts_t[:, 128:192].bitcast(BF16)
        halfpi = consts_t[:, 192:193]
        small = consts_t[:, 200:768]
        dly_t = st.enter_context(nc.sbuf_tensor("dly", [128, 1024], F32))
        S.delay_fn = lambda e: e.memset(dly_t[:], 0.0)
        banks = [st.enter_context(nc.psum_tensor("bank%d" % i, [128, 512], F32)) for i in range(8)]
        BK = [Buf("bank%d" % i) for i in range(8)]
        B_const = Buf("const")

        S.op("pool", lambda e: e.memset(consts_t[:, 0:192], 0.0), writes=[B_const])
        S.op("pool", lambda e: e.memset(halfpi, PI / 2), writes=[B_const])
        ones_tmp = small[:, 0:128]
        S.op("pool", lambda e: e.memset(ones_tmp, 1.0), writes=[B_const])
        S.op("pool", lambda e: e.affine_select(out=ident_f, in_=ones_tmp, pattern=[[1, 128]], compare_op=ALU.is_equal,
                                               fill=0.0, base=0, channel_multiplier=-1), reads=[B_const], writes=[B_const])
        S.op("dve", lambda e: e.tensor_copy(out=ident_b, in_=ident_f), reads=[B_const], writes=[B_const])
        S.barrier()

        def rstd_from_ss(ss, rstd, rd, wr):
            S.op("dve", lambda e: e.tensor_scalar(out=rstd, in0=ss, scalar1=1.0 / D, scalar2=EPS, op0=ALU.mult, op1=ALU.add),
                 reads=rd, writes=wr)
            S.op("act", lambda e: e.activation(out=rstd, in_=rstd, func=AF.Sqrt), reads=wr, writes=wr)
            S.op("dve", lambda e: e.reciprocal(out=rstd, in_=rstd), reads=wr, writes=wr)

        def load_cols(dst, vec, n, Bdst, tmp, Btmp, bank, Bbank):
            S.dma("sp", lambda e: e.dma_start(out=tmp[0:n, :], in_=vec.rearrange("(c p) -> c p", p=128)), writes=[Btmp])
            S.op("pe", lambda e: e.transpose(out=bank[:, 0:n], in_=tmp[0:n, :], identity=ident_f[0:n, 0:n]),
                 reads=[Btmp, B_const], writes=[Bbank])
            S.op("dve", lambda e: e.tensor_copy(out=dst, in_=bank[:, 0:n]), reads=[Bbank], writes=[Bdst])

        def norm_transpose(src, Bsrc, gain_bc, Bgain, dstT_d, Bdst, col0, bufs, idx):
            junk, Bjunk, hn, Bhn, hnT, BhnT, ssb, Bss = bufs
            ss = ssb[:, 0:1]
            rstd = ssb[:, 1:2]
            S.op("act", lambda e: e.activation(out=junk, in_=src, func=AF.Square), reads=[Bsrc], writes=[Bjunk])
            S.op("dve", lambda e: e.reduce_sum(out=ss, in_=junk, axis=AX.X), reads=[Bjunk], writes=[Bss])
            rstd_from_ss(ss, rstd, [Bss], [Bss])
            S.op("dve", lambda e: e.scalar_tensor_tensor(out=hn, in0=src, scalar=rstd, in1=gain_bc, op0=ALU.mult, op1=ALU.mult),
                 reads=[Bsrc, Bss, Bgain], writes=[Bhn])
            for q in range(4):
                bk = 4 + (q % 2) + 2 * (idx % 2)
                pb = banks[bk][:].bitcast(BF16)
                for i in range(8):
                    c = 8 * q + i
                    S.op("pe", lambda e, c=c, i=i, pb=pb: e.transpose(out=pb[:, 128 * i:128 * i + 128], in_=hn[:, 128 * c:128 * c + 128], identity=ident_b),
                         reads=[Bhn, B_const], writes=[BK[bk]], pe_acc=(i > 0))
                dsl = hnT[:, 8 * q:8 * q + 8, :]
                src_ps = pb.rearrange("p (a b) -> p a b", a=8)
                if q % 2 == 0:
                    S.op("act", lambda e, dsl=dsl, src_ps=src_ps: e.copy(out=dsl, in_=src_ps), reads=[BK[bk]], writes=[BhnT])
                else:
                    S.op("dve", lambda e, dsl=dsl, src_ps=src_ps: e.tensor_copy(out=dsl, in_=src_ps), reads=[BK[bk]], writes=[BhnT])
            S.dma("act", lambda e: e.dma_start(out=dstT_d.rearrange("c f t -> f c t")[:, :, col0:col0 + 128], in_=hnT),
                  reads=[BhnT], writes=[Bdst])

        def nt_bufs(tag):
            return (AR.alloc([128, D], BF16), Buf("junk" + tag), AR.alloc([128, D], BF16), Buf("hn" + tag),
                    AR.alloc([128, KC, 128], BF16), Buf("hnT" + tag), AR.alloc([128, 4], F32), Buf("ss" + tag))

        def load_gain(row):
            g = AR.alloc([128, D], F32)
            Bg = Buf("gain")
            S.dma("sp", lambda e: e.dma_start(out=g, in_=row.partition_broadcast(128)), writes=[Bg])
            return g, Bg

        B_hnT_d = Buf("hnT_d", multi=True)
        AR.reset()
        g0, Bg0 = load_gain(g_pre_mix[0])
        xs = [AR.alloc([128, D], F32) for _ in range(2)]
        Bxs = [Buf("xs%d" % i) for i in range(2)]
        ntb = [nt_bufs("a"), nt_bufs("b")]
        for j in range(T // 128):
            x_t, Bx = xs[j % 2], Bxs[j % 2]
            S.dma("sp", lambda e, x_t=x_t, j=j: e.dma_start(out=x_t, in_=xcat[128 * j:128 * j + 128, :]), writes=[Bx])
            norm_transpose(x_t, Bx, g0, Bg0, hnT_d, B_hnT_d, 128 * j, ntb[j % 2], j)
        S.barrier()
        if stop_after == 1:
            dbg_out["hnT_d"] = hnT_d

        B_actA = Buf("actA_d", multi=True)
        if stop_after is None or stop_after >= 2:
            AR.reset()
            lamre = AR.alloc([128, NG], F32)
            lamim = AR.alloc([128, NG], F32)
            stepb = AR.alloc([128, NG], F32)
            Rr = AR.alloc([128, NG], F32)
            THW = AR.alloc([128, NG], F32)
            THN = AR.alloc([128, NG], F32)
            BRE = AR.alloc([128, NG], F32)
            BIM = AR.alloc([128, NG], F32)
            w1 = AR.alloc([128, NG], F32)
            w2 = AR.alloc([128, NG], F32)
            w3 = AR.alloc([128, NG], F32)
            w4 = AR.alloc([128, NG], F32)
            dcol = AR.alloc([128, KC], F32)
            tmp32 = AR.alloc([128, 128], F32)
            Bp = Buf("s5prep")
            for h in range(2):
                S.dma("sp", lambda e, h=h: e.dma_start(out=lamre[64 * h:64 * h + 64, :], in_=lam_re[0].rearrange("g p -> p g"),
                                                       allow_slow_non_contiguous=True), writes=[Bp])
                S.dma("sp", lambda e, h=h: e.dma_start(out=lamim[64 * h:64 * h + 64, :], in_=lam_im[0].rearrange("g p -> p g"),
                                                       allow_slow_non_contiguous=True), writes=[Bp])
            S.dma("sp", lambda e: e.dma_start(out=stepb, in_=log_step[0].partition_broadcast(128)), writes=[Bp])
            Btmp32 = Buf("tmp32")
            load_cols(dcol, s5_d[0], KC, Bp, tmp32, Btmp32, banks[0], BK[0])
            S.barrier()
            V = lambda fn: S.op("dve", fn, reads=[Bp], writes=[Bp])
            Aop = lambda fn: S.op("act", fn, reads=[Bp], writes=[Bp])
            Aop(lambda e: e.activation(out=stepb, in_=stepb, func=AF.Exp))
            V(lambda e: e.tensor_tensor(out=w1, in0=lamre, in1=stepb, op=ALU.mult))
            V(lambda e: e.tensor_tensor(out=w2, in0=lamim, in1=stepb, op=ALU.mult))
            Aop(lambda e: e.activation(out=Rr, in_=w1, func=AF.Exp))
            V(lambda e: e.tensor_scalar(out=w3, in0=w2, scalar1=1.0 / TWO_PI, scalar2=MAGIC, op0=ALU.mult, op1=ALU.add))
            V(lambda e: e.tensor_scalar(out=w3, in0=w3, scalar1=-MAGIC, scalar2=-TWO_PI, op0=ALU.add, op1=ALU.mult))
            V(lambda e: e.tensor_tensor(out=THW, in0=w3, in1=w2, op=ALU.add))
            V(lambda e: e.tensor_scalar(out=THW, in0=THW, scalar1=-PI, scalar2=PI, op0=ALU.max, op1=ALU.min))
            V(lambda e: e.tensor_scalar(out=THN, in0=THW, scalar1=1.0 / TWO_PI, scalar2=None, op0=ALU.mult))
            Aop(lambda e: e.activation(out=w3, in_=THW, func=AF.Sin))
            Aop(lambda e: e.activation(out=w4, in_=THW, func=AF.Abs))
            Aop(lambda e: e.activation(out=w4, in_=w4, func=AF.Sin, scale=-1.0, bias=halfpi))
            V(lambda e: e.tensor_tensor(out=w3, in0=w3, in1=Rr, op=ALU.mult))
            V(lambda e: e.tensor_tensor(out=w4, in0=w4, in1=Rr, op=ALU.mult))
            V(lambda e: e.tensor_scalar(out=w4, in0=w4, scalar1=-1.0, scalar2=None, op0=ALU.add))
            V(lambda e: e.tensor_tensor(out=w1, in0=lamre, in1=lamre, op=ALU.mult))
            V(lambda e: e.tensor_tensor(out=w2, in0=lamim, in1=lamim, op=ALU.mult))
            V(lambda e: e.tensor_tensor(out=w1, in0=w1, in1=w2, op=ALU.add))
            V(lambda e: e.reciprocal(out=w1, in_=w1))
            V(lambda e: e.tensor_tensor(out=BRE, in0=w4, in1=lamre, op=ALU.mult))
            V(lambda e: e.tensor_tensor(out=w2, in0=w3, in1=lamim, op=ALU.mult))
            V(lambda e: e.tensor_tensor(out=BRE, in0=BRE, in1=w2, op=ALU.add))
            V(lambda e: e.tensor_tensor(out=BRE, in0=BRE, in1=w1, op=ALU.mult))
            V(lambda e: e.tensor_tensor(out=BIM, in0=w3, in1=lamre, op=ALU.mult))
            V(lambda e: e.tensor_tensor(out=w2, in0=w4, in1=lamim, op=ALU.mult))
            V(lambda e: e.tensor_tensor(out=BIM, in0=BIM, in1=w2, op=ALU.subtract))
            V(lambda e: e.tensor_tensor(out=BIM, in0=BIM, in1=w1, op=ALU.mult))
            rmask = AR.alloc([128, 8], F32)
            cmask = AR.alloc([128, 8, 128], F32)
            onesb = AR.alloc([128, 8, 128], F32)
            S.op("pool", lambda e: e.memset(onesb, 1.0), reads=[Bp], writes=[Bp])
            S.op("pool", lambda e: e.affine_select(out=rmask, in_=onesb[:, 0, 0:8], pattern=[[-16, 8]], compare_op=ALU.is_ge, fill=0.0,
                                                   base=0, channel_multiplier=1), reads=[Bp], writes=[Bp])
            S.op("pool", lambda e: e.affine_select(out=rmask, in_=rmask, pattern=[[16, 8]], compare_op=ALU.is_ge, fill=0.0,
                                                   base=15, channel_multiplier=-1), reads=[Bp], writes=[Bp])
            S.op("pool", lambda e: e.affine_select(out=cmask, in_=onesb, pattern=[[-16, 8], [1, 128]], compare_op=ALU.is_ge, fill=0.0,
                                                   base=0, channel_multiplier=0), reads=[Bp], writes=[Bp])
            S.op("pool", lambda e: e.affine_select(out=cmask, in_=cmask, pattern=[[16, 8], [-1, 128]], compare_op=ALU.is_ge, fill=0.0,
                                                   base=15, channel_multiplier=0), reads=[Bp], writes=[Bp])
            tposi = AR.alloc([128, T], I32)
            tposf = AR.alloc([128, T], F32)
            S.op("pool", lambda e: e.iota(tposi, pattern=[[1, T]], base=0, channel_multiplier=0), writes=[Bp])
            S.op("dve", lambda e: e.tensor_copy(out=tposf, in_=tposi), reads=[Bp], writes=[Bp])
            S.barrier()

            hk = [AR.alloc([128, T], BF16) for _ in range(2)]
            Bhk = [Buf("hk%d" % i) for i in range(2)]
            Bp_f = AR.alloc([64, 2, 8, 16], F32)
            BBp = Buf("Bp_f")
            Cn = AR.alloc([128, 2, 128], F32)
            BCn = Buf("Cn")
            BTcat = AR.alloc([128, 256], F32)
            BBT = Buf("BTcat")
            CT = AR.alloc([128, 2, 128], F32)
            E1 = AR.alloc([128, 128], F32)
            E2 = AR.alloc([128, 128], F32)
            cw1 = AR.alloc([128, 128], F32)
            cw2 = AR.alloc([128, 128], F32)
            BE = Buf("E")
            WB = AR.alloc([128, 8, 256], BF16)
            WC = AR.alloc([128, 8, 256], BF16)
            BW = Buf("W")
            carry = AR.alloc([128, 8], F32)
            Bcarry = Buf("carry")
            NR = 3
            tA = [AR.alloc([128, 512], F32) for _ in range(NR)]
            tB = [AR.alloc([128, 512], F32) for _ in range(NR)]
            CSt = [AR.alloc([128, 512], F32) for _ in range(NR)]
            SNt = [AR.alloc([128, 512], F32) for _ in range(NR)]
            X2s = [AR.alloc([128, 512], F32) for _ in range(NR)]
            M1 = [AR.alloc([128, 512], F32) for _ in range(NR)]
            sT = [AR.alloc([128, 512], F32) for _ in range(NR)]
            Z1 = [AR.alloc([128, 512], BF16) for _ in range(NR)]
            Z2 = [AR.alloc([128, 512], BF16) for _ in range(NR)]
            BtA = [Buf("tA%d" % i) for i in range(NR)]
            BtB = [Buf("tB%d" % i) for i in range(NR)]
            BCS = [Buf("CS%d" % i) for i in range(NR)]
            BSN = [Buf("SN%d" % i) for i in range(NR)]
            BX2 = [Buf("X2s%d" % i) for i in range(NR)]
            BM1 = [Buf("M1%d" % i) for i in range(NR)]
            BsT = [Buf("sT%d" % i) for i in range(NR)]
            BZ1 = [Buf("Z1%d" % i) for i in range(NR)]
            BZ2 = [Buf("Z2%d" % i) for i in range(NR)]
            yb = [AR.alloc([128, 512], F32) for _ in range(2)]
            Byb = [Buf("yb%d" % i) for i in range(2)]
            gb = [AR.alloc([128, 512], BF16) for _ in range(2)]
            Bgb = [Buf("gb%d" % i) for i in range(2)]
            it = 0
            for k in range(KC):
                hkt, Bh = hk[k % 2], Bhk[k % 2]
                S.dma("sp", lambda e, hkt=hkt, k=k: e.dma_start(out=hkt, in_=hnT_d[k]), reads=[B_hnT_d], writes=[Bh])
                S.dma("sp", lambda e, k=k: e.dma_start(out=Bp_f[:, 0], in_=b_re[0, 8 * k:8 * k + 8].rearrange("g p c -> p g c")), writes=[BBp])
                S.dma("sp", lambda e, k=k: e.dma_start(out=Bp_f[:, 1], in_=b_im[0, 8 * k:8 * k + 8].rearrange("g p c -> p g c")), writes=[BBp])
                for h in range(2):
                    S.dma("sp", lambda e, k=k, h=h: e.dma_start(out=Cn[:, 0, 64 * h:64 * h + 64], in_=c_re[0, 8 * k:8 * k + 8].rearrange("g c p -> (g c) p")), writes=[BCn])
                    S.dma("sp", lambda e, k=k, h=h: e.dma_start(out=Cn[:, 1, 64 * h:64 * h + 64], in_=c_im[0, 8 * k:8 * k + 8].rearrange("g c p -> (g c) p")), writes=[BCn])
                S.op("pe", lambda e: e.transpose(out=banks[0][:, 0:64], in_=Bp_f[:, 0].rearrange("p g c -> p (g c)"), identity=ident_f[0:64, 0:64]),
                     reads=[BBp, B_const], writes=[BK[0]])
                S.op("pe", lambda e: e.transpose(out=banks[0][:, 64:128], in_=Bp_f[:, 1].rearrange("p g c -> p (g c)"), identity=ident_f[0:64, 0:64]),
                     reads=[BBp, B_const], writes=[BK[0]], pe_acc=True)
                S.op("dve", lambda e: e.tensor_copy(out=BTcat[:, 0:128], in_=banks[0][:, 0:128]), reads=[BK[0]], writes=[BBT])
                S.op("dve", lambda e: e.tensor_copy(out=BTcat[:, 128:192], in_=banks[0][:, 64:128]), reads=[BK[0]], writes=[BBT])
                S.op("dve", lambda e: e.tensor_scalar(out=BTcat[:, 192:256], in0=banks[0][:, 0:64], scalar1=-1.0, scalar2=None, op0=ALU.mult),
                     reads=[BK[0]], writes=[BBT])
                S.op("pe", lambda e: e.transpose(out=banks[1][:, 0:128], in_=Cn[:, 0], identity=ident_f), reads=[BCn, B_const], writes=[BK[1]])
                S.op("pe", lambda e: e.transpose(out=banks[1][:, 128:256], in_=Cn[:, 1], identity=ident_f), reads=[BCn, B_const], writes=[BK[1]], pe_acc=True)
                S.op("dve", lambda e: e.tensor_copy(out=CT.rearrange("p a b -> p (a b)"), in_=banks[1][:, 0:256]), reads=[BK[1]], writes=[BE])
                bre_bc = BRE[:, 8 * k:8 * k + 8].unsqueeze(2).to_broadcast([128, 8, 16])
                bim_bc = BIM[:, 8 * k:8 * k + 8].unsqueeze(2).to_broadcast([128, 8, 16])
                r3 = lambda a: a.rearrange("p (g c) -> p g c", g=8)
                VE = lambda fn: S.op("dve", fn, reads=[BE, Bp], writes=[BE])
                VE(lambda e, bre_bc=bre_bc: e.tensor_tensor(out=r3(cw1), in0=r3(CT[:, 0]), in1=bre_bc, op=ALU.mult))
                VE(lambda e, bim_bc=bim_bc: e.tensor_tensor(out=r3(cw2), in0=r3(CT[:, 1]), in1=bim_bc, op=ALU.mult))
                VE(lambda e: e.tensor_tensor(out=cw1, in0=cw1, in1=cw2, op=ALU.subtract))
                VE(lambda e, bim_bc=bim_bc: e.tensor_tensor(out=r3(cw2), in0=r3(CT[:, 0]), in1=bim_bc, op=ALU.mult))
                VE(lambda e, bre_bc=bre_bc: e.tensor_tensor(out=r3(E2), in0=r3(CT[:, 1]), in1=bre_bc, op=ALU.mult))
                VE(lambda e: e.tensor_tensor(out=cw2, in0=cw2, in1=E2, op=ALU.add))
                VE(lambda e: e.tensor_copy(out=E1[0:64], in_=cw1[0:64]))
                VE(lambda e: e.tensor_scalar(out=E1[64:128], in0=cw2[64:128], scalar1=-1.0, scalar2=None, op0=ALU.mult))
                VE(lambda e: e.tensor_scalar(out=E2[0:64], in0=cw2[0:64], scalar1=-1.0, scalar2=None, op0=ALU.mult))
                VE(lambda e: e.tensor_scalar(out=E2[64:128], in0=cw1[64:128], scalar1=-1.0, scalar2=None, op0=ALU.mult))
                for gl in range(8):
                    S.op("dve", lambda e, gl=gl: e.tensor_scalar(out=WB[:, gl, :], in0=BTcat, scalar1=rmask[:, gl:gl + 1], scalar2=None, op0=ALU.mult),
                         reads=[BBT, Bp], writes=[BW])
                    S.op("dve", lambda e, gl=gl: e.tensor_tensor(out=WC[:, gl, 0:128], in0=E1, in1=cmask[:, gl, :], op=ALU.mult), reads=[BE, Bp], writes=[BW])
                    S.op("dve", lambda e, gl=gl: e.tensor_tensor(out=WC[:, gl, 128:256], in0=E2, in1=cmask[:, gl, :], op=ALU.mult), reads=[BE, Bp], writes=[BW])
                S.op("pool", lambda e: e.memset(carry, 0.0), writes=[Bcarry])
                for b in range(NB):
                    c0 = 512 * b
                    need_lo = max(c0, P - 128)
                    need = need_lo < c0 + 512
                    ybk = 2 + (b % 2)
                    for gl in range(8):
                        g = 8 * k + gl
                        r = it % NR
                        it += 1
                        xb = 4 + 2 * (it % 2)
                        psX1, psX2 = banks[xb], banks[xb + 1]
                        S.op("pe", lambda e, psX1=psX1, gl=gl, hkt=hkt, c0=c0: e.matmul(psX1[:, :], lhsT=WB[:, gl, 0:128], rhs=hkt[:, c0:c0 + 512], start=True, stop=True),
                             reads=[BW, Bh], writes=[BK[xb]])
                        S.op("pe", lambda e, psX2=psX2, gl=gl, hkt=hkt, c0=c0: e.matmul(psX2[:, :], lhsT=WB[:, gl, 128:256], rhs=hkt[:, c0:c0 + 512], start=True, stop=True),
                             reads=[BW, Bh], writes=[BK[xb + 1]])
                        tp = tposf[:, c0:c0 + 512]
                        S.op("pool", lambda e, r=r, g=g, tp=tp: e.tensor_scalar(out=tA[r], in0=tp, scalar1=THN[:, g:g + 1], scalar2=MAGIC, op0=ALU.mult, op1=ALU.add),
                             reads=[Bp], writes=[BtA[r]])
                        S.op("pool", lambda e, r=r: e.tensor_scalar(out=tA[r], in0=tA[r], scalar1=-MAGIC, scalar2=-TWO_PI, op0=ALU.add, op1=ALU.mult),
                             reads=[BtA[r]], writes=[BtA[r]])
                        S.op("dve", lambda e, r=r, g=g, tp=tp: e.scalar_tensor_tensor(out=tA[r], in0=tp, scalar=THW[:, g:g + 1], in1=tA[r], op0=ALU.mult, op1=ALU.add),
                             reads=[BtA[r], Bp], writes=[BtA[r]])
                        S.op("pool", lambda e, r=r: e.tensor_scalar(out=tA[r], in0=tA[r], scalar1=PI, scalar2=-PI, op0=ALU.min, op1=ALU.max),
                             reads=[BtA[r]], writes=[BtA[r]])
                        S.op("act", lambda e, r=r: e.activation(out=tB[r], in_=tA[r], func=AF.Abs), reads=[BtA[r]], writes=[BtB[r]])
                        S.op("act", lambda e, r=r: e.activation(out=SNt[r], in_=tA[r], func=AF.Sin), reads=[BtA[r]], writes=[BSN[r]])
                        S.op("act", lambda e, r=r: e.activation(out=CSt[r], in_=tB[r], func=AF.Sin, scale=-1.0, bias=halfpi), reads=[BtB[r], B_const], writes=[BCS[r]])
                        S.op("act", lambda e, r=r, psX2=psX2: e.copy(out=X2s[r], in_=psX2[:, :]), reads=[BK[xb + 1]], writes=[BX2[r]])
                        S.op("dve", lambda e, r=r, psX1=psX1: e.tensor_tensor(out=M1[r], in0=psX1[:, :], in1=CSt[r], op=ALU.mult), reads=[BK[xb], BCS[r]], writes=[BM1[r]])
                        S.op("pool", lambda e, r=r: e.tensor_tensor(out=X2s[r], in0=X2s[r], in1=SNt[r], op=ALU.mult), reads=[BX2[r], BSN[r]], writes=[BX2[r]])
                        S.op("pool", lambda e, r=r: e.tensor_tensor(out=M1[r], in0=M1[r], in1=X2s[r], op=ALU.add), reads=[BM1[r], BX2[r]], writes=[BM1[r]])
                        S.op("dve", lambda e, r=r, g=g, gl=gl: e.tensor_tensor_scan(out=sT[r], data0=Rr[:, g:g + 1].to_broadcast([128, 512]), data1=M1[r],
                                                                                   initial=carry[:, gl:gl + 1], op0=ALU.mult, op1=ALU.add),
                             reads=[BM1[r], Bp, Bcarry], writes=[BsT[r]])
                        S.op("act", lambda e, r=r, gl=gl: e.copy(out=carry[:, gl:gl + 1], in_=sT[r][:, 511:512]), reads=[BsT[r]], writes=[Bcarry])
                        if need:
                            S.op("pool", lambda e, r=r: e.tensor_tensor(out=Z1[r], in0=sT[r], in1=CSt[r], op=ALU.mult), reads=[BsT[r], BCS[r]], writes=[BZ1[r]])
                            S.op("dve", lambda e, r=r: e.tensor_tensor(out=Z2[r], in0=sT[r], in1=SNt[r], op=ALU.mult), reads=[BsT[r], BSN[r]], writes=[BZ2[r]])
                            S.op("pe", lambda e, r=r, gl=gl, ybk=ybk: e.matmul(banks[ybk][:, :], lhsT=WC[:, gl, 0:128], rhs=Z1[r], start=(gl == 0), stop=False),
                                 reads=[BW, BZ1[r]], writes=[BK[ybk]], pe_acc=(gl > 0))
                            S.op("pe", lambda e, r=r, gl=gl, ybk=ybk: e.matmul(banks[ybk][:, :], lhsT=WC[:, gl, 128:256], rhs=Z2[r], start=False, stop=(gl == 7)),
                                 reads=[BW, BZ2[r]], writes=[BK[ybk]], pe_acc=True)
                    if need:
                        lo = need_lo - c0
                        n = 512 - lo
                        yt, By = yb[b % 2], Byb[b % 2]
                        gt, Bg = gb[b % 2], Bgb[b % 2]
                        S.op("dve", lambda e, yt=yt, k=k, hkt=hkt, c0=c0, lo=lo, n=n, ybk=ybk: e.scalar_tensor_tensor(
                            out=yt[:, 0:n], in0=hkt[:, c0 + lo:c0 + 512], scalar=dcol[:, k:k + 1], in1=banks[ybk][:, lo:512], op0=ALU.mult, op1=ALU.add),
                            reads=[Bh, BK[ybk], Bp], writes=[By])
                        t2 = tA[0]
                        S.op("dve", lambda e, yt=yt, n=n, t2=t2: e.tensor_tensor(out=t2[:, 0:n], in0=yt[:, 0:n], in1=yt[:, 0:n], op=ALU.mult), reads=[By], writes=[BtA[0]])
                        S.op("dve", lambda e, n=n, t2=t2: e.tensor_scalar(out=t2[:, 0:n], in0=t2[:, 0:n], scalar1=0.044715, scalar2=1.0, op0=ALU.mult, op1=ALU.add),
                             reads=[BtA[0]], writes=[BtA[0]])
                        S.op("dve", lambda e, yt=yt, n=n, t2=t2: e.tensor_tensor(out=t2[:, 0:n], in0=t2[:, 0:n], in1=yt[:, 0:n], op=ALU.mult), reads=[BtA[0], By], writes=[BtA[0]])
                        S.op("act", lambda e, n=n, t2=t2: e.activation(out=t2[:, 0:n], in_=t2[:, 0:n], func=AF.Sigmoid, scale=1.5957691216057308), reads=[BtA[0]], writes=[BtA[0]])
                        S.op("dve", lambda e, yt=yt, gt=gt, n=n, t2=t2: e.tensor_tensor(out=gt[:, 0:n], in0=t2[:, 0:n], in1=yt[:, 0:n], op=ALU.mult), reads=[BtA[0], By], writes=[Bg])
                        a0 = need_lo - (P - 128)
                        S.dma("act", lambda e, gt=gt, k=k, a0=a0, n=n: e.dma_start(out=actA_d[k, :, a0:a0 + n], in_=gt[:, 0:n]), reads=[Bg], writes=[B_actA])
            S.barrier()
        if stop_after == 2:
            dbg_out["actA_d"] = actA_d

        def gemm_b(act_d, Bact, KCn, tok_blocks, steps, nw, epilogue, slab_bufs=2):
            AR_mark = AR.off
            assert len(steps) % 2 == 0
            maxnt = max(nt for _, nt in tok_blocks)
            act = AR.alloc([128, KCn, maxnt], BF16)
            Bact_sb = Buf("act_sb", multi=True)
            slabs = [[AR.alloc([128, KCn, 256], BF16) for _ in range(nw)] for _ in range(slab_bufs)]
            Bsl = [[Buf("slab%d_%d" % (i, w), multi=True) for w in range(nw)] for i in range(slab_bufs)]
            pairs = []
            for p in range(len(steps) // 2):
                per_w = []
                for w in range(nw):
                    pcs = [(W2d, c0, n, d) for (W2d, c0, n, d) in steps[2 * p][w]] + [(W2d, c0, n, d + 128) for (W2d, c0, n, d) in steps[2 * p + 1][w]]
                    merged = []
                    for pc in pcs:
                        if merged and merged[-1][0] is pc[0] and merged[-1][1] + merged[-1][2] == pc[1] and merged[-1][3] + merged[-1][2] == pc[3]:
                            m = merged[-1]
                            merged[-1] = (m[0], m[1], m[2] + pc[2], m[3])
                        else:
                            merged.append(pc)
                    per_w.append(merged)
                pairs.append(per_w)
            ctr = 0
            for (t0, nt) in tok_blocks:
                step_kc = 8
                for kc0 in range(0, KCn, step_kc):
                    kc1 = min(KCn, kc0 + step_kc)
                    S.dma("sp", lambda e, kc0=kc0, kc1=kc1, t0=t0, nt=nt: e.dma_start(
                        out=act[:, kc0:kc1, 0:nt], in_=act_d.rearrange("c f t -> f c t")[:, kc0:kc1, t0:t0 + nt]),
                        reads=[Bact], writes=[Bact_sb])
                for p, per_w in enumerate(pairs):
                    sb_i = ctr % slab_bufs
                    pbase = (ctr % 2) * 2 * nw
                    ctr += 1
                    for w in range(nw):
                        for (W2d, c0, ncols, doff) in per_w[w]:
                            S.dma("pool", lambda e, w=w, sb_i=sb_i, W2d=W2d, c0=c0, ncols=ncols, doff=doff: e.dma_start(
                                out=slabs[sb_i][w][:, :, doff:doff + ncols], in_=W2d[:, c0:c0 + ncols].rearrange("(kc p) n -> p kc n", p=128)),
                                writes=[Bsl[sb_i][w]])
                    for half in range(2):
                        for w in range(nw):
                            bk = pbase + 2 * w + half
                            for kc in range(KCn):
                                S.op("pe", lambda e, w=w, kc=kc, sb_i=sb_i, bk=bk, half=half, nt=nt: e.matmul(
                                    banks[bk][:, 0:nt], lhsT=slabs[sb_i][w][:, kc, 128 * half:128 * half + 128], rhs=act[:, kc, 0:nt], start=(kc == 0), stop=(kc == KCn - 1)),
                                    reads=[Bsl[sb_i][w], Bact_sb], writes=[BK[bk]], pe_acc=(kc > 0))
                        epilogue(2 * p + half, [banks[pbase + 2 * w + half][:, 0:nt] for w in range(nw)], [BK[pbase + 2 * w + half] for w in range(nw)], t0, nt)
            return AR_mark

        def tokblocks(lo, hi):
            out_ = []
            t = lo
            while t < hi:
                n = min(512, hi - t)
                out_.append((t, n))
                t += n
            return out_

        B_mT = Buf("mT_d", multi=True)
        if stop_after is None or stop_after >= 3:
            AR.reset()
            b1c = AR.alloc([128, KC], F32)
            b2c = AR.alloc([128, KC], F32)
            tmp32 = AR.alloc([128, 128], F32)
            Bb = Buf("bias")
            Bt32 = Buf("t32")
            load_cols(b1c, b_out1[0], KC, Bb, tmp32, Bt32, banks[7], BK[7])
            load_cols(b2c, b_out2[0], KC, Bb, tmp32, Bt32, banks[7], BK[7])
            S.barrier()
            sg = [AR.alloc([128, 512], F32) for _ in range(2)]
            Bsg = [Buf("sg%d" % i) for i in range(2)]
            mo = [AR.alloc([128, 512], F32) for _ in range(2)]
            Bmo = [Buf("mo%d" % i) for i in range(2)]
            cnt3 = [0]

            def epi3(si, ps, Bps, t0, nt):
                i = cnt3[0] % 2
                cnt3[0] += 1
                S.op("act", lambda e: e.activation(out=sg[i][:, 0:nt], in_=ps[1], func=AF.Sigmoid, bias=b2c[:, si:si + 1], scale=1.0),
                     reads=[Bps[1], Bb], writes=[Bsg[i]])
                S.op("dve", lambda e: e.scalar_tensor_tensor(out=mo[i][:, 0:nt], in0=ps[0], scalar=b1c[:, si:si + 1], in1=sg[i][:, 0:nt], op0=ALU.add, op1=ALU.mult),
                     reads=[Bps[0], Bsg[i], Bb], writes=[Bmo[i]])
                S.dma("act", lambda e: e.dma_start(out=mT_d[si, :, t0:t0 + nt], in_=mo[i][:, 0:nt]), reads=[Bmo[i]], writes=[B_mT])
            steps = [[[(w_out1[0], 128 * n, 128, 0)], [(w_out2[0], 128 * n, 128, 0)]] for n in range(KC)]
            gemm_b(actA_d, B_actA, KC, tokblocks(0, A), steps, 2, epi3)
            S.barrier()
        if stop_after == 3:
            dbg_out["mT_d"] = mT_d

        def epilogue_phase(h_src, h_dst, Bh_dst, g_post_row, g_next_row, dstT_d, Bdst, tok_lo, tok_hi, h_src_is_x=False, final_out=None):
            AR.reset()
            gp, Bgp = load_gain(g_post_row)
            if g_next_row is not None:
                gn, Bgn = load_gain(g_next_row)
                ntb_ = [nt_bufs("e0"), nt_bufs("e1")]
            mTt = [AR.alloc([128, KC, 128], F32) for _ in range(2)]
            BmTt = [Buf("mTt%d" % i) for i in range(2)]
            mtok = AR.alloc([128, D], F32)
            Bmtok = Buf("mtok")
            hres = [AR.alloc([128, D], F32) for _ in range(2)]
            Bhres = [Buf("hres%d" % i) for i in range(2)]
            ssb = AR.alloc([128, 4], F32)
            Bssb = Buf("ssb")
            junk = AR.alloc([128, D], BF16)
            Bjunk = Buf("junk")
            for j, a0 in enumerate(range(tok_lo, tok_hi, 128)):
                i = j % 2
                S.dma("sp", lambda e, i=i, a0=a0: e.dma_start(out=mTt[i], in_=mT_d.rearrange("c f t -> f c t")[:, :, a0:a0 + 128]), reads=[B_mT], writes=[BmTt[i]])
                S.dma("sp", lambda e, i=i, a0=a0: e.dma_start(out=hres[i], in_=h_src[a0:a0 + 128, :]), reads=([] if h_src_is_x else [Bh_dst]), writes=[Bhres[i]])
                for q in range(8):
                    bk = q % 4
                    for c4 in range(4):
                        c = 4 * q + c4
                        S.op("pe", lambda e, i=i, c=c, c4=c4, bk=bk: e.transpose(out=banks[bk][:, 128 * c4:128 * c4 + 128], in_=mTt[i][:, c, :], identity=ident_f),
                             reads=[BmTt[i], B_const], writes=[BK[bk]], pe_acc=(c4 > 0))
                    if q % 2 == 0:
                        S.op("act", lambda e, q=q, bk=bk: e.copy(out=mtok[:, 512 * q:512 * q + 512], in_=banks[bk][:, :]), reads=[BK[bk]], writes=[Bmtok])
                    else:
                        S.op("dve", lambda e, q=q, bk=bk: e.tensor_copy(out=mtok[:, 512 * q:512 * q + 512], in_=banks[bk][:, :]), reads=[BK[bk]], writes=[Bmtok])
                ss, rstd = ssb[:, 0:1], ssb[:, 1:2]
                S.op("act", lambda e: e.activation(out=junk, in_=mtok, func=AF.Square), reads=[Bmtok], writes=[Bjunk])
                S.op("dve", lambda e: e.reduce_sum(out=ss, in_=junk, axis=AX.X), reads=[Bjunk], writes=[Bssb])
                rstd_from_ss(ss, rstd, [Bssb], [Bssb])
                S.op("dve", lambda e: e.scalar_tensor_tensor(out=mtok, in0=mtok, scalar=rstd, in1=gp, op0=ALU.mult, op1=ALU.mult),
                     reads=[Bmtok, Bssb, Bgp], writes=[Bmtok])
                S.op("dve", lambda e, i=i: e.tensor_tensor(out=hres[i], in0=hres[i], in1=mtok, op=ALU.add), reads=[Bmtok, Bhres[i]], writes=[Bhres[i]])
                if final_out is not None:
                    S.dma("act", lambda e, i=i, a0=a0: e.dma_start(out=final_out[a0 - tok_lo:a0 - tok_lo + 128, :], in_=hres[i]), reads=[Bhres[i]], writes=[Bh_dst])
                else:
                    S.dma("act", lambda e, i=i, a0=a0: e.dma_start(out=h_dst[a0:a0 + 128, :], in_=hres[i]), reads=[Bhres[i]], writes=[Bh_dst])
                if g_next_row is not None:
                    norm_transpose(hres[i], Bhres[i], gn, Bgn, dstT_d, Bdst, a0, ntb_[i], j)
            S.barrier()

        B_h = Buf("h_d", multi=True)
        B_actB = Buf("actB_d", multi=True)
        if stop_after is None or stop_after >= 4:
            epilogue_phase(xcat[P - 128:T, :], h_d, B_h, g_post_mix[0], g_pre_ffn[0], actB_d, B_actB, 0, A, h_src_is_x=True)
        if stop_after == 4:
            dbg_out["h_d"] = h_d
            dbg_out["actB_d"] = actB_d

        def ffn_phase(layer, actin_d, Bactin, tok_lo, tok_hi):
            wg, wu, wd = w_gate[layer], w_up[layer], w_down[layer]
            for (t0, nt) in tokblocks(tok_lo, tok_hi):
                AR.reset()
                aT = AR.alloc([128, HC, 512], BF16)
                BaT = Buf("aT")
                sg = [AR.alloc([128, 512], F32) for _ in range(2)]
                Bsg = [Buf("fsg%d" % i) for i in range(2)]
                cnt = [0]

                def epi_gu(si, ps, Bps, t0_, nt_):
                    i = cnt[0] % 2
                    cnt[0] += 1
                    S.op("act", lambda e: e.activation(out=sg[i][:, 0:nt_], in_=ps[0], func=AF.Silu), reads=[Bps[0]], writes=[Bsg[i]])
                    S.op("dve", lambda e: e.tensor_tensor(out=aT[:, si, 0:nt_], in0=ps[1], in1=sg[i][:, 0:nt_], op=ALU.mult), reads=[Bps[1], Bsg[i]], writes=[BaT])
                steps = [[[(wg, 128 * n, 128, 0)], [(wu, 128 * n, 128, 0)]] for n in range(HC)]
                mark = gemm_b(actin_d, Bactin, KC, [(t0, nt)], steps, 2, epi_gu)
                S.barrier()
                AR.off = mark
                fo = [AR.alloc([128, 512], F32) for _ in range(2)]
                Bfo = [Buf("fo%d" % i) for i in range(2)]
                slabs = [AR.alloc([128, HC, 256], BF16) for _ in range(2)]
                Bsl = [Buf("dsl%d" % i, multi=True) for i in range(2)]
                for n2 in range(KC // 2):
                    i = n2 % 2
                    for (h0, h1) in ((0, 22), (22, 44), (44, 65), (65, 86)):
                        S.dma("pool", lambda e, i=i, n2=n2, h0=h0, h1=h1: e.dma_start(
                            out=slabs[i][:, h0:h1, :], in_=wd[128 * h0:128 * h1, 256 * n2:256 * n2 + 256].rearrange("(kc p) n -> p kc n", p=128)),
                            writes=[Bsl[i]])
                    for half in range(2):
                        n = 2 * n2 + half
                        bk = 2 * i + half
                        fi = n % 2
                        for kc in range(HC):
                            S.op("pe", lambda e, i=i, kc=kc, nt=nt, bk=bk, half=half: e.matmul(banks[bk][:, 0:nt], lhsT=slabs[i][:, kc, 128 * half:128 * half + 128], rhs=aT[:, kc, 0:nt], start=(kc == 0), stop=(kc == HC - 1)),
                                 reads=[Bsl[i], BaT], writes=[BK[bk]], pe_acc=(kc > 0))
                        if fi == 0:
                            S.op("act", lambda e, fi=fi, nt=nt, bk=bk: e.copy(out=fo[fi][:, 0:nt], in_=banks[bk][:, 0:nt]), reads=[BK[bk]], writes=[Bfo[fi]])
                        else:
                            S.op("dve", lambda e, fi=fi, nt=nt, bk=bk: e.tensor_copy(out=fo[fi][:, 0:nt], in_=banks[bk][:, 0:nt]), reads=[BK[bk]], writes=[Bfo[fi]])
                        S.dma("act", lambda e, fi=fi, n=n, t0=t0, nt=nt: e.dma_start(out=mT_d[n, :, t0:t0 + nt], in_=fo[fi][:, 0:nt]), reads=[Bfo[fi]], writes=[B_mT])
                S.barrier()

        if stop_after is None or stop_after >= 5:
            ffn_phase(0, actB_d, B_actB, 0, A)
        if stop_after == 5:
            dbg_out["mT_d"] = mT_d
        if stop_after is None or stop_after >= 6:
            epilogue_phase(h_d, h_d, B_h, g_post_ffn[0], g_pre_mix[1], actA_d, B_actA, 0, A)
        if stop_after == 6:
            dbg_out["h_d"] = h_d
            dbg_out["actA_d"] = actA_d

        B_qT = Buf("qT_d", multi=True)
        B_kT = Buf("kT_d", multi=True)
        B_v = Buf("v_d", multi=True)
        if stop_after is None or stop_after >= 7:
            AR.reset()
            wq = w_qkv[0]
            COS = AR.alloc([128, A], F32)
            SINS = AR.alloc([128, A], F32)
            bq = AR.alloc([128, 40], F32)
            bqs = AR.alloc([128, 40], F32)
            bkd = AR.alloc([128, 8], F32)
            bkds = AR.alloc([128, 8], F32)
            invf = AR.alloc([128, 2], F32)
            sgn = AR.alloc([128, 2], F32)
            pi_i = AR.alloc([128, 2], I32)
            posi = AR.alloc([128, A], I32)
            wk_ = AR.alloc([128, A], F32)
            wk2 = AR.alloc([128, A], F32)
            tmp32 = AR.alloc([128, 128], F32)
            Bt32 = Buf("t32")
            Bat = Buf("attnprep")
            load_cols(bq, b_qkv[0], 40, Bat, tmp32, Bt32, banks[7], BK[7])
            S.barrier()
            bsw = b_qkv[0, 0:5120].rearrange("(c h t i) -> c h t i", h=2, t=2, i=32)
            for hh in range(2):
                for tt in range(2):
                    S.dma("sp", lambda e, hh=hh, tt=tt: e.dma_start(out=tmp32[0:40, 64 * hh + 32 * tt:64 * hh + 32 * tt + 32], in_=bsw[:, hh, 1 - tt, :]), writes=[Bt32])
            S.op("pe", lambda e: e.transpose(out=banks[7][:, 0:40], in_=tmp32[0:40, :], identity=ident_f[0:40, 0:40]), reads=[Bt32, B_const], writes=[BK[7]])
            S.op("dve", lambda e: e.tensor_copy(out=bqs, in_=banks[7][:, 0:40]), reads=[BK[7]], writes=[Bat])
            S.barrier()
            bkv = b_qkv[0, 4096:4608].rearrange("(j d) -> j d", d=64)
            bkvs = b_qkv[0, 4096:4608].rearrange("(j t i) -> j t i", t=2, i=32)
            for hh in range(2):
                S.dma("sp", lambda e, hh=hh: e.dma_start(out=tmp32[0:8, 64 * hh:64 * hh + 64], in_=bkv), writes=[Bt32])
            S.op("pe", lambda e: e.transpose(out=banks[7][:, 0:8], in_=tmp32[0:8, :], identity=ident_f[0:8, 0:8]), reads=[Bt32, B_const], writes=[BK[7]])
            S.op("dve", lambda e: e.tensor_copy(out=bkd, in_=banks[7][:, 0:8]), reads=[BK[7]], writes=[Bat])
            S.barrier()
            for hh in range(2):
                for tt in range(2):
                    S.dma("sp", lambda e, hh=hh, tt=tt: e.dma_start(out=tmp32[0:8, 64 * hh + 32 * tt:64 * hh + 32 * tt + 32], in_=bkvs[:, 1 - tt, :]), writes=[Bt32])
            S.op("pe", lambda e: e.transpose(out=banks[7][:, 0:8], in_=tmp32[0:8, :], identity=ident_f[0:8, 0:8]), reads=[Bt32, B_const], writes=[BK[7]])
            S.op("dve", lambda e: e.tensor_copy(out=bkds, in_=banks[7][:, 0:8]), reads=[BK[7]], writes=[Bat])
            S.op("pool", lambda e: e.iota(pi_i, pattern=[[0, 2]], base=0, channel_multiplier=1), writes=[Bat])
            S.op("dve", lambda e: e.tensor_single_scalar(out=pi_i[:, 1:2], in_=pi_i[:, 0:1], scalar=31, op=ALU.bitwise_and), reads=[Bat], writes=[Bat])
            S.op("dve", lambda e: e.tensor_copy(out=invf[:, 0:1], in_=pi_i[:, 1:2]), reads=[Bat], writes=[Bat])
            S.op("act", lambda e: e.activation(out=invf[:, 0:1], in_=invf[:, 0:1], func=AF.Exp, scale=float(-np.log(10000.0) / 32.0)), reads=[Bat], writes=[Bat])
            S.op("dve", lambda e: e.tensor_scalar(out=invf[:, 1:2], in0=invf[:, 0:1], scalar1=1.0 / TWO_PI, scalar2=None, op0=ALU.mult), reads=[Bat], writes=[Bat])
            S.op("dve", lambda e: e.tensor_single_scalar(out=pi_i[:, 1:2], in_=pi_i[:, 0:1], scalar=32, op=ALU.bitwise_and), reads=[Bat], writes=[Bat])
            S.op("dve", lambda e: e.tensor_copy(out=sgn[:, 0:1], in_=pi_i[:, 1:2]), reads=[Bat], writes=[Bat])
            S.op("dve", lambda e: e.tensor_scalar(out=sgn[:, 0:1], in0=sgn[:, 0:1], scalar1=1.0 / 16.0, scalar2=-1.0, op0=ALU.mult, op1=ALU.add), reads=[Bat], writes=[Bat])
            S.dma("sp", lambda e: e.dma_start(out=posi, in_=posa[0].partition_broadcast(128)), writes=[Bat])
            S.barrier()
            Vq = lambda fn: S.op("dve", fn, reads=[Bat], writes=[Bat])
            Vq(lambda e: e.tensor_copy(out=wk_, in_=posi))
            Vq(lambda e: e.tensor_scalar(out=wk2, in0=wk_, scalar1=invf[:, 1:2], scalar2=MAGIC, op0=ALU.mult, op1=ALU.add))
            Vq(lambda e: e.tensor_scalar(out=wk2, in0=wk2, scalar1=-MAGIC, scalar2=-TWO_PI, op0=ALU.add, op1=ALU.mult))
            Vq(lambda e: e.scalar_tensor_tensor(out=wk2, in0=wk_, scalar=invf[:, 0:1], in1=wk2, op0=ALU.mult, op1=ALU.add))
            Vq(lambda e: e.tensor_scalar(out=wk2, in0=wk2, scalar1=-PI, scalar2=PI, op0=ALU.max, op1=ALU.min))
            S.op("act", lambda e: e.activation(out=SINS, in_=wk2, func=AF.Sin), reads=[Bat], writes=[Bat])
            Vq(lambda e: e.tensor_scalar(out=SINS, in0=SINS, scalar1=sgn[:, 0:1], scalar2=None, op0=ALU.mult))
            S.op("act", lambda e: e.activation(out=wk2, in_=wk2, func=AF.Abs), reads=[Bat], writes=[Bat])
            S.op("act", lambda e: e.activation(out=COS, in_=wk2, func=AF.Sin, scale=-1.0, bias=halfpi), reads=[Bat, B_const], writes=[Bat])
            S.barrier()
            keep = AR.off
            t1 = [AR.alloc([128, 512], F32) for _ in range(2)]
            t2 = [AR.alloc([128, 512], F32) for _ in range(2)]
            qo = [AR.alloc([128, 512], BF16) for _ in range(2)]
            Bt1 = [Buf("rt1%d" % i) for i in range(2)]
            Bt2 = [Buf("rt2%d" % i) for i in range(2)]
            Bqo = [Buf("qo%d" % i) for i in range(2)]
            cntq = [0]

            def epi_rope(si, ps, Bps, t0, nt):
                i = cntq[0] % 2
                cntq[0] += 1
                if si < 32:
                    bc, bsc, dst, Bd, ci = bq[:, si:si + 1], bqs[:, si:si + 1], qT_d, B_qT, si
                else:
                    j = si - 32
                    bc, bsc, dst, Bd, ci = bkd[:, j:j + 1], bkds[:, j:j + 1], kT_d, B_kT, j
                S.op("dve", lambda e: e.scalar_tensor_tensor(out=t1[i][:, 0:nt], in0=ps[0], scalar=bc, in1=COS[:, t0:t0 + nt], op0=ALU.add, op1=ALU.mult),
                     reads=[Bps[0], Bat], writes=[Bt1[i]])
                S.op("dve", lambda e: e.scalar_tensor_tensor(out=t2[i][:, 0:nt], in0=ps[1], scalar=bsc, in1=SINS[:, t0:t0 + nt], op0=ALU.add, op1=ALU.mult),
                     reads=[Bps[1], Bat], writes=[Bt2[i]])
                S.op("dve", lambda e: e.tensor_tensor(out=qo[i][:, 0:nt], in0=t1[i][:, 0:nt], in1=t2[i][:, 0:nt], op=ALU.add), reads=[Bt1[i], Bt2[i]], writes=[Bqo[i]])
                S.dma("act", lambda e: e.dma_start(out=dst[ci, :, t0:t0 + nt], in_=qo[i][:, 0:nt]), reads=[Bqo[i]], writes=[Bd])

            def swap_pieces(cb):
                return [(wq, cb + 32, 32, 0), (wq, cb, 32, 32), (wq, cb + 96, 32, 64), (wq, cb + 64, 32, 96)]
            steps = []
            for c in range(32):
                steps.append([[(wq, 128 * c, 128, 0)], swap_pieces(128 * c)])
            for j in range(8):
                cb = 4096 + 64 * j
                steps.append([[(wq, cb, 64, 0), (wq, cb, 64, 64)],
                              [(wq, cb + 32, 32, 0), (wq, cb, 32, 32), (wq, cb + 32, 32, 64), (wq, cb, 32, 96)]])
            gemm_b(actA_d, B_actA, KC, tokblocks(0, A), steps, 2, epi_rope)
            S.barrier()
            AR.off = keep
            wv = AR.alloc([128, KC, 512], BF16)
            Bwv = Buf("wv")
            bvb = AR.alloc([128, 512], F32)
            actv = [AR.alloc([128, KC, 128], BF16) for _ in range(2)]
            Bactv = [Buf("actv%d" % i) for i in range(2)]
            vo = [AR.alloc([128, 512], BF16) for _ in range(2)]
            Bvo = [Buf("vo%d" % i) for i in range(2)]
            for q in range(4):
                S.dma("pool", lambda e, q=q: e.dma_start(out=wv[:, 8 * q:8 * q + 8, :], in_=wq[1024 * q:1024 * q + 1024, 4608:5120].rearrange("(kc p) n -> p kc n", p=128)), writes=[Bwv])
            S.dma("sp", lambda e: e.dma_start(out=bvb, in_=b_qkv[0, 4608:5120].partition_broadcast(128)), writes=[Bwv])
            for j in range(A // 128):
                i = j % 2
                S.dma("sp", lambda e, i=i, j=j: e.dma_start(out=actv[i], in_=actA_d.rearrange("c f t -> f c t")[:, :, 128 * j:128 * j + 128]), reads=[B_actA], writes=[Bactv[i]])
                for kc in range(KC):
                    S.op("pe", lambda e, i=i, kc=kc: e.matmul(banks[i][:, :], lhsT=actv[i][:, kc, :], rhs=wv[:, kc, :], start=(kc == 0), stop=(kc == KC - 1)),
                         reads=[Bactv[i], Bwv], writes=[BK[i]], pe_acc=(kc > 0))
                S.op("dve", lambda e, i=i: e.tensor_tensor(out=vo[i], in0=banks[i][:, :], in1=bvb, op=ALU.add), reads=[BK[i], Bwv], writes=[Bvo[i]])
                S.dma("act", lambda e, i=i, j=j: e.dma_start(out=v_d[128 * j:128 * j + 128, :], in_=vo[i]), reads=[Bvo[i]], writes=[B_v])
            S.barrier()
            if stop_after == 7:
                dbg_out["qT_d"] = qT_d
                dbg_out["kT_d"] = kT_d
                dbg_out["v_d"] = v_d

        if stop_after is None or stop_after >= 8:
            AR.reset()
            NT = A // 128
            mask = AR.alloc([128, 256], F32)
            mask0 = AR.alloc([128, 256], F32)
            sinkb = AR.alloc([128, NQH], F32)
            Bm = Buf("mask")
            S.op("pool", lambda e: e.memset(mask, 0.0), writes=[Bm])
            S.op("pool", lambda e: e.affine_select(out=mask, in_=mask, pattern=[[1, 256]], compare_op=ALU.is_ge, fill=NEG, base=-1, channel_multiplier=-1), reads=[Bm], writes=[Bm])
            S.op("pool", lambda e: e.affine_select(out=mask, in_=mask, pattern=[[-1, 256]], compare_op=ALU.is_ge, fill=NEG, base=128, channel_multiplier=1), reads=[Bm], writes=[Bm])
            S.dma("sp", lambda e: e.dma_start(out=mask0[:, 0:128], in_=halo_mask), writes=[Bm])
            S.dma("sp", lambda e: e.dma_start(out=sinkb, in_=sinks[0].partition_broadcast(128)), writes=[Bm])
            S.barrier()
            S.op("dve", lambda e: e.tensor_tensor(out=mask0[:, 0:128], in0=mask0[:, 0:128], in1=mask[:, 0:128], op=ALU.add), reads=[Bm], writes=[Bm])
            S.op("dve", lambda e: e.tensor_copy(out=mask0[:, 128:256], in_=mask[:, 128:256]), reads=[Bm], writes=[Bm])
            S.barrier()
            kTj = AR.alloc([128, A], BF16)
            BkTj = Buf("kTj")
            vj = AR.alloc([128, NT, 64], BF16)
            Bvj = Buf("vj")
            vpad = [AR.alloc([128, NT, 128], BF16) for _ in range(2)]
            Bvpad = Buf("vpad")
            qTc = [AR.alloc([128, A], BF16) for _ in range(2)]
            BqTc = [Buf("qTc%d" % i) for i in range(2)]
            aTc = [AR.alloc([128, M], BF16) for _ in range(2)]
            BaTc = [Buf("aTc%d" % i) for i in range(2)]
            NRr = 2
            s1 = [AR.alloc([128, 256], F32) for _ in range(NRr)]
            ex = [AR.alloc([128, 256], F32) for _ in range(NRr)]
            pb_ = [AR.alloc([128, 256], BF16) for _ in range(NRr)]
            pT = [AR.alloc([128, 256], BF16) for _ in range(NRr)]
            st_ = [AR.alloc([128, 8], F32) for _ in range(NRr)]
            Bs1 = [Buf("s1%d" % i) for i in range(NRr)]
            Bex = [Buf("ex%d" % i) for i in range(NRr)]
            Bpb = [Buf("pb%d" % i) for i in range(NRr)]
            BpT = [Buf("pT%d" % i) for i in range(NRr)]
            Bst = [Buf("st%d" % i) for i in range(NRr)]
            SC = 1.0 / 8.0
            B_attnT = B_actB
            it = 0
            for j in range(NKV):
                S.dma("sp", lambda e, j=j: e.dma_start(out=kTj, in_=kT_d[j]), reads=[B_kT], writes=[BkTj])
                S.dma("sp", lambda e, j=j: e.dma_start(out=vj, in_=v_d[:, 64 * j:64 * j + 64].rearrange("(t p) d -> p t d", p=128)), reads=[B_v], writes=[Bvj])
                S.op("pool", lambda e: e.memset(vpad[0], 0.0), writes=[Bvpad])
                S.op("pool", lambda e: e.memset(vpad[1], 0.0), writes=[Bvpad])
                S.op("dve", lambda e: e.tensor_copy(out=vpad[0][:, :, 0:64], in_=vj), reads=[Bvj, Bvpad], writes=[Bvpad])
                S.op("dve", lambda e: e.tensor_copy(out=vpad[1][:, :, 64:128], in_=vj), reads=[Bvj, Bvpad], writes=[Bvpad])
                for ci in range(4):
                    c = 4 * j + ci
                    qt, Bq = qTc[c % 2], BqTc[c % 2]
                    at, Ba = aTc[c % 2], BaTc[c % 2]
                    S.dma("sp", lambda e, qt=qt, c=c: e.dma_start(out=qt, in_=qT_d[c]), reads=[B_qT], writes=[Bq])
                    for n in range(M // 128):
                        a0 = 128 + 128 * n
                        ob = 6 + (n % 2)
                        for hh in range(2):
                            r = it % NRr
                            it += 1
                            sb_ = 2 * (it % 2)
                            hsl = slice(64 * hh, 64 * hh + 64)
                            head = 2 * c + hh
                            S.op("pe", lambda e, qt=qt, hsl=hsl, a0=a0, sb_=sb_: e.matmul(banks[sb_][:, 0:256], lhsT=qt[hsl, a0:a0 + 128], rhs=kTj[hsl, a0 - 128:a0 + 128], start=True, stop=True),
                                 reads=[Bq, BkTj], writes=[BK[sb_]])
                            mk = mask0 if n == 0 else mask
                            S.op("dve", lambda e, r=r, sb_=sb_, mk=mk: e.tensor_tensor(out=s1[r], in0=banks[sb_][:, 0:256], in1=mk, op=ALU.add), reads=[BK[sb_], Bm], writes=[Bs1[r]])
                            S.op("dve", lambda e, r=r: e.reduce_max(out=st_[r][:, 0:1], in_=s1[r], axis=AX.X), reads=[Bs1[r]], writes=[Bst[r]])
                            S.op("dve", lambda e, r=r: e.tensor_scalar(out=st_[r][:, 1:2], in0=st_[r][:, 0:1], scalar1=-SC, scalar2=None, op0=ALU.mult), reads=[Bst[r]], writes=[Bst[r]])
                            S.op("act", lambda e, r=r: e.activation(out=ex[r], in_=s1[r], func=AF.Exp, bias=st_[r][:, 1:2], scale=SC),
                                 reads=[Bs1[r], Bst[r]], writes=[Bex[r]])
                            S.op("dve", lambda e, r=r: e.reduce_sum(out=st_[r][:, 2:3], in_=ex[r], axis=AX.X), reads=[Bex[r]], writes=[Bst[r]])
                            S.op("act", lambda e, r=r, head=head: e.activation(out=st_[r][:, 3:4], in_=st_[r][:, 1:2], func=AF.Exp, bias=sinkb[:, head:head + 1], scale=1.0),
                                 reads=[Bst[r], Bm], writes=[Bst[r]])
                            S.op("dve", lambda e, r=r: e.tensor_tensor(out=st_[r][:, 4:5], in0=st_[r][:, 2:3], in1=st_[r][:, 3:4], op=ALU.add), reads=[Bst[r]], writes=[Bst[r]])
                            S.op("dve", lambda e, r=r: e.reciprocal(out=st_[r][:, 5:6], in_=st_[r][:, 4:5]), reads=[Bst[r]], writes=[Bst[r]])
                            S.op("dve", lambda e, r=r: e.tensor_scalar(out=pb_[r], in0=ex[r], scalar1=st_[r][:, 5:6], scalar2=None, op0=ALU.mult), reads=[Bex[r], Bst[r]], writes=[Bpb[r]])
                            ptb = banks[sb_ + 1][:].bitcast(BF16)
                            S.op("pe", lambda e, r=r, ptb=ptb: e.transpose(out=ptb[:, 0:128], in_=pb_[r][:, 0:128], identity=ident_b), reads=[Bpb[r], B_const], writes=[BK[sb_ + 1]])
                            S.op("pe", lambda e, r=r, ptb=ptb: e.transpose(out=ptb[:, 128:256], in_=pb_[r][:, 128:256], identity=ident_b), reads=[Bpb[r], B_const], writes=[BK[sb_ + 1]], pe_acc=True)
                            S.op("act", lambda e, r=r, ptb=ptb: e.copy(out=pT[r], in_=ptb[:, 0:256]), reads=[BK[sb_ + 1]], writes=[BpT[r]])
                            tl = a0 // 128
                            S.op("pe", lambda e, r=r, hh=hh, tl=tl, ob=ob: e.matmul(banks[ob][:, 0:128], lhsT=vpad[hh][:, tl - 1, :], rhs=pT[r][:, 0:128], start=(hh == 0), stop=False),
                                 reads=[Bvpad, BpT[r]], writes=[BK[ob]], pe_acc=(hh > 0))
                            S.op("pe", lambda e, r=r, hh=hh, tl=tl, ob=ob: e.matmul(banks[ob][:, 0:128], lhsT=vpad[hh][:, tl, :], rhs=pT[r][:, 128:256], start=False, stop=(hh == 1)),
                                 reads=[Bvpad, BpT[r]], writes=[BK[ob]], pe_acc=True)
                        S.op("act", lambda e, at=at, n=n, ob=ob: e.copy(out=at[:, 128 * n:128 * n + 128], in_=banks[ob][:, 0:128]), reads=[BK[ob]], writes=[Ba])
                    S.dma("act", lambda e, at=at, c=c: e.dma_start(out=actB_d[c, :, 128:A], in_=at), reads=[Ba], writes=[B_attnT])
            S.barrier()
            if stop_after == 8:
                dbg_out["actB_d"] = actB_d

        if stop_after is None or stop_after >= 9:
            AR.reset()
            boc = AR.alloc([128, KC], F32)
            tmp32 = AR.alloc([128, 128], F32)
            Bb = Buf("bo")
            Bt32 = Buf("t32")
            load_cols(boc, b_o[0], KC, Bb, tmp32, Bt32, banks[7], BK[7])
            S.barrier()
            mo = [AR.alloc([128, 512], F32) for _ in range(2)]
            Bmo = [Buf("omo%d" % i) for i in range(2)]
            cnt9 = [0]

            def epi9(si, ps, Bps, t0, nt):
                i = cnt9[0] % 2
                cnt9[0] += 1
                S.op("act", lambda e: e.activation(out=mo[i][:, 0:nt], in_=ps[0], func=AF.Identity, bias=boc[:, si:si + 1], scale=1.0), reads=[Bps[0], Bb], writes=[Bmo[i]])
                S.dma("act", lambda e: e.dma_start(out=mT_d[si, :, t0:t0 + nt], in_=mo[i][:, 0:nt]), reads=[Bmo[i]], writes=[B_mT])
            steps = [[[(w_o[0], 128 * n, 128, 0)]] for n in range(KC)]
            gemm_b(actB_d, B_actB, KC, tokblocks(128, A), steps, 1, epi9)
            S.barrier()
            epilogue_phase(h_d, h_d, B_h, g_post_mix[1], g_pre_ffn[1], actA_d, B_actA, 128, A)
        if stop_after == 9:
            dbg_out["h_d"] = h_d
        if stop_after is None or stop_after >= 10:
            ffn_phase(1, actA_d, B_actA, 128, A)
            B_out = Buf("out", multi=True)
            epilogue_phase(h_d, None, B_out, g_post_ffn[1], None, None, None, 128, A, final_out=out)

        if dump_all:
            dbg_out.update(dict(h_d=h_d, actA_d=actA_d, actB_d=actB_d, mT_d=mT_d, qT_d=qT_d, kT_d=kT_d, v_d=v_d))
            if stop_after is not None and stop_after <= 4:
                for kk in ("qT_d", "kT_d", "v_d"):
                    dbg_out.pop(kk)
                dbg_out["hnT_d"] = hnT_d
        dbg_specs = {}
        if dbg_out:
            AR.reset()
            for name, src in dbg_out.items():
                shp = list(src.shape)
                dt = src.dtype
                o = nc.dram_tensor("dbg_" + name, shp, dt, kind="ExternalOutput").ap()
                dbg_specs[name] = (shp, dt)
                flat_s = src.rearrange("a b c -> (a b) c") if len(shp) == 3 else src
                flat_o = o.rearrange("a b c -> (a b) c") if len(shp) == 3 else o
                rows = flat_s.shape[0]
                cols = flat_s.shape[1]
                tb = AR.alloc([128, cols], dt)
                Btb = Buf("dbgt")
                for r0 in range(0, rows, 128):
                    S.dma("sp", lambda e, r0=r0, tb=tb, flat_s=flat_s: e.dma_start(out=tb, in_=flat_s[r0:r0 + 128, :]), writes=[Btb])
                    S.dma("sp", lambda e, r0=r0, tb=tb, flat_o=flat_o: e.dma_start(out=flat_o[r0:r0 + 128, :], in_=tb), reads=[Btb], writes=[Buf("dbgo")])
        S.barrier()
        S.emit()
    return nc, S


WEIGHT_KEYS = ["norm_pre_mix", "norm_post_mix", "norm_pre_ffn", "norm_post_ffn", "s5_lam_re", "s5_lam_im", "s5_log_step",
               "s5_b_re", "s5_b_im", "s5_c_re", "s5_c_im", "s5_d", "s5_w_out1", "s5_b_out1", "s5_w_out2", "s5_b_out2",
               "attn_w_qkv", "attn_b_qkv", "attn_w_o", "attn_b_o", "attn_sinks", "ffn_w_gate", "ffn_w_up", "ffn_w_down"]

_CACHE = {}


def make_in_maps(inputs, P, M, n_cores, seq):
    x = np.asarray(inputs["x"])
    pos = np.asarray(inputs["positions"])
    w = {k: np.ascontiguousarray(np.asarray(inputs[k], dtype=np.float32)) for k in WEIGHT_KEYS}
    in_maps = []
    for core in range(n_cores):
        b, s = core // 2, core % 2
        xc = np.zeros((P + M, D), np.float32)
        pa = np.zeros((1, 128 + M), np.int32)
        if s == 0:
            xc[P:] = x[b, 0:M]
            pa[0, 128:] = pos[b, 0:M]
            hm = np.full((128, 128), NEG, np.float32)
        else:
            xc[:] = x[b, 0:P + M]
            pa[0, :] = pos[b, P - 128:P + M]
            hm = np.zeros((128, 128), np.float32)
        m = dict(w)
        m["xcat"] = xc
        m["posa"] = pa
        m["halo_mask"] = hm
        in_maps.append(m)
    return in_maps


def kernel(**inputs):
    P = M = 2048
    n_cores = 8
    if "prog" not in _CACHE:
        _CACHE["prog"] = build_program(P, M)[0]
    nc = _CACHE["prog"]
    in_maps = make_in_maps(inputs, P, M, n_cores, 4096)
    res = run_bass_kernel_spmd(nc, in_maps, core_ids=list(range(n_cores)))
    out = np.zeros((4, 4096, D), np.float32)
    for core in range(n_cores):
        b, s = core // 2, core % 2
        out[b, s * M:(s + 1) * M] = res.results[core]["out"]
    return out
```
